# Optimizing a Trainium2 kernel written in Bass

```python
import jax, jax.numpy as jnp
from jax import lax
import numpy as np

D_MODEL = 4096
BATCH = 4
SEQ = 4096
DEPTH = 4
DEC_BATCH = 8
DEC_SEQ = 32
PAST_LEN = 2048

CHUNK = 64
HEAD_DIM = 128
GDN_WIDTH = 3 * D_MODEL // 8
GDN_HEADS = GDN_WIDTH // HEAD_DIM
SB_WIDTH = 3 * D_MODEL // 8
SB_HEADS = SB_WIDTH // HEAD_DIM
SC_WIDTH = D_MODEL - GDN_WIDTH - SB_WIDTH
D_MIX = GDN_WIDTH + SB_WIDTH + SC_WIDTH
D_IN = 4 * GDN_WIDTH + 2 * GDN_HEADS + 4 * SB_WIDTH + 4 * SC_WIDTH
GDN_CONV = 4
SC_CONV = 3
SB_BLOCK = 128
EPS = 1e-6

kernel_name = "hybrid_gdn_stickbreak_shortconv_stream_step"


def rms_norm(x, g):
    xf = x.astype(jnp.float32)
    y = xf * lax.rsqrt(jnp.mean(xf * xf, axis=-1, keepdims=True) + EPS)
    return (y * g.astype(jnp.float32)).astype(x.dtype)


def l2_normalize(x):
    return x * lax.rsqrt(jnp.sum(x * x, axis=-1, keepdims=True) + EPS)


def causal_dwconv(x, w, prefix):
    width = w.shape[0]
    t = x.shape[1]
    xp = jnp.concatenate([prefix.astype(x.dtype), x], axis=1)
    w = w.astype(x.dtype)
    y = xp[:, 0:t] * w[0]
    for i in range(1, width):
        y = y + xp[:, i:i + t] * w[i]
    return y, xp[:, t:]


def gated_delta_rule(q, k, v, g, beta, s0):
    bn, t, h, dk = q.shape
    n = t // CHUNK

    def chunks(x):
        x = x.reshape((bn, n, CHUNK) + x.shape[2:])
        return x.transpose((1, 0, 3, 2) + tuple(range(4, x.ndim)))

    qc, kc, vc, gc, bc = chunks(q), chunks(k), chunks(v), chunks(g), chunks(beta)
    gcum = jnp.cumsum(gc, axis=-1)
    idx = jnp.arange(CHUNK)
    incl = idx[:, None] >= idx[None, :]
    strict = idx[:, None] > idx[None, :]
    diff = gcum[..., :, None] - gcum[..., None, :]
    decay = jnp.where(incl, jnp.exp(jnp.where(incl, diff, 0.0)), 0.0)
    kbeta = kc * bc[..., None]
    lmat = jnp.where(strict, jnp.einsum("nbhid,nbhjd->nbhij", kbeta, kc) * decay, 0.0)
    eye = jnp.eye(CHUNK, dtype=lmat.dtype)
    tinv = lax.linalg.triangular_solve(lmat + eye, jnp.broadcast_to(eye, lmat.shape),
                                       left_side=True, lower=True, unit_diagonal=True)
    u = jnp.matmul(tinv, vc * bc[..., None])
    w = jnp.matmul(tinv, kbeta * jnp.exp(gcum)[..., None])
    attn = jnp.where(incl, jnp.einsum("nbhid,nbhjd->nbhij", qc, kc) * decay, 0.0)
    q_dec = qc * jnp.exp(gcum)[..., None]
    k_dec = kc * jnp.exp(gcum[..., -1:] - gcum)[..., None]
    g_last = jnp.exp(gcum[..., -1])

    def step(s, inp):
        u_i, w_i, a_i, qd_i, kd_i, gl_i = inp
        v_new = u_i - jnp.matmul(w_i, s)
        o = jnp.matmul(qd_i, s) + jnp.matmul(a_i, v_new)
        s = s * gl_i[..., None, None] + jnp.matmul(jnp.swapaxes(kd_i, -1, -2), v_new)
        return s, o

    s_fin, o = lax.scan(step, s0, (u, w, attn, q_dec, k_dec, g_last))
    o = o.transpose(1, 0, 3, 2, 4).reshape(bn, t, h, v.shape[-1])
    return o, s_fin


def stick_breaking(q, k, v, q_pos, k_pos):
    z = jnp.einsum("bqhd,bkhd->bhqk", q.astype(jnp.float32), k.astype(jnp.float32)) * (HEAD_DIM ** -0.5)
    causal = k_pos[None, :] < q_pos[:, None]
    log_keep = jnp.where(causal, jax.nn.log_sigmoid(-z), 0.0)
    log_between = lax.cumsum(log_keep, axis=3, reverse=True) - log_keep
    a = jnp.where(causal, jnp.exp(jax.nn.log_sigmoid(z) + log_between), 0.0)
    return jnp.einsum("bhqk,bkhd->bqhd", a, v.astype(jnp.float32))


def stick_breaking_blocked(q, k, v):
    b, t, h, d = q.shape
    nb = t // SB_BLOCK
    q_blocks = jnp.moveaxis(q.reshape(b, nb, SB_BLOCK, h, d), 1, 0)
    starts = jnp.arange(nb, dtype=jnp.int32) * SB_BLOCK
    k_pos = jnp.arange(t, dtype=jnp.int32)

    def one_block(args):
        q_blk, start = args
        return stick_breaking(q_blk, k, v, start + jnp.arange(SB_BLOCK, dtype=jnp.int32), k_pos)

    o = lax.map(one_block, (q_blocks, starts))
    return jnp.moveaxis(o, 0, 1).reshape(b, t, h, d)


def trunk_layer(x, w_in, w_out, g_pre, g_post, gdn_conv_w, gdn_a_log, gdn_dt_bias, gdn_norm, sc_conv_w,
                sb_k_past, sb_v_past, gdn_s0, gdn_conv0, sc_conv0):
    bn, t, _ = x.shape
    f32 = jnp.float32
    h = rms_norm(x, g_pre)
    proj = h @ w_in.astype(x.dtype)
    sizes = (3 * GDN_WIDTH, GDN_WIDTH, GDN_HEADS, GDN_HEADS,
             SB_WIDTH, SB_WIDTH, SB_WIDTH, SB_WIDTH,
             SC_WIDTH, SC_WIDTH, SC_WIDTH, SC_WIDTH)
    offsets = np.cumsum(sizes)[:-1].tolist()
    (a_qkv, a_z, a_a, a_b, b_q, b_k, b_v, b_z, c_b, c_c, c_h, c_z) = jnp.split(proj, offsets, axis=-1)

    qkv, gdn_conv_new = causal_dwconv(a_qkv, gdn_conv_w, gdn_conv0)
    qkv = jax.nn.silu(qkv.astype(f32))
    qa, ka, va = jnp.split(qkv, 3, axis=-1)
    qa = l2_normalize(qa.reshape(bn, t, GDN_HEADS, HEAD_DIM)) * (HEAD_DIM ** -0.5)
    ka = l2_normalize(ka.reshape(bn, t, GDN_HEADS, HEAD_DIM))
    va = va.reshape(bn, t, GDN_HEADS, HEAD_DIM)
    g = -jnp.exp(gdn_a_log.astype(f32)) * jax.nn.softplus(a_a.astype(f32) + gdn_dt_bias.astype(f32))
    beta = jax.nn.sigmoid(a_b.astype(f32))
    pad = (-t) % CHUNK

    def padt(arr):
        return jnp.pad(arr, [(0, 0), (0, pad)] + [(0, 0)] * (arr.ndim - 2))

    o_a, gdn_s_new = gated_delta_rule(padt(qa), padt(ka), padt(va), padt(g), padt(beta), gdn_s0.astype(f32))
    o_a = rms_norm(o_a[:, :t], gdn_norm).reshape(bn, t, GDN_WIDTH) * jax.nn.silu(a_z.astype(f32))

    qb = b_q.reshape(bn, t, SB_HEADS, HEAD_DIM)
    kb = b_k.reshape(bn, t, SB_HEADS, HEAD_DIM)
    vb = b_v.reshape(bn, t, SB_HEADS, HEAD_DIM)
    if sb_k_past is None:
        o_b = stick_breaking_blocked(qb, kb, vb)
    else:
        past = sb_k_past.shape[1]
        k_all = jnp.concatenate([sb_k_past.astype(x.dtype), kb], axis=1)
        v_all = jnp.concatenate([sb_v_past.astype(x.dtype), vb], axis=1)
        o_b = stick_breaking(qb, k_all, v_all, past + jnp.arange(t, dtype=jnp.int32),
                             jnp.arange(past + t, dtype=jnp.int32))
    o_b = o_b.reshape(bn, t, SB_WIDTH) * jax.nn.silu(b_z.astype(f32))

    conv_u, sc_conv_new = causal_dwconv(c_c * c_h, sc_conv_w, sc_conv0)
    o_c = c_b * conv_u * jax.nn.silu(c_z)

    mix = jnp.concatenate([o_a.astype(x.dtype), o_b.astype(x.dtype), o_c.astype(x.dtype)], axis=-1)
    x = x + rms_norm(mix @ w_out.astype(x.dtype), g_post)
    return x, (kb, vb, gdn_s_new.astype(x.dtype), gdn_conv_new, sc_conv_new)


def setup_inputs(seed: int = 0) -> dict:
    key = jax.random.key(seed)
    ks = jax.random.split(key, 18)
    nrm = jax.random.normal
    f32 = jnp.float32
    x_prompt = nrm(ks[0], (BATCH, SEQ, D_MODEL), f32)
    x_sample = nrm(ks[1], (DEC_BATCH, DEC_SEQ, D_MODEL), f32)
    cache_sb_k = nrm(ks[2], (DEPTH, DEC_BATCH, PAST_LEN, SB_HEADS, HEAD_DIM), f32)
    cache_sb_v = nrm(ks[3], (DEPTH, DEC_BATCH, PAST_LEN, SB_HEADS, HEAD_DIM), f32)
    state_gdn = 0.5 * nrm(ks[4], (DEPTH, DEC_BATCH, GDN_HEADS, HEAD_DIM, HEAD_DIM), f32)
    state_gdn_conv = nrm(ks[5], (DEPTH, DEC_BATCH, GDN_CONV - 1, 3 * GDN_WIDTH), f32)
    state_sc_conv = nrm(ks[6], (DEPTH, DEC_BATCH, SC_CONV - 1, SC_WIDTH), f32)
    w_in = nrm(ks[7], (DEPTH, D_MODEL, D_IN), f32) * (D_MODEL ** -0.5)
    w_out = nrm(ks[8], (DEPTH, D_MIX, D_MODEL), f32) * (D_MIX ** -0.5)
    norm_pre = 1.0 + 0.05 * nrm(ks[9], (DEPTH, D_MODEL), f32)
    norm_post = 1.0 + 0.05 * nrm(ks[10], (DEPTH, D_MODEL), f32)
    gdn_conv_w = nrm(ks[11], (DEPTH, GDN_CONV, 3 * GDN_WIDTH), f32) * (GDN_CONV ** -0.5)
    gdn_a_log = jnp.log(jax.random.uniform(ks[12], (DEPTH, GDN_HEADS), f32, 1.0, 16.0))
    dt = jnp.exp(jax.random.uniform(ks[13], (DEPTH, GDN_HEADS), f32, float(np.log(1e-3)), float(np.log(1e-1))))
    gdn_dt_bias = dt + jnp.log(-jnp.expm1(-dt))
    gdn_norm = 1.0 + 0.05 * nrm(ks[14], (DEPTH, HEAD_DIM), f32)
    sc_conv_w = nrm(ks[15], (DEPTH, SC_CONV, SC_WIDTH), f32) * (SC_CONV ** -0.5)
    return {"x_prompt": x_prompt, "x_sample": x_sample,
            "cache_sb_k": cache_sb_k, "cache_sb_v": cache_sb_v,
            "state_gdn": state_gdn, "state_gdn_conv": state_gdn_conv, "state_sc_conv": state_sc_conv,
            "w_in": w_in, "w_out": w_out, "norm_pre": norm_pre, "norm_post": norm_post,
            "gdn_conv_w": gdn_conv_w, "gdn_a_log": gdn_a_log, "gdn_dt_bias": gdn_dt_bias,
            "gdn_norm": gdn_norm, "sc_conv_w": sc_conv_w}


def reference(x_prompt, x_sample, cache_sb_k, cache_sb_v, state_gdn, state_gdn_conv, state_sc_conv,
              w_in, w_out, norm_pre, norm_post, gdn_conv_w, gdn_a_log, gdn_dt_bias, gdn_norm, sc_conv_w):
    bp = x_prompt.shape[0]
    dt_p = x_prompt.dtype
    s0_p = jnp.zeros((bp, GDN_HEADS, HEAD_DIM, HEAD_DIM), dt_p)
    gconv0_p = jnp.zeros((bp, GDN_CONV - 1, 3 * GDN_WIDTH), dt_p)
    sconv0_p = jnp.zeros((bp, SC_CONV - 1, SC_WIDTH), dt_p)
    yp, ys = x_prompt, x_sample
    new_p = ([], [], [], [], [])
    new_s = ([], [], [], [], [])
    for l in range(DEPTH):
        params = (w_in[l], w_out[l], norm_pre[l], norm_post[l], gdn_conv_w[l], gdn_a_log[l],
                  gdn_dt_bias[l], gdn_norm[l], sc_conv_w[l])
        yp, st_p = trunk_layer(yp, *params, None, None, s0_p, gconv0_p, sconv0_p)
        ys, st_s = trunk_layer(ys, *params, cache_sb_k[l], cache_sb_v[l], state_gdn[l],
                               state_gdn_conv[l], state_sc_conv[l])
        for i in range(5):
            new_p[i].append(st_p[i])
            new_s[i].append(st_s[i])
    sb_k_p, sb_v_p, gdn_p, gdn_conv_p, sc_conv_p = [jnp.stack(a, axis=0) for a in new_p]
    sb_k_s, sb_v_s, gdn_s, gdn_conv_s, sc_conv_s = [jnp.stack(a, axis=0) for a in new_s]
    return (yp, ys, sb_k_p, sb_v_p, gdn_p, gdn_conv_p, sc_conv_p,
            sb_k_s, sb_v_s, gdn_s, gdn_conv_s, sc_conv_s)
```

```python
import contextlib
import numpy as np
import concourse.bass as bass
import concourse.mybir as mybir
from concourse.bass_utils import run_bass_kernel_spmd

F32 = mybir.dt.float32
BF16 = mybir.dt.bfloat16
AF = mybir.ActivationFunctionType
ALU = mybir.AluOpType
AX = mybir.AxisListType

D = 4096
DIN = 16408
H = 12
HD = 128
GW = 1536
EPS = 1e-6
NEG = -30000.0
DEBUG_STOP = 10 ** 9
DEBUG_OPS = 10 ** 12


class Buf:
    def __init__(self, name, t=None):
        self.name = name
        self.t = t
        self.w = {}
        self.r = {}
        self.dsem = None
        self.dcnt = 0
        self.excl = False


class _Rec:
    def __init__(self):
        self.call = None

    def __getattr__(self, name):
        def f(*a, **k):
            self.call = (name, a, k)
            return self
        return f

    def then_inc(self, *a):
        return self


def _record(fn):
    r = _Rec()
    fn(r)
    assert r.call is not None
    return r.call


class MK:
    ENG = ("pe", "act", "dve", "pool", "sp")

    def __init__(self, nc, es, block):
        self.nc = nc
        self.es = es
        self.block = block
        self.sem = {}
        self.cnt = {}
        self.q = {e: [] for e in self.ENG}
        self.waited = {e: {} for e in self.ENG}
        for e in ("pe", "act", "dve", "pool"):
            self.sem[e] = es.enter_context(nc.semaphore("s_" + e))
            self.cnt[e] = 0
        self.free_sems = []
        self.live = []
        self.nsem = 0
        self.ninst = 0
        self.scope = None

    def sb(self, name, shape, dt=F32, es=None):
        es = es or self.scope or self.es
        self.nalloc = getattr(self, "nalloc", 0) + 1
        name = "%s_u%d" % (name, self.nalloc)
        t = es.enter_context(self.nc.sbuf_tensor(name, list(shape), dt))
        return Buf(name, t)

    def ps(self, name, shape, dt=F32, es=None):
        es = es or self.scope or self.es
        self.nalloc = getattr(self, "nalloc", 0) + 1
        name = "%s_u%d" % (name, self.nalloc)
        t = es.enter_context(self.nc.psum_tensor(name, list(shape), dt))
        b = Buf(name, t)
        b.excl = True
        return b

    def dram(self, name):
        return Buf(name, None)

    def _getsem(self, b):
        if b.dsem is None:
            if self.free_sems:
                b.dsem, b.dcnt = self.free_sems.pop()
            else:
                self.nsem += 1
                b.dsem = self.es.enter_context(self.nc.semaphore("d%d" % self.nsem))
                b.dcnt = 0
            self.live.append(b)

    def release(self, bufs):
        for b in bufs:
            if b.dsem is not None:
                self.free_sems.append((b.dsem, b.dcnt))
                self.live.remove(b)
                b.dsem = None

    def _deps(self, eng, reads, writes, dma_buf=None):
        deps = {}

        def add(d):
            for k, (s, v) in d.items():
                if k not in deps or deps[k][1] < v:
                    deps[k] = (s, v)
        own = self.sem.get(eng)
        for b in reads:
            add(b.w)
            if b.excl:
                add({k: v for k, v in b.r.items() if v[0] is not own})
        for b in writes:
            if dma_buf is not None and b is dma_buf and not b.r and b.w and all(k == id(b.dsem) for k in b.w):
                continue
            add(b.w)
            add(b.r)
        out = []
        wd = self.waited[eng]
        pes = self.sem["pe"]
        for k, (s, v) in deps.items():
            if eng == "pe" and s is pes:
                continue
            if wd.get(k, 0) >= v:
                continue
            wd[k] = v
            out.append((s, v))
        return out

    def _post(self, reads, writes, ev):
        k = id(ev[0])
        for b in reads:
            b.r[k] = ev
        for b in writes:
            b.w = {k: ev}
            b.r = {}

    def op(self, eng, fn, reads=(), writes=()):
        self.nops = getattr(self, "nops", 0) + 1
        if self.nops > getattr(self, "limit", 10 ** 12):
            return
        waits = self._deps(eng, reads, writes)
        self.cnt[eng] += 1
        s = self.sem[eng]
        v = self.cnt[eng]
        call = _record(fn)

        def emit(e, call=call, waits=waits, s=s):
            for (ws, wv) in waits:
                e.wait_ge(ws, wv)
            getattr(e, call[0])(*call[1], **call[2]).then_inc(s, 1)
        self.q[eng].append(emit)
        self.ninst += 1 + len(waits)
        self._post(reads, writes, (s, v))

    def dma(self, eng, fn, sbuf, reads=(), writes=()):
        self.nops = getattr(self, "nops", 0) + 1
        if self.nops > getattr(self, "limit", 10 ** 12):
            return
        self._getsem(sbuf)
        waits = self._deps(eng, reads, writes, dma_buf=sbuf)
        sbuf.dcnt += 16
        s = sbuf.dsem
        v = sbuf.dcnt
        call = _record(fn)

        def emit(e, call=call, waits=waits, s=s):
            for (ws, wv) in waits:
                e.wait_ge(ws, wv)
            getattr(e, call[0])(*call[1], **call[2]).then_inc(s, 16)
        self.q[eng].append(emit)
        self.ninst += 1 + len(waits)
        self._post(reads, writes, (s, v))

    def barrier(self):
        evs = [(self.sem[e], self.cnt[e]) for e in ("pe", "act", "dve", "pool") if self.cnt[e] > 0]
        evs += [(b.dsem, b.dcnt) for b in self.live if b.dcnt > 0]
        for eng in self.ENG:
            wd = self.waited[eng]
            waits = []
            for (s, v) in evs:
                if wd.get(id(s), 0) >= v:
                    continue
                wd[id(s)] = v
                waits.append((s, v))
            if waits:
                def emit(e, waits=waits):
                    for (ws, wv) in waits:
                        e.wait_ge(ws, wv)
                self.q[eng].append(emit)
                self.ninst += len(waits)

    def flush(self):
        b = self.block
        m = {"pe": b.tensor, "act": b.scalar, "dve": b.vector, "pool": b.gpsimd, "sp": b.sync}
        for eng in self.ENG:
            lst = self.q[eng]
            if not lst:
                continue

            def body(e, lst=lst):
                for f in lst:
                    f(e)
            m[eng](body)
            self.q[eng] = []

    @contextlib.contextmanager
    def phase(self):
        with contextlib.ExitStack() as pes:
            old = self.scope
            self.scope = pes
            nlive = list(self.live)
            yield pes
            self.barrier()
            self.flush()
            self.release([b for b in self.live if b not in nlive])
            self.scope = old


class Rot:
    def __init__(self, bufs):
        self.bufs = bufs
        self.i = 0

    def next(self):
        b = self.bufs[self.i % len(self.bufs)]
        self.i += 1
        return b


def sblk_col(s):
    return 512 * s if s < 12 else 6168 + 512 * (s - 12)


def build_program(T_P, T_S, PAST, DEPTH):
    nc = bass.Bass("TRN2", target_bir_lowering=False)

    def din(name, shape, dt=F32):
        return nc.dram_tensor(name, list(shape), dt, kind="ExternalInput").ap()

    def dout(name, shape, dt=F32):
        return nc.dram_tensor(name, list(shape), dt, kind="ExternalOutput").ap()

    def dscr(name, shape, dt=BF16):
        return nc.dram_tensor(name, list(shape), dt, kind="Internal").ap()

    NPB = PAST // 128
    I = dict(
        xp=din("xp", [T_P, D]), xs=din("xs", [T_S, D]),
        ck=din("ck", [DEPTH, PAST, GW]), cv=din("cv", [DEPTH, PAST, GW]),
        sg=din("sg", [DEPTH, H, HD, HD]), sgc=din("sgc", [DEPTH, 3, 3 * GW]), ssc=din("ssc", [DEPTH, 2, 1024]),
        w_in=din("w_in", [DEPTH, D, DIN]), w_out=din("w_out", [DEPTH, D, D]),
        norm_pre=din("norm_pre", [DEPTH, D]), norm_post=din("norm_post", [DEPTH, D]),
        gcw=din("gcw", [DEPTH, 4, 3 * GW]), alog=din("alog", [DEPTH, H]), dtb=din("dtb", [DEPTH, H]),
        gnorm=din("gnorm", [DEPTH, HD]), scw=din("scw", [DEPTH, 3, 1024]),
        cst=din("cst", [128, 5 * 128 + 4 * 512]),
        mks=din("mks", [128, 14 * 128]),
    )
    O = {}
    for sfx, T in (("p", T_P), ("s", T_S)):
        O["y" + sfx] = dout("y" + sfx, [T, D])
        O["k" + sfx] = dout("k" + sfx, [DEPTH, T, GW])
        O["v" + sfx] = dout("v" + sfx, [DEPTH, T, GW])
        O["g" + sfx] = dout("g" + sfx, [DEPTH, H, HD, HD])
        O["gc" + sfx] = dout("gc" + sfx, [DEPTH, 3, 3 * GW])
        O["sc" + sfx] = dout("sc" + sfx, [DEPTH, 2, 1024])
    wbf = dscr("wbf", [32, 128, 32, 512])
    wab = dscr("wab", [128, 32, 24])
    wobf = dscr("wobf", [8, 128, 32, 512])
    SCR = {}
    for sfx, T in (("p", T_P), ("s", T_S)):
        SCR[sfx] = dict(
            gqT=dscr("gqT" + sfx, [H, 128, T]), gkT=dscr("gkT" + sfx, [H, 128, T]),
            gk=dscr("gk" + sfx, [T, GW]), gv=dscr("gv" + sfx, [T, GW]),
            azT=dscr("azT" + sfx, [H, 128, T]), gbt=dscr("gbt" + sfx, [T, 24], F32),
            QT=dscr("QT" + sfx, [H, 128, T]), bzT=dscr("bzT" + sfx, [H, 128, T]),
            mixT=dscr("mixT" + sfx, [32, 128, T]),
        )

    with contextlib.ExitStack() as es:
        es.enter_context(nc.allow_non_contiguous_dma(reason="small strided parameter / state transfers"))
        es.enter_context(nc.allow_low_precision(reason="bf16 matmul operands, fp32 accumulation"))
        block = es.enter_context(nc.Block())
        mk = MK(nc, es, block)
        Dw = mk.dram("wbf")
        Dwo = mk.dram("wobf")
        DS = {sfx: {k: mk.dram(k + sfx) for k in SCR[sfx]} for sfx in ("p", "s")}
        DY = {sfx: mk.dram("y" + sfx) for sfx in ("p", "s")}
        DKV = {sfx: mk.dram("kv" + sfx) for sfx in ("p", "s")}

        cst = mk.sb("cst", [128, 5 * 128 + 4 * 512])
        mk.dma("sp", lambda e: e.dma_start(out=cst.t[:], in_=I["cst"]), cst, writes=[cst])
        ident = cst.t[:, 0:128]
        UINC = cst.t[:, 128:256]
        SUS = cst.t[:, 256:384]
        MASKU = cst.t[:, 384:512]
        MASKL = cst.t[:, 512:640]
        MD = cst.t[:, 640:640 + 2048].rearrange("p (a b) -> p a b", a=4)
        cbf = mk.sb("cbf", [128, 4 * 128], BF16)
        onesf = mk.sb("onesf", [128, 128])
        mk.op("dve", lambda e: e.tensor_copy(out=cbf.t[:, 0:128], in_=ident), reads=[cst], writes=[cbf])
        mk.op("dve", lambda e: e.memset(cbf.t[:, 128:256], 1.0), writes=[cbf])
        mk.op("dve", lambda e: e.tensor_copy(out=cbf.t[:, 256:384], in_=SUS), reads=[cst], writes=[cbf])
        mk.op("dve", lambda e: e.tensor_tensor(out=cbf.t[:, 384:512], in0=SUS, in1=ident, op=ALU.add), reads=[cst], writes=[cbf])
        mk.op("dve", lambda e: e.memset(onesf.t[:], 1.0), writes=[onesf])
        mkb = mk.sb("mkb", [128, 14, 128], BF16)
        with mk.phase():
            mkf = mk.sb("mkf", [128, 14 * 128])
            mk.dma("sp", lambda e: e.dma_start(out=mkf.t[:], in_=I["mks"]), mkf, writes=[mkf])
            mk.op("dve", lambda e: e.tensor_copy(out=mkb.t[:].rearrange("p a b -> p (a b)"), in_=mkf.t[:]), reads=[mkf], writes=[mkb])
        IDB = cbf.t[:, 0:128]
        ONESB = cbf.t[:, 128:256]
        SUSB = cbf.t[:, 256:384]
        TRIB = cbf.t[:, 384:512]
        gpreT = mk.sb("gpreT", [128, 32])
        cwT = mk.sb("cwT", [128, 36, 4])
        scwT = mk.sb("scwT", [128, 8, 3])
        negA = mk.sb("negA", [128, H])
        dtbb = mk.sb("dtbb", [128, H])
        gnT = mk.sb("gnT", [128, 1])

        def load_params(l):
            mk.dma("sp", lambda e: e.dma_start(out=gpreT.t[:], in_=I["norm_pre"][l].rearrange("(c p) -> p c", p=128)), gpreT, writes=[gpreT])
            for i in range(4):
                mk.dma("sp", lambda e, i=i: e.dma_start(out=cwT.t[:, :, i], in_=I["gcw"][l, i].rearrange("(c p) -> p c", p=128)), cwT, writes=[cwT])
            for i in range(3):
                mk.dma("sp", lambda e, i=i: e.dma_start(out=scwT.t[:, :, i], in_=I["scw"][l, i].rearrange("(c p) -> p c", p=128)), scwT, writes=[scwT])
            mk.dma("sp", lambda e: e.dma_start(out=negA.t[:], in_=I["alog"][l:l + 1, :].to_broadcast([128, H])), negA, writes=[negA])
            mk.dma("sp", lambda e: e.dma_start(out=dtbb.t[:], in_=I["dtb"][l:l + 1, :].to_broadcast([128, H])), dtbb, writes=[dtbb])
            mk.dma("sp", lambda e: e.dma_start(out=gnT.t[:], in_=I["gnorm"][l].rearrange("(p o) -> p o", o=1)), gnT, writes=[gnT])
            mk.op("act", lambda e: e.activation(out=negA.t[:], in_=negA.t[:], func=AF.Exp), reads=[negA], writes=[negA])
            mk.op("dve", lambda e: e.tensor_scalar(out=negA.t[:], in0=negA.t[:], scalar1=-1.0, scalar2=None, op0=ALU.mult), reads=[negA], writes=[negA])

        def precast(l):
            with mk.phase():
                wf = Rot([mk.sb("wf%d" % i, [128, 8, 512]) for i in range(3)])
                wb = Rot([mk.sb("wb%d" % i, [128, 8, 512], BF16) for i in range(3)])
                n = 0
                for s in range(32):
                    c0 = sblk_col(s)
                    for kg in range(4):
                        f = wf.next()
                        b = wb.next()
                        src = I["w_in"][l, kg * 1024:(kg + 1) * 1024, c0:c0 + 512].rearrange("(kc p) n -> p kc n", p=128)
                        mk.dma("sp", lambda e, f=f, src=src: e.dma_start(out=f.t[:], in_=src), f, writes=[f])
                        eng = "dve" if n % 2 == 0 else "pool"
                        n += 1
                        gsl = gpreT.t[:, kg * 8:(kg + 1) * 8].unsqueeze(2).to_broadcast([128, 8, 512])
                        mk.op(eng, lambda e, f=f, b=b, gsl=gsl: e.tensor_tensor(out=b.t[:], in0=f.t[:], in1=gsl, op=ALU.mult), reads=[f, gpreT], writes=[b])
                        dst = wbf[s, :, kg * 8:(kg + 1) * 8, :]
                        mk.dma("sp", lambda e, b=b, dst=dst: e.dma_start(out=dst, in_=b.t[:]), b, reads=[b], writes=[Dw])
                fab = mk.sb("fab", [128, 32, 24])
                bab = mk.sb("bab", [128, 32, 24], BF16)
                mk.dma("sp", lambda e: e.dma_start(out=fab.t[:], in_=I["w_in"][l, :, 6144:6168].rearrange("(kc p) n -> p kc n", p=128)), fab, writes=[fab])
                mk.op("dve", lambda e: e.tensor_tensor(out=bab.t[:], in0=fab.t[:], in1=gpreT.t[:].unsqueeze(2).to_broadcast([128, 32, 24]), op=ALU.mult), reads=[fab, gpreT], writes=[bab])
                mk.dma("sp", lambda e: e.dma_start(out=wab, in_=bab.t[:]), bab, reads=[bab], writes=[Dw])
                for nb in range(8):
                    for kg in range(4):
                        f = wf.next()
                        b = wb.next()
                        src = I["w_out"][l, kg * 1024:(kg + 1) * 1024, nb * 512:(nb + 1) * 512].rearrange("(kc p) n -> p kc n", p=128)
                        mk.dma("sp", lambda e, f=f, src=src: e.dma_start(out=f.t[:], in_=src), f, writes=[f])
                        eng = "dve" if n % 2 == 0 else "pool"
                        n += 1
                        mk.op(eng, lambda e, f=f, b=b: e.tensor_copy(out=b.t[:], in_=f.t[:]), reads=[f], writes=[b])
                        dst = wobf[nb, :, kg * 8:(kg + 1) * 8, :]
                        mk.dma("sp", lambda e, b=b, dst=dst: e.dma_start(out=dst, in_=b.t[:]), b, reads=[b], writes=[Dwo])

        SECT = ([("qkv", s) for s in range(9)] + [("az", s) for s in range(9, 12)] + [("ab", None)] +
                [("bq", s) for s in range(12, 15)] + [("bk", s) for s in range(15, 18)] + [("bv", s) for s in range(18, 21)] +
                [("bz", s) for s in range(21, 24)] +
                [("cc", 26), ("ch", 28), ("cb", 24), ("cz", 30), ("cc", 27), ("ch", 29), ("cb", 25), ("cz", 31)])
        SECBASE = dict(qkv=0, az=9, bq=12, bk=15, bv=18, bz=21, cb=24, cc=26, ch=28, cz=30)

        def proj_phase(sfx, T, l, xin, Dxin):
            S = SCR[sfx]
            DSs = DS[sfx]
            TT = min(128, T)
            TS = min(512, T)
            NTT = TS // TT
            NST = T // TS
            with mk.phase():
                hT = mk.sb("hT", [128, 32, TS], BF16)
                Wt = Rot([mk.sb("Wt%d" % i, [128, 32, 512], BF16) for i in range(2)])
                Wab = mk.sb("Wab", [128, 32, 24], BF16)
                xt = Rot([mk.sb("xt%d" % i, [TT, D]) for i in range(1)])
                junk = mk.sb("junk", [TT, D], BF16)
                st1 = Rot([mk.sb("st1_%d" % i, [TT, 2]) for i in range(2)])
                pm = Rot([mk.ps("pm%d" % i, [128, 512]) for i in range(4)])
                ptr = Rot([mk.ps("ptr%d" % i, [128, 512]) for i in range(2)])
                pn = Rot([mk.ps("pn%d" % i, [128, 512]) for i in range(2)])
                xa = Rot([mk.sb("xa%d" % i, [128, TS + 3]) for i in range(2)])
                acc = Rot([mk.sb("acc%d" % i, [128, TS]) for i in range(2)])
                sil = Rot([mk.sb("sil%d" % i, [128, TS]) for i in range(2)])
                sqb = Rot([mk.sb("sqb%d" % i, [128, TS], BF16) for i in range(2)])
                rr = Rot([mk.sb("rr%d" % i, [128, TS]) for i in range(2)])
                snf = Rot([mk.sb("snf%d" % i, [128, TS]) for i in range(2)])
                obf = Rot([mk.sb("obf%d" % i, [128, TS], BF16) for i in range(3)])
                tokb = Rot([mk.sb("tokb%d" % i, [TT, NTT, 128], BF16) for i in range(2)])
                kvst = Rot([mk.sb("kvst%d" % i, [TT, 512]) for i in range(2)])
                abt = Rot([mk.sb("abt%d" % i, [TT, 24]) for i in range(2)])
                abo = Rot([mk.sb("abo%d" % i, [TT, 24]) for i in range(2)])
                carry = mk.sb("carry", [128, 36, 3])
                sccarry = mk.sb("sccarry", [128, 8, 2])
                scU = mk.sb("scU", [128, 4, TS + 2])
                scV = mk.sb("scV", [128, 4, TS])
                if sfx == "p":
                    mk.op("dve", lambda e: e.memset(carry.t[:], 0.0), writes=[carry])
                    mk.op("dve", lambda e: e.memset(sccarry.t[:], 0.0), writes=[sccarry])
                else:
                    for t in range(3):
                        mk.dma("sp", lambda e, t=t: e.dma_start(out=carry.t[:, :, t], in_=I["sgc"][l, t].rearrange("(c p) -> p c", p=128)), carry, writes=[carry])
                    for t in range(2):
                        mk.dma("sp", lambda e, t=t: e.dma_start(out=sccarry.t[:, :, t], in_=I["ssc"][l, t].rearrange("(c p) -> p c", p=128)), sccarry, writes=[sccarry])
                mk.dma("sp", lambda e: e.dma_start(out=Wab.t[:], in_=wab), Wab, reads=[Dw], writes=[Wab])
                nev = [0]

                def evac_eng():
                    nev[0] += 1
                    return "act" if nev[0] % 2 else "dve"

                def copy_op(eng, out, in_, reads, writes, scale=None):
                    if eng == "act":
                        if scale is None:
                            mk.op("act", lambda e: e.activation(out=out, in_=in_, func=AF.Copy), reads=reads, writes=writes)
                        else:
                            mk.op("act", lambda e: e.activation(out=out, in_=in_, func=AF.Copy, scale=scale), reads=reads, writes=writes)
                    else:
                        if scale is None:
                            mk.op("dve", lambda e: e.tensor_copy(out=out, in_=in_), reads=reads, writes=writes)
                        else:
                            mk.op("dve", lambda e: e.tensor_scalar(out=out, in0=in_, scalar1=scale, scalar2=None, op0=ALU.mult), reads=reads, writes=writes)

                for st in range(NST):
                    t0 = st * TS
                    for tt in range(NTT):
                        x = xt.next()
                        s1 = st1.next()
                        r0 = t0 + tt * TT
                        mk.dma("sp", lambda e, x=x, r0=r0: e.dma_start(out=x.t[:], in_=xin[r0:r0 + TT, :]), x, reads=[Dxin], writes=[x])
                        mk.op("act", lambda e, x=x, s1=s1: e.activation(out=junk.t[:], in_=x.t[:], func=AF.Square, accum_out=s1.t[:, 0:1]), reads=[x], writes=[junk, s1])
                        mk.op("act", lambda e, s1=s1: e.activation(out=s1.t[:, 1:2], in_=s1.t[:, 0:1], func=AF.Sqrt, scale=1.0 / D, bias=EPS), reads=[s1], writes=[s1])
                        mk.op("dve", lambda e, s1=s1: e.reciprocal(out=s1.t[:, 1:2], in_=s1.t[:, 1:2]), reads=[s1], writes=[s1])
                        xn = x
                        mk.op("dve", lambda e, x=x, s1=s1: e.tensor_scalar(out=x.t[:], in0=x.t[:], scalar1=s1.t[:, 1:2], scalar2=None, op0=ALU.mult), reads=[x, s1], writes=[x])
                        for k4 in range(8):
                            p = ptr.next()
                            for j in range(4):
                                kc = k4 * 4 + j
                                mk.op("pe", lambda e, p=p, j=j, kc=kc: e.transpose(out=p.t[:, j * TT:(j + 1) * TT], in_=xn.t[:TT, kc * 128:(kc + 1) * 128], identity=ident[:TT, :TT]), reads=[xn, cst], writes=[p])
                            copy_op(evac_eng(), hT.t[:, k4 * 4:(k4 + 1) * 4, tt * TT:(tt + 1) * TT], p.t[:, 0:4 * TT].rearrange("p (a b) -> p a b", a=4), [p], [hT])
                    for (sec, s) in SECT:
                        if sec == "ab":
                            for tt in range(NTT):
                                p = pm.next()
                                for kc in range(32):
                                    mk.op("pe", lambda e, p=p, kc=kc, tt=tt: e.matmul(p.t[:TT, 0:24], lhsT=hT.t[:, kc, tt * TT:(tt + 1) * TT], rhs=Wab.t[:, kc, :], start=(kc == 0), stop=(kc == 31)), reads=[hT, Wab], writes=[p])
                                a1 = abt.next()
                                ao = abo.next()
                                mk.op("dve", lambda e, p=p, a1=a1: e.tensor_tensor(out=a1.t[:, 0:12], in0=p.t[:TT, 0:12], in1=dtbb.t[:TT, :], op=ALU.add), reads=[p, dtbb], writes=[a1])
                                mk.op("act", lambda e, a1=a1: e.activation(out=a1.t[:, 0:12], in_=a1.t[:, 0:12], func=AF.Exp), reads=[a1], writes=[a1])
                                mk.op("act", lambda e, a1=a1: e.activation(out=a1.t[:, 0:12], in_=a1.t[:, 0:12], func=AF.Ln, bias=1.0), reads=[a1], writes=[a1])
                                mk.op("dve", lambda e, a1=a1, ao=ao: e.tensor_tensor(out=ao.t[:, 0:12], in0=a1.t[:, 0:12], in1=negA.t[:TT, :], op=ALU.mult), reads=[a1, negA], writes=[ao])
                                mk.op("act", lambda e, p=p, ao=ao: e.activation(out=ao.t[:, 12:24], in_=p.t[:TT, 12:24], func=AF.Sigmoid), reads=[p, ao], writes=[ao])
                                r0 = t0 + tt * TT
                                mk.dma("pool", lambda e, ao=ao, r0=r0: e.dma_start(out=S["gbt"][r0:r0 + TT, :], in_=ao.t[:]), ao, reads=[ao], writes=[DSs["gbt"]])
                            continue
                        W = Wt.next()
                        mk.dma("sp", lambda e, W=W, s=s: e.dma_start(out=W.t[:], in_=wbf[s]), W, reads=[Dw], writes=[W])
                        if sec in ("bk", "bv"):
                            okv = O[("k" if sec == "bk" else "v") + sfx]
                            c0 = (s - SECBASE[sec]) * 512
                            for tt in range(NTT):
                                p = pm.next()
                                for kc in range(32):
                                    mk.op("pe", lambda e, p=p, kc=kc, tt=tt, W=W: e.matmul(p.t[:TT, :], lhsT=hT.t[:, kc, tt * TT:(tt + 1) * TT], rhs=W.t[:, kc, :], start=(kc == 0), stop=(kc == 31)), reads=[hT, W], writes=[p])
                                kv = kvst.next()
                                copy_op(evac_eng(), kv.t[:], p.t[:TT, :], [p], [kv])
                                r0 = t0 + tt * TT
                                mk.dma("pool", lambda e, kv=kv, r0=r0, c0=c0, okv=okv: e.dma_start(out=okv[l, r0:r0 + TT, c0:c0 + 512], in_=kv.t[:]), kv, reads=[kv], writes=[DKV[sfx]])
                            continue
                        for c in range(4):
                            ci = (s - SECBASE[sec]) * 4 + c
                            p = pm.next()
                            for kc in range(32):
                                mk.op("pe", lambda e, p=p, kc=kc, c=c, W=W: e.matmul(p.t[:, 0:TS], lhsT=W.t[:, kc, c * 128:(c + 1) * 128], rhs=hT.t[:, kc, :], start=(kc == 0), stop=(kc == 31)), reads=[hT, W], writes=[p])
                            P = p.t[:, 0:TS]
                            if sec == "qkv":
                                a = xa.next()
                                mk.op("act", lambda e, a=a, P=P: e.activation(out=a.t[:, 3:3 + TS], in_=P, func=AF.Copy), reads=[p], writes=[a])
                                mk.op("dve", lambda e, a=a, ci=ci: e.tensor_copy(out=a.t[:, 0:3], in_=carry.t[:, ci, :]), reads=[carry, a], writes=[a])
                                ac = acc.next()
                                mk.op("dve", lambda e, a=a, ac=ac, ci=ci: e.tensor_scalar(out=ac.t[:], in0=a.t[:, 0:TS], scalar1=cwT.t[:, ci, 0:1], scalar2=None, op0=ALU.mult), reads=[a, cwT], writes=[ac])
                                for i in range(1, 4):
                                    mk.op("dve", lambda e, a=a, ac=ac, ci=ci, i=i: e.scalar_tensor_tensor(out=ac.t[:], in0=a.t[:, i:i + TS], scalar=cwT.t[:, ci, i:i + 1], in1=ac.t[:], op0=ALU.mult, op1=ALU.add), reads=[a, cwT, ac], writes=[ac])
                                mk.op("dve", lambda e, a=a, ci=ci: e.tensor_copy(out=carry.t[:, ci, :], in_=a.t[:, TS:TS + 3]), reads=[a, carry], writes=[carry])
                                sl = sil.next()
                                mk.op("act", lambda e, ac=ac, sl=sl: e.activation(out=sl.t[:], in_=ac.t[:], func=AF.Silu), reads=[ac], writes=[sl])
                                kind = ci // 12
                                hh = ci % 12
                                if kind < 2:
                                    sq = sqb.next()
                                    mk.op("dve", lambda e, sl=sl, sq=sq: e.tensor_tensor(out=sq.t[:], in0=sl.t[:], in1=sl.t[:], op=ALU.mult), reads=[sl], writes=[sq])
                                    pp = pn.next()
                                    mk.op("pe", lambda e, pp=pp, sq=sq: e.matmul(pp.t[:, 0:TS], lhsT=ONESB, rhs=sq.t[:], start=True, stop=True), reads=[cbf, sq], writes=[pp])
                                    r = rr.next()
                                    scl = 128.0 if kind == 0 else 1.0
                                    mk.op("act", lambda e, pp=pp, r=r, scl=scl: e.activation(out=r.t[:], in_=pp.t[:, 0:TS], func=AF.Sqrt, scale=scl, bias=EPS * scl), reads=[pp], writes=[r])
                                    mk.op("dve", lambda e, r=r: e.reciprocal(out=r.t[:], in_=r.t[:]), reads=[r], writes=[r])
                                    ob = obf.next()
                                    mk.op("dve", lambda e, sl=sl, r=r, ob=ob: e.tensor_tensor(out=ob.t[:], in0=sl.t[:], in1=r.t[:], op=ALU.mult), reads=[sl, r], writes=[ob])
                                    dstT = (S["gqT"] if kind == 0 else S["gkT"])[hh, :, t0:t0 + TS]
                                    mk.dma("pool", lambda e, ob=ob, dstT=dstT: e.dma_start(out=dstT, in_=ob.t[:]), ob, reads=[ob], writes=[DSs["gqT" if kind == 0 else "gkT"]])
                                    if kind == 1:
                                        sn = snf.next()
                                        mk.op("dve", lambda e, sl=sl, r=r, sn=sn: e.tensor_tensor(out=sn.t[:], in0=sl.t[:], in1=r.t[:], op=ALU.mult), reads=[sl, r], writes=[sn])
                                        src_f = sn
                                else:
                                    src_f = sl
                                if kind >= 1:
                                    pt_ = ptr.next()
                                    for j in range(NTT):
                                        mk.op("pe", lambda e, pt_=pt_, j=j, src_f=src_f: e.transpose(out=pt_.t[:TT, j * 128:(j + 1) * 128], in_=src_f.t[:, j * TT:(j + 1) * TT], identity=ident), reads=[src_f, cst], writes=[pt_])
                                    tb = tokb.next()
                                    copy_op(evac_eng(), tb.t[:], pt_.t[:TT, 0:NTT * 128].rearrange("p (a b) -> p a b", a=NTT), [pt_], [tb])
                                    dtok = (S["gk"] if kind == 1 else S["gv"])[t0:t0 + TS, hh * 128:(hh + 1) * 128].rearrange("(j p) d -> p j d", p=TT)
                                    mk.dma("pool", lambda e, tb=tb, dtok=dtok: e.dma_start(out=dtok, in_=tb.t[:]), tb, reads=[tb], writes=[DSs["gk" if kind == 1 else "gv"]])
                            elif sec == "az":
                                sl = sil.next()
                                mk.op("act", lambda e, sl=sl, P=P: e.activation(out=sl.t[:], in_=P, func=AF.Silu), reads=[p], writes=[sl])
                                ob = obf.next()
                                mk.op("dve", lambda e, sl=sl, ob=ob: e.tensor_scalar(out=ob.t[:], in0=sl.t[:], scalar1=gnT.t[:, 0:1], scalar2=None, op0=ALU.mult), reads=[sl, gnT], writes=[ob])
                                mk.dma("pool", lambda e, ob=ob, ci=ci: e.dma_start(out=S["azT"][ci, :, t0:t0 + TS], in_=ob.t[:]), ob, reads=[ob], writes=[DSs["azT"]])
                            elif sec == "bq":
                                ob = obf.next()
                                copy_op(evac_eng(), ob.t[:], P, [p], [ob], scale=HD ** -0.5)
                                mk.dma("pool", lambda e, ob=ob, ci=ci: e.dma_start(out=S["QT"][ci, :, t0:t0 + TS], in_=ob.t[:]), ob, reads=[ob], writes=[DSs["QT"]])
                            elif sec == "bz":
                                ob = obf.next()
                                mk.op("act", lambda e, ob=ob, P=P: e.activation(out=ob.t[:], in_=P, func=AF.Silu), reads=[p], writes=[ob])
                                mk.dma("pool", lambda e, ob=ob, ci=ci: e.dma_start(out=S["bzT"][ci, :, t0:t0 + TS], in_=ob.t[:]), ob, reads=[ob], writes=[DSs["bzT"]])
                            elif sec == "cc":
                                mk.op("act", lambda e, c=c, P=P: e.activation(out=scU.t[:, c, 2:2 + TS], in_=P, func=AF.Copy), reads=[p], writes=[scU])
                            elif sec == "ch":
                                mk.op("dve", lambda e, c=c, P=P: e.tensor_tensor(out=scU.t[:, c, 2:2 + TS], in0=P, in1=scU.t[:, c, 2:2 + TS], op=ALU.mult), reads=[p, scU], writes=[scU])
                                mk.op("dve", lambda e, c=c, ci=ci: e.tensor_copy(out=scU.t[:, c, 0:2], in_=sccarry.t[:, ci, :]), reads=[sccarry, scU], writes=[scU])
                                mk.op("dve", lambda e, c=c, ci=ci: e.tensor_scalar(out=scV.t[:, c, :], in0=scU.t[:, c, 0:TS], scalar1=scwT.t[:, ci, 0:1], scalar2=None, op0=ALU.mult), reads=[scU, scwT], writes=[scV])
                                for i in range(1, 3):
                                    mk.op("dve", lambda e, c=c, ci=ci, i=i: e.scalar_tensor_tensor(out=scV.t[:, c, :], in0=scU.t[:, c, i:i + TS], scalar=scwT.t[:, ci, i:i + 1], in1=scV.t[:, c, :], op0=ALU.mult, op1=ALU.add), reads=[scU, scwT, scV], writes=[scV])
                                mk.op("dve", lambda e, c=c, ci=ci: e.tensor_copy(out=sccarry.t[:, ci, :], in_=scU.t[:, c, TS:TS + 2]), reads=[scU, sccarry], writes=[sccarry])
                            elif sec == "cb":
                                mk.op("dve", lambda e, c=c, P=P: e.tensor_tensor(out=scV.t[:, c, :], in0=P, in1=scV.t[:, c, :], op=ALU.mult), reads=[p, scV], writes=[scV])
                            elif sec == "cz":
                                sl = sil.next()
                                mk.op("act", lambda e, sl=sl, P=P: e.activation(out=sl.t[:], in_=P, func=AF.Silu), reads=[p], writes=[sl])
                                ob = obf.next()
                                mk.op("dve", lambda e, sl=sl, ob=ob, c=c: e.tensor_tensor(out=ob.t[:], in0=sl.t[:], in1=scV.t[:, c, :], op=ALU.mult), reads=[sl, scV], writes=[ob])
                                mk.dma("pool", lambda e, ob=ob, ci=ci: e.dma_start(out=S["mixT"][24 + ci, :, t0:t0 + TS], in_=ob.t[:]), ob, reads=[ob], writes=[DSs["mixT"]])
                for t in range(3):
                    mk.dma("pool", lambda e, t=t: e.dma_start(out=O["gc" + sfx][l, t].rearrange("(c p) -> p c", p=128), in_=carry.t[:, :, t]), carry, reads=[carry])
                for t in range(2):
                    mk.dma("pool", lambda e, t=t: e.dma_start(out=O["sc" + sfx][l, t].rearrange("(c p) -> p c", p=128), in_=sccarry.t[:, :, t]), sccarry, reads=[sccarry])

        def gdn_phase(sfx, T, l):
            S = SCR[sfx]
            DSs = DS[sfx]
            C = min(128, T)
            NCH = T // C
            NLEV = 6 if C == 128 else 4
            W3 = H * C
            with mk.phase():
                mk.limit = getattr(mk, "nops", 0) + DEBUG_OPS
                Sst = mk.sb("Sst", [128, H, 128])
                Sbf = mk.sb("Sbf", [128, H, 128], BF16)
                if sfx == "p":
                    mk.op("dve", lambda e: e.memset(Sst.t[:], 0.0), writes=[Sst])
                else:
                    mk.dma("sp", lambda e: e.dma_start(out=Sst.t[:], in_=I["sg"][l].rearrange("h k v -> k h v")), Sst, writes=[Sst])
                mk.op("act", lambda e: e.activation(out=Sbf.t[:], in_=Sst.t[:], func=AF.Copy), reads=[Sst], writes=[Sbf])
                NB = 2
                qT = Rot([mk.sb("qT%d" % i, [128, H, C], BF16) for i in range(NB)])
                kT = Rot([mk.sb("kT%d" % i, [128, H, C], BF16) for i in range(NB)])
                ktok = Rot([mk.sb("ktok%d" % i, [C, H, 128], BF16) for i in range(NB)])
                vtok = Rot([mk.sb("vtok%d" % i, [C, H, 128], BF16) for i in range(NB)])
                gbt = Rot([mk.sb("gbt%d" % i, [C, 24]) for i in range(NB)])
                azt = Rot([mk.sb("azt%d" % i, [128, H, C], BF16) for i in range(NB)])
                pA = [mk.ps("pA%d" % i, [128, 512]) for i in range(3)]
                pB = [mk.ps("pB%d" % i, [128, 512]) for i in range(3)]
                pC = mk.ps("pC", [128, 512])
                pD = mk.ps("pD", [128, 512])
                sm = mk.sb("sm", [128, 8, H])
                X2 = mk.sb("X2", [C, H, C])
                D0 = mk.sb("D0", [C, H, C])
                DU = mk.sb("DU", [C, H, C])
                DL = mk.sb("DL", [C, H, C])
                EGB = mk.sb("EGB", [128, H, C], BF16)
                BBs = mk.sb("BBs", [C, H, C])
                tmp = mk.sb("tmp", [C, H, C])
                X1 = tmp
                Mx = [mk.sb("Mx%d" % i, [C, H, C]) for i in range(1)]
                MTx = [mk.sb("MTx%d" % i, [C, H, C]) for i in range(1)]
                Pm = [mk.sb("Pm%d" % i, [C, H, C]) for i in range(2)]
                Rm = [mk.sb("Rm%d" % i, [C, H, C]) for i in range(2)]
                PTb = mk.sb("PTb", [C, H, C], BF16)
                attnT = mk.sb("attnT", [C, H, C], BF16)
                vb = mk.sb("vb", [C, H, 128], BF16)
                kbg = mk.sb("kbg", [C, H, 128], BF16)
                kdec = mk.sb("kdec", [C, H, 128], BF16)
                usb = mk.sb("usb", [C, H, 128])
                wT = mk.sb("wT", [128, H, C], BF16)
                qdT = mk.sb("qdT", [128, H, C], BF16)
                vnew = mk.sb("vnew", [C, H, 128], BF16)
                sqo = mk.sb("sqo", [128, H, C], BF16)
                rno = D0 if C == 128 else mk.sb("rno", [128, H, C])
                yo = DL if C == 128 else mk.sb("yo", [128, H, C])
                yb = Rot([mk.sb("yb%d" % i, [128, H, C], BF16) for i in range(2)])
                f2 = lambda ap: ap.rearrange("p a b -> p (a b)")
                HG = min(H, 512 // C)
                HP = [(h0, min(H, h0 + HG)) for h0 in range(0, H, HG)]
                HP128 = [(h0, h0 + 4) for h0 in range(0, H, 4)]

                def BK(ps, h, w):
                    return ps[(h * w) // 512]

                def PW(ps, rows, h, w):
                    o = (h * w) % 512
                    return ps[(h * w) // 512].t[:rows, o:o + w]

                def P3(ps, rows, h0, h1, w):
                    o = (h0 * w) % 512
                    return ps[(h0 * w) // 512].t[:rows, o:o + (h1 - h0) * w].rearrange("p (a b) -> p a b", a=h1 - h0)

                def bch(ap2, h0, h1, n):
                    return ap2[:, h0:h1].unsqueeze(2).to_broadcast([ap2.shape[0], h1 - h0, n])

                def mb(m2, h0, h1):
                    return m2.unsqueeze(1).to_broadcast([m2.shape[0], h1 - h0, m2.shape[1]])

                def bc(ap2, n):
                    return ap2.unsqueeze(2).to_broadcast([ap2.shape[0], H, n])

                def mm_banks(dst, lhsT, rhs_buf, rhs2, width, reads):
                    for c0 in range(0, width, 512):
                        c1 = min(width, c0 + 512)
                        mk.op("pe", lambda e, c0=c0, c1=c1: e.matmul(dst[:, c0:c1], lhsT=lhsT, rhs=rhs2[:, c0:c1], start=True, stop=True), reads=reads, writes=[rhs_buf[1]])

                for c in range(NCH):
                    t0 = c * C
                    q_ = qT.next(); k_ = kT.next(); kt = ktok.next(); vt = vtok.next(); gb = gbt.next(); az = azt.next()
                    mk.dma("sp", lambda e, q_=q_: e.dma_start(out=q_.t[:], in_=S["gqT"][:, :, t0:t0 + C].rearrange("h d t -> d h t")), q_, reads=[DSs["gqT"]], writes=[q_])
                    mk.dma("sp", lambda e, k_=k_: e.dma_start(out=k_.t[:], in_=S["gkT"][:, :, t0:t0 + C].rearrange("h d t -> d h t")), k_, reads=[DSs["gkT"]], writes=[k_])
                    mk.dma("sp", lambda e, kt=kt: e.dma_start(out=kt.t[:].rearrange("p a b -> p (a b)"), in_=S["gk"][t0:t0 + C, :]), kt, reads=[DSs["gk"]], writes=[kt])
                    mk.dma("sp", lambda e, vt=vt: e.dma_start(out=vt.t[:].rearrange("p a b -> p (a b)"), in_=S["gv"][t0:t0 + C, :]), vt, reads=[DSs["gv"]], writes=[vt])
                    mk.dma("sp", lambda e, gb=gb: e.dma_start(out=gb.t[:], in_=S["gbt"][t0:t0 + C, :]), gb, reads=[DSs["gbt"]], writes=[gb])
                    mk.dma("sp", lambda e, az=az: e.dma_start(out=az.t[:], in_=S["azT"][:, :, t0:t0 + C].rearrange("h d t -> d h t")), az, reads=[DSs["azT"]], writes=[az])
                    G = gb.t[:, 0:12]
                    Bt = gb.t[:, 12:24]
                    gcum = sm.t[:C, 0, :]; eg = sm.t[:C, 1, :]; egl = sm.t[:C, 2, :]; bk = sm.t[:C, 3, :]; gl = sm.t[:, 4, :]
                    mk.op("pe", lambda e: e.matmul(pC.t[:C, 0:12], lhsT=UINC[:C, :C], rhs=G, start=True, stop=True), reads=[cst, gb], writes=[pC])
                    mk.op("pe", lambda e: e.matmul(pD.t[:, 0:12], lhsT=onesf.t[:C, :], rhs=G, start=True, stop=True), reads=[onesf, gb], writes=[pD])
                    mk.op("dve", lambda e: e.tensor_copy(out=gcum, in_=pC.t[:C, 0:12]), reads=[pC], writes=[sm])
                    mk.op("act", lambda e: e.activation(out=eg, in_=pC.t[:C, 0:12], func=AF.Exp), reads=[pC], writes=[sm])
                    mk.op("act", lambda e: e.activation(out=gl, in_=pD.t[:, 0:12], func=AF.Exp), reads=[pD], writes=[sm])
                    mk.op("dve", lambda e: e.tensor_tensor(out=egl, in0=pD.t[:C, 0:12], in1=gcum, op=ALU.subtract), reads=[pD, sm], writes=[sm])
                    mk.op("act", lambda e: e.activation(out=egl, in_=egl, func=AF.Exp), reads=[sm], writes=[sm])
                    mk.op("dve", lambda e: e.tensor_tensor(out=bk, in0=Bt, in1=eg, op=ALU.mult), reads=[gb, sm], writes=[sm])
                    mk.op("dve", lambda e: e.tensor_tensor(out=X1.t[:], in0=bc(G, C), in1=UINC[:C, :C].unsqueeze(1).to_broadcast([C, H, C]), op=ALU.mult), reads=[gb, cst], writes=[X1])
                    for c0 in range(0, W3, 512):
                        c1 = min(W3, c0 + 512)
                        mk.op("pe", lambda e, c0=c0, c1=c1: e.matmul(pA[c0 // 512].t[:, 0:c1 - c0], lhsT=onesf.t[:C, :], rhs=f2(X1.t[:])[:, c0:c1], start=True, stop=True), reads=[onesf, X1], writes=[pA[c0 // 512]])
                    for (h0, h1) in HP:
                        mk.op("dve", lambda e, h0=h0, h1=h1: e.tensor_tensor(out=D0.t[:, h0:h1, :], in0=P3(pA, C, h0, h1, C), in1=bch(gcum, h0, h1, C), op=ALU.subtract), reads=[BK(pA, h0, C), sm], writes=[D0])
                        mk.op("act", lambda e, h0=h0, h1=h1: e.activation(out=EGB.t[:, h0:h1, :], in_=P3(pA, 128, h0, h1, C), func=AF.Exp), reads=[BK(pA, h0, C), D0], writes=[EGB])
                    mk.op("pool", lambda e: e.tensor_tensor(out=DU.t[:], in0=D0.t[:], in1=MASKU[:C, :C].unsqueeze(1).to_broadcast([C, H, C]), op=ALU.add), reads=[D0, cst], writes=[DU])
                    mk.op("dve", lambda e: e.scalar_tensor_tensor(out=DL.t[:], in0=D0.t[:], scalar=-1.0, in1=MASKL[:C, :C].unsqueeze(1).to_broadcast([C, H, C]), op0=ALU.mult, op1=ALU.add), reads=[D0, cst], writes=[DL])
                    mk.op("act", lambda e: e.activation(out=f2(DU.t[:]), in_=f2(DU.t[:]), func=AF.Exp), reads=[DU], writes=[DU])
                    mk.op("act", lambda e: e.activation(out=f2(DL.t[:]), in_=f2(DL.t[:]), func=AF.Exp), reads=[DL], writes=[DL])
                    mk.op("pool", lambda e: e.tensor_tensor(out=X2.t[:], in0=bc(Bt, C), in1=ident[:C, :C].unsqueeze(1).to_broadcast([C, H, C]), op=ALU.mult), reads=[gb, cst], writes=[X2])
                    for c0 in range(0, W3, 512):
                        c1 = min(W3, c0 + 512)
                        mk.op("pe", lambda e, c0=c0, c1=c1: e.matmul(pB[c0 // 512].t[:C, 0:c1 - c0], lhsT=SUS[:C, :C], rhs=f2(X2.t[:])[:, c0:c1], start=True, stop=True), reads=[cst, X2], writes=[pB[c0 // 512]])
                    for (h0, h1) in HP:
                        mk.op("act", lambda e, h0=h0, h1=h1: e.activation(out=BBs.t[:, h0:h1, :], in_=P3(pB, C, h0, h1, C), func=AF.Copy), reads=[BK(pB, h0, C)], writes=[BBs])
                    for h in range(H):
                        mk.op("pe", lambda e, h=h: e.matmul(PW(pA, C, h, C), lhsT=k_.t[:, h, :], rhs=k_.t[:, h, :], start=True, stop=True), reads=[k_], writes=[BK(pA, h, C)])
                    for h in range(H):
                        mk.op("pe", lambda e, h=h: e.matmul(PW(pB, C, h, C), lhsT=k_.t[:, h, :], rhs=q_.t[:, h, :], start=True, stop=True), reads=[k_, q_], writes=[BK(pB, h, C)])
                    M0, MT0 = Mx[0], MTx[0]
                    for (h0, h1) in HP:
                        mk.op("dve", lambda e, h0=h0, h1=h1: e.tensor_tensor(out=tmp.t[:, h0:h1, :], in0=P3(pA, C, h0, h1, C), in1=DU.t[:, h0:h1, :], op=ALU.mult), reads=[BK(pA, h0, C), DU], writes=[tmp])
                    mk.op("dve", lambda e: e.scalar_tensor_tensor(out=f2(M0.t[:]), in0=f2(tmp.t[:]), scalar=-1.0, in1=f2(BBs.t[:]), op0=ALU.mult, op1=ALU.mult), reads=[tmp, BBs], writes=[M0])
                    for (h0, h1) in HP:
                        mk.op("dve", lambda e, h0=h0, h1=h1: e.tensor_tensor(out=tmp.t[:, h0:h1, :], in0=P3(pA, C, h0, h1, C), in1=DL.t[:, h0:h1, :], op=ALU.mult), reads=[BK(pA, h0, C), DL], writes=[tmp])
                    mk.op("dve", lambda e: e.scalar_tensor_tensor(out=MT0.t[:], in0=tmp.t[:], scalar=-1.0, in1=bc(Bt, C), op0=ALU.mult, op1=ALU.mult), reads=[tmp, gb], writes=[MT0])
                    for (h0, h1) in HP:
                        mk.op("dve", lambda e, h0=h0, h1=h1: e.tensor_tensor(out=attnT.t[:, h0:h1, :], in0=P3(pB, C, h0, h1, C), in1=DU.t[:, h0:h1, :], op=ALU.mult), reads=[BK(pB, h0, C), DU], writes=[attnT])
                    Mm, MTm, Es, Eps = DU, BBs, tmp, D0
                    NL = 7 if C == 128 else 5
                    Q, R = Pm[0], Rm[0]
                    mk.op("pool", lambda e: e.tensor_tensor(out=Q.t[:], in0=M0.t[:], in1=mb(mkb.t[:C, 7, 0:C], 0, H), op=ALU.mult), reads=[M0, mkb], writes=[Q])
                    mk.op("pool", lambda e: e.tensor_tensor(out=Q.t[:], in0=Q.t[:], in1=mb(ident[:C, :C], 0, H), op=ALU.add), reads=[Q, cst], writes=[Q])
                    mk.op("pool", lambda e: e.tensor_tensor(out=R.t[:], in0=MT0.t[:], in1=mb(mkb.t[:C, 0, 0:C], 0, H), op=ALU.mult), reads=[MT0, mkb], writes=[R])
                    mk.op("pool", lambda e: e.tensor_tensor(out=R.t[:], in0=R.t[:], in1=mb(ident[:C, :C], 0, H), op=ALU.add), reads=[R, cst], writes=[R])
                    cur = 0
                    for lev in range(1, NL):
                        Q, R = Pm[cur], Rm[cur]
                        Qn, Rn = Pm[1 - cur], Rm[1 - cur]
                        mk.op("pool", lambda e, lev=lev: e.tensor_tensor(out=Mm.t[:], in0=M0.t[:], in1=mb(mkb.t[:C, 7 + lev, 0:C], 0, H), op=ALU.mult), reads=[M0, mkb], writes=[Mm])
                        mk.op("pool", lambda e, lev=lev: e.tensor_tensor(out=MTm.t[:], in0=MT0.t[:], in1=mb(mkb.t[:C, lev, 0:C], 0, H), op=ALU.mult), reads=[MT0, mkb], writes=[MTm])
                        for h in range(H):
                            mk.op("pe", lambda e, h=h, Q=Q: e.matmul(PW(pA, C, h, C), lhsT=MTm.t[:, h, :], rhs=Q.t[:, h, :], start=True, stop=True), reads=[MTm, Q], writes=[BK(pA, h, C)])
                        for h in range(H):
                            mk.op("pe", lambda e, h=h, R=R: e.matmul(PW(pB, C, h, C), lhsT=Mm.t[:, h, :], rhs=R.t[:, h, :], start=True, stop=True), reads=[Mm, R], writes=[BK(pB, h, C)])
                        for (h0, h1) in HP:
                            mk.op("act", lambda e, h0=h0, h1=h1: e.activation(out=Es.t[:, h0:h1, :], in_=P3(pA, C, h0, h1, C), func=AF.Copy), reads=[BK(pA, h0, C)], writes=[Es])
                            mk.op("dve", lambda e, h0=h0, h1=h1: e.tensor_copy(out=Eps.t[:, h0:h1, :], in_=P3(pB, C, h0, h1, C)), reads=[BK(pB, h0, C)], writes=[Eps])
                        for h in range(H):
                            mk.op("pe", lambda e, h=h, R=R: e.matmul(PW(pA, C, h, C), lhsT=R.t[:, h, :], rhs=Es.t[:, h, :], start=True, stop=True), reads=[R, Es], writes=[BK(pA, h, C)])
                        for h in range(H):
                            mk.op("pe", lambda e, h=h, Q=Q: e.matmul(PW(pB, C, h, C), lhsT=Q.t[:, h, :], rhs=Eps.t[:, h, :], start=True, stop=True), reads=[Q, Eps], writes=[BK(pB, h, C)])
                        for (h0, h1) in HP:
                            mk.op("dve", lambda e, h0=h0, h1=h1, Q=Q, Qn=Qn: e.tensor_tensor(out=Qn.t[:, h0:h1, :], in0=P3(pA, C, h0, h1, C), in1=Q.t[:, h0:h1, :], op=ALU.add), reads=[BK(pA, h0, C), Q], writes=[Qn])
                            mk.op("dve", lambda e, h0=h0, h1=h1, R=R, Rn=Rn: e.tensor_tensor(out=Rn.t[:, h0:h1, :], in0=P3(pB, C, h0, h1, C), in1=R.t[:, h0:h1, :], op=ALU.add), reads=[BK(pB, h0, C), R], writes=[Rn])
                        cur = 1 - cur
                    mk.op("act", lambda e, Pf=Pm[cur]: e.activation(out=f2(PTb.t[:]), in_=f2(Pf.t[:]), func=AF.Copy), reads=[Pm[cur]], writes=[PTb])
                    PT = PTb
                    mk.op("pool", lambda e: e.tensor_tensor(out=vb.t[:], in0=vt.t[:], in1=bc(Bt, 128), op=ALU.mult), reads=[vt, gb], writes=[vb])
                    mk.op("pool", lambda e: e.tensor_tensor(out=kbg.t[:], in0=kt.t[:], in1=bc(bk, 128), op=ALU.mult), reads=[kt, sm], writes=[kbg])
                    mk.op("pool", lambda e: e.tensor_tensor(out=kdec.t[:], in0=kt.t[:], in1=bc(egl, 128), op=ALU.mult), reads=[kt, sm], writes=[kdec])
                    mk.op("pool", lambda e: e.tensor_tensor(out=qdT.t[:], in0=q_.t[:], in1=EGB.t[:], op=ALU.mult), reads=[q_, EGB], writes=[qdT])
                    for h in range(H):
                        mk.op("pe", lambda e, h=h: e.matmul(PW(pA, C, h, 128), lhsT=PT.t[:, h, :], rhs=vb.t[:, h, :], start=True, stop=True), reads=[PT, vb], writes=[BK(pA, h, 128)])
                    for h in range(H):
                        mk.op("pe", lambda e, h=h: e.matmul(PW(pB, 128, h, C), lhsT=kbg.t[:, h, :], rhs=PT.t[:, h, :], start=True, stop=True), reads=[PT, kbg], writes=[BK(pB, h, C)])
                    for (h0, h1) in HP128:
                        mk.op("act", lambda e, h0=h0, h1=h1: e.activation(out=usb.t[:, h0:h1, :], in_=P3(pA, C, h0, h1, 128), func=AF.Copy), reads=[BK(pA, h0, 128)], writes=[usb])
                    for (h0, h1) in HP:
                        mk.op("dve", lambda e, h0=h0, h1=h1: e.tensor_copy(out=wT.t[:, h0:h1, :], in_=P3(pB, 128, h0, h1, C)), reads=[BK(pB, h0, C)], writes=[wT])
                    for h in range(H):
                        mk.op("pe", lambda e, h=h: e.matmul(PW(pA, C, h, 128), lhsT=wT.t[:, h, :], rhs=Sbf.t[:, h, :], start=True, stop=True), reads=[wT, Sbf], writes=[BK(pA, h, 128)])
                    for (h0, h1) in HP128:
                        mk.op("dve", lambda e, h0=h0, h1=h1: e.tensor_tensor(out=vnew.t[:, h0:h1, :], in0=usb.t[:, h0:h1, :], in1=P3(pA, C, h0, h1, 128), op=ALU.subtract), reads=[usb, BK(pA, h0, 128)], writes=[vnew])
                    for h in range(H):
                        mk.op("pe", lambda e, h=h: e.matmul(PW(pB, 128, h, C), lhsT=Sbf.t[:, h, :], rhs=qdT.t[:, h, :], start=True, stop=False), reads=[Sbf, qdT], writes=[BK(pB, h, C)])
                        mk.op("pe", lambda e, h=h: e.matmul(PW(pB, 128, h, C), lhsT=vnew.t[:, h, :], rhs=attnT.t[:, h, :], start=False, stop=True), reads=[vnew, attnT], writes=[BK(pB, h, C)])
                    for h in range(H):
                        mk.op("pe", lambda e, h=h: e.matmul(PW(pA, 128, h, 128), lhsT=kdec.t[:, h, :], rhs=vnew.t[:, h, :], start=True, stop=True), reads=[kdec, vnew], writes=[BK(pA, h, 128)])
                    mk.op("dve", lambda e: e.tensor_tensor(out=Sst.t[:], in0=Sst.t[:], in1=sm.t[:, 4, :].unsqueeze(2).to_broadcast([128, H, 128]), op=ALU.mult), reads=[Sst, sm], writes=[Sst])
                    for (h0, h1) in HP128:
                        mk.op("dve", lambda e, h0=h0, h1=h1: e.tensor_tensor(out=Sst.t[:, h0:h1, :], in0=Sst.t[:, h0:h1, :], in1=P3(pA, 128, h0, h1, 128), op=ALU.add), reads=[Sst, BK(pA, h0, 128)], writes=[Sst])
                    mk.op("act", lambda e: e.activation(out=f2(Sbf.t[:]), in_=f2(Sst.t[:]), func=AF.Copy), reads=[Sst], writes=[Sbf])
                    for (h0, h1) in HP:
                        mk.op("act", lambda e, h0=h0, h1=h1: e.activation(out=sqo.t[:, h0:h1, :], in_=P3(pB, 128, h0, h1, C), func=AF.Square), reads=[BK(pB, h0, C)], writes=[sqo])
                    for c0 in range(0, W3, 512):
                        c1 = min(W3, c0 + 512)
                        mk.op("pe", lambda e, c0=c0, c1=c1: e.matmul(pA[c0 // 512].t[:, 0:c1 - c0], lhsT=ONESB, rhs=f2(sqo.t[:])[:, c0:c1], start=True, stop=True), reads=[cbf, sqo], writes=[pA[c0 // 512]])
                    for (h0, h1) in HP:
                        mk.op("act", lambda e, h0=h0, h1=h1: e.activation(out=rno.t[:, h0:h1, :], in_=P3(pA, 128, h0, h1, C), func=AF.Sqrt, scale=1.0 / 128, bias=EPS), reads=[BK(pA, h0, C)], writes=[rno])
                    mk.op("dve", lambda e: e.reciprocal(out=f2(rno.t[:]), in_=f2(rno.t[:])), reads=[rno], writes=[rno])
                    for (h0, h1) in HP:
                        mk.op("dve", lambda e, h0=h0, h1=h1: e.tensor_tensor(out=yo.t[:, h0:h1, :], in0=P3(pB, 128, h0, h1, C), in1=rno.t[:, h0:h1, :], op=ALU.mult), reads=[BK(pB, h0, C), rno], writes=[yo])
                    y_ = yb.next()
                    mk.op("pool", lambda e, y_=y_: e.tensor_tensor(out=y_.t[:], in0=yo.t[:], in1=az.t[:], op=ALU.mult), reads=[yo, az], writes=[y_])
                    mk.dma("pool", lambda e, y_=y_: e.dma_start(out=S["mixT"][0:12, :, t0:t0 + C].rearrange("h d t -> d h t"), in_=y_.t[:]), y_, reads=[y_], writes=[DSs["mixT"]])
                mk.dma("pool", lambda e: e.dma_start(out=O["g" + sfx][l].rearrange("h k v -> k h v"), in_=Sst.t[:]), Sst, reads=[Sst])

        def sb_phase(sfx, T, l):
            S = SCR[sfx]
            DSs = DS[sfx]
            KB = min(128, T)
            NQ_ = min(512, T)
            NQT = T // NQ_
            NNB = T // KB
            npast = NPB if sfx == "s" else 0
            NBT = npast + NNB
            with mk.phase():
                QTh = Rot([mk.sb("QTh%d" % i, [128, T], BF16) for i in range(2)])
                bzh = Rot([mk.sb("bzh%d" % i, [128, T], BF16) for i in range(2)])
                Kf = Rot([mk.sb("Kf%d" % i, [128, NBT, 128]) for i in range(2)])
                Vf = Rot([mk.sb("Vf%d" % i, [128, NBT, 128]) for i in range(2)])
                KTh = Rot([mk.sb("KTh%d" % i, [128, NBT, 128], BF16) for i in range(2)])
                Vb = Rot([mk.sb("Vb%d" % i, [128, NBT, 128], BF16) for i in range(2)])
                pz = Rot([mk.ps("pz%d" % i, [128, 512]) for i in range(2)])
                pc = Rot([mk.ps("pc%d" % i, [128, 512]) for i in range(2)])
                po = Rot([mk.ps("po%d" % i, [128, 512]) for i in range(2)])
                pt = Rot([mk.ps("pt%d" % i, [128, 512]) for i in range(2)])
                zs = Rot([mk.sb("zs%d" % i, [128, NQ_]) for i in range(3)])
                ee = Rot([mk.sb("ee%d" % i, [128, NQ_]) for i in range(2)])
                sp_ = Rot([mk.sb("sp%d" % i, [128, NQ_], BF16) for i in range(3)])
                lg = Rot([mk.sb("lg%d" % i, [128, NQ_]) for i in range(2)])
                aT = Rot([mk.sb("aT%d" % i, [128, NQ_], BF16) for i in range(3)])
                Rr = Rot([mk.sb("R%d" % i, [128, NQ_], BF16) for i in range(2)])
                om = Rot([mk.sb("om%d" % i, [128, NQ_], BF16) for i in range(2)])
                kout = O["k" + sfx]
                vout = O["v" + sfx]
                for h in range(H):
                    qh = QTh.next(); bz = bzh.next(); kf = Kf.next(); vf = Vf.next(); kth = KTh.next(); vb = Vb.next()
                    hs = slice(h * 128, (h + 1) * 128)
                    mk.dma("sp", lambda e, qh=qh, h=h: e.dma_start(out=qh.t[:], in_=S["QT"][h]), qh, reads=[DSs["QT"]], writes=[qh])
                    mk.dma("sp", lambda e, bz=bz, h=h: e.dma_start(out=bz.t[:], in_=S["bzT"][h]), bz, reads=[DSs["bzT"]], writes=[bz])
                    if npast:
                        mk.dma("sp", lambda e, kf=kf, hs=hs: e.dma_start(out=kf.t[:, 0:npast, :], in_=I["ck"][l, :, hs].rearrange("(b p) d -> p b d", p=128)), kf, writes=[kf])
                        mk.dma("sp", lambda e, vf=vf, hs=hs: e.dma_start(out=vf.t[:, 0:npast, :], in_=I["cv"][l, :, hs].rearrange("(b p) d -> p b d", p=128)), vf, writes=[vf])
                    mk.dma("sp", lambda e, kf=kf, hs=hs: e.dma_start(out=kf.t[:KB, npast:NBT, :], in_=kout[l, :, hs].rearrange("(b p) d -> p b d", p=KB)), kf, reads=[DKV[sfx]], writes=[kf])
                    mk.dma("sp", lambda e, vf=vf, hs=hs: e.dma_start(out=vf.t[:KB, npast:NBT, :], in_=vout[l, :, hs].rearrange("(b p) d -> p b d", p=KB)), vf, reads=[DKV[sfx]], writes=[vf])
                    for b0 in range(0, NBT, 4):
                        p = pt.next()
                        nb_ = min(4, NBT - b0)
                        kbs = []
                        for j in range(nb_):
                            b = b0 + j
                            kb = 128 if b < npast else KB
                            kbs.append(kb)
                            mk.op("pe", lambda e, p=p, j=j, b=b, kb=kb, kf=kf: e.transpose(out=p.t[:, j * 128:j * 128 + kb], in_=kf.t[:kb, b, :], identity=ident[:kb, :kb]), reads=[kf, cst], writes=[p])
                        if all(k == 128 for k in kbs):
                            mk.op("act", lambda e, p=p, b0=b0, nb_=nb_, kth=kth: e.activation(out=kth.t[:, b0:b0 + nb_, :].rearrange("p a b -> p (a b)"), in_=p.t[:, 0:nb_ * 128], func=AF.Copy), reads=[p], writes=[kth])
                        else:
                            for j in range(nb_):
                                mk.op("act", lambda e, p=p, b0=b0, j=j, kth=kth, kb=kbs[j]: e.activation(out=kth.t[:, b0 + j, 0:kb], in_=p.t[:, j * 128:j * 128 + kb], func=AF.Copy), reads=[p], writes=[kth])
                    if npast:
                        mk.op("pool", lambda e, vb=vb, vf=vf: e.tensor_copy(out=vb.t[:, 0:npast, :], in_=vf.t[:, 0:npast, :]), reads=[vf], writes=[vb])
                    mk.op("pool", lambda e, vb=vb, vf=vf: e.tensor_copy(out=vb.t[:KB, npast:NBT, :], in_=vf.t[:KB, npast:NBT, :]), reads=[vf], writes=[vb])
                    for qi in range(NQT):
                        q0 = qi * NQ_
                        blocks = []
                        nb_hi = (q0 + NQ_ - 1) // KB
                        for b in range(nb_hi, -1, -1):
                            bd = b - q0 // KB
                            blocks.append((npast + b, KB, bd if bd >= 0 else None))
                        for b in range(npast - 1, -1, -1):
                            blocks.append((b, 128, None))
                        R = Rr.next()
                        mk.op("pool", lambda e, R=R: e.memset(R.t[:], 0.0), writes=[R])
                        pov = po.next()
                        for bi, (b, kb, bd) in enumerate(blocks):
                            first = bi == 0
                            lastb = bi == len(blocks) - 1
                            z = pz.next()
                            mk.op("pe", lambda e, z=z, b=b, kb=kb, kth=kth, qh=qh: e.matmul(z.t[:kb, 0:NQ_], lhsT=kth.t[:, b, 0:kb], rhs=qh.t[:, q0:q0 + NQ_], start=True, stop=True), reads=[kth, qh], writes=[z])
                            zz = zs.next()
                            if bd is None:
                                mk.op("dve", lambda e, z=z, zz=zz, kb=kb: e.tensor_copy(out=zz.t[:kb, :], in_=z.t[:kb, 0:NQ_]), reads=[z], writes=[zz])
                            else:
                                mk.op("dve", lambda e, z=z, zz=zz, kb=kb, bd=bd: e.tensor_tensor(out=zz.t[:kb, :], in0=z.t[:kb, 0:NQ_], in1=MD[:kb, bd, 0:NQ_], op=ALU.add), reads=[z, cst], writes=[zz])
                            ex = ee.next()
                            mk.op("act", lambda e, zz=zz, ex=ex, kb=kb: e.activation(out=ex.t[:kb, :], in_=zz.t[:kb, :], func=AF.Exp), reads=[zz], writes=[ex])
                            sp = sp_.next()
                            mk.op("act", lambda e, sp=sp, ex=ex, kb=kb: e.activation(out=sp.t[:kb, :], in_=ex.t[:kb, :], func=AF.Ln, bias=1.0), reads=[ex], writes=[sp])
                            cc = pc.next()
                            mk.op("pe", lambda e, cc=cc, sp=sp, kb=kb, first=first: e.matmul(cc.t[:kb, 0:NQ_], lhsT=TRIB[:kb, :kb], rhs=sp.t[:kb, :], start=True, stop=first), reads=[cbf, sp], writes=[cc])
                            if not first:
                                mk.op("pe", lambda e, cc=cc, R=R, kb=kb: e.matmul(cc.t[:kb, 0:NQ_], lhsT=ONESB[:, :kb], rhs=R.t[:], start=False, stop=True), reads=[cbf, R], writes=[cc])
                            if not lastb:
                                mk.op("pool", lambda e, R=R, sp=sp, kb=kb: e.tensor_tensor(out=R.t[:kb, :], in0=R.t[:kb, :], in1=sp.t[:kb, :], op=ALU.add), reads=[R, sp], writes=[R])
                            lgt = lg.next()
                            mk.op("dve", lambda e, lgt=lgt, zz=zz, cc=cc, kb=kb: e.tensor_tensor(out=lgt.t[:kb, :], in0=zz.t[:kb, :], in1=cc.t[:kb, 0:NQ_], op=ALU.subtract), reads=[zz, cc], writes=[lgt])
                            a_ = aT.next()
                            mk.op("act", lambda e, a_=a_, lgt=lgt, kb=kb: e.activation(out=a_.t[:kb, :], in_=lgt.t[:kb, :], func=AF.Exp), reads=[lgt], writes=[a_])
                            mk.op("pe", lambda e, pov=pov, vb=vb, a_=a_, b=b, kb=kb, first=first, lastb=lastb: e.matmul(pov.t[:, 0:NQ_], lhsT=vb.t[:kb, b, :], rhs=a_.t[:kb, :], start=first, stop=lastb), reads=[vb, a_], writes=[pov])
                        o_ = om.next()
                        mk.op("dve", lambda e, o_=o_, pov=pov, bz=bz: e.tensor_tensor(out=o_.t[:], in0=pov.t[:, 0:NQ_], in1=bz.t[:, q0:q0 + NQ_], op=ALU.mult), reads=[pov, bz], writes=[o_])
                        mk.dma("pool", lambda e, o_=o_, h=h: e.dma_start(out=S["mixT"][12 + h, :, q0:q0 + NQ_], in_=o_.t[:]), o_, reads=[o_], writes=[DSs["mixT"]])

        def out_phase(sfx, T, l, xin, Dxin):
            S = SCR[sfx]
            DSs = DS[sfx]
            TT = min(128, T)
            TSO = min(256, T)
            NTT = TSO // TT
            yout = O["y" + sfx]
            with mk.phase():
                mx = Rot([mk.sb("mx%d" % i, [128, 32, TSO], BF16) for i in range(2)])
                Wo = Rot([mk.sb("Wo%d" % i, [128, 32, 512], BF16) for i in range(2)])
                ybuf = [mk.sb("ybuf%d" % i, [TT, D]) for i in range(NTT)]
                xt = Rot([mk.sb("xto%d" % i, [TT, D]) for i in range(2)])
                junk = mk.sb("junko", [TT, D], BF16)
                st1 = Rot([mk.sb("sto%d" % i, [TT, 2]) for i in range(2)])
                pm = Rot([mk.ps("pmo%d" % i, [128, 512]) for i in range(4)])
                gpostb = mk.sb("gpostb", [128, D])
                mk.dma("sp", lambda e: e.dma_start(out=gpostb.t[:], in_=I["norm_post"][l:l + 1, :].to_broadcast([128, D])), gpostb, writes=[gpostb])
                nev = 0
                for so in range(T // TSO):
                    t0 = so * TSO
                    m = mx.next()
                    mk.dma("sp", lambda e, m=m: e.dma_start(out=m.t[:], in_=S["mixT"][:, :, t0:t0 + TSO].rearrange("c p t -> p c t")), m, reads=[DSs["mixT"]], writes=[m])
                    for nb in range(8):
                        W = Wo.next()
                        mk.dma("sp", lambda e, W=W, nb=nb: e.dma_start(out=W.t[:], in_=wobf[nb]), W, reads=[Dwo], writes=[W])
                        for tt in range(NTT):
                            p = pm.next()
                            for mc in range(32):
                                mk.op("pe", lambda e, p=p, mc=mc, tt=tt, W=W, m=m: e.matmul(p.t[:TT, :], lhsT=m.t[:, mc, tt * TT:(tt + 1) * TT], rhs=W.t[:, mc, :], start=(mc == 0), stop=(mc == 31)), reads=[m, W], writes=[p])
                            nev += 1
                            yb_ = ybuf[tt]
                            if nev % 2:
                                mk.op("act", lambda e, p=p, yb_=yb_, nb=nb: e.activation(out=yb_.t[:, nb * 512:(nb + 1) * 512], in_=p.t[:TT, :], func=AF.Copy), reads=[p], writes=[yb_])
                            else:
                                mk.op("dve", lambda e, p=p, yb_=yb_, nb=nb: e.tensor_copy(out=yb_.t[:, nb * 512:(nb + 1) * 512], in_=p.t[:TT, :]), reads=[p], writes=[yb_])
                    for tt in range(NTT):
                        yb_ = ybuf[tt]
                        x = xt.next()
                        s1 = st1.next()
                        r0 = t0 + tt * TT
                        mk.dma("sp", lambda e, x=x, r0=r0: e.dma_start(out=x.t[:], in_=xin[r0:r0 + TT, :]), x, reads=[Dxin], writes=[x])
                        mk.op("act", lambda e, yb_=yb_, s1=s1: e.activation(out=junk.t[:], in_=yb_.t[:], func=AF.Square, accum_out=s1.t[:, 0:1]), reads=[yb_], writes=[junk, s1])
                        mk.op("act", lambda e, s1=s1: e.activation(out=s1.t[:, 1:2], in_=s1.t[:, 0:1], func=AF.Sqrt, scale=1.0 / D, bias=EPS), reads=[s1], writes=[s1])
                        mk.op("dve", lambda e, s1=s1: e.reciprocal(out=s1.t[:, 1:2], in_=s1.t[:, 1:2]), reads=[s1], writes=[s1])
                        mk.op("dve", lambda e, yb_=yb_, s1=s1: e.scalar_tensor_tensor(out=yb_.t[:], in0=yb_.t[:], scalar=s1.t[:, 1:2], in1=gpostb.t[:TT, :], op0=ALU.mult, op1=ALU.mult), reads=[yb_, s1, gpostb], writes=[yb_])
                        mk.op("pool", lambda e, yb_=yb_, x=x: e.tensor_tensor(out=x.t[:], in0=yb_.t[:], in1=x.t[:], op=ALU.add), reads=[yb_, x], writes=[x])
                        mk.dma("pool", lambda e, x=x, r0=r0: e.dma_start(out=yout[r0:r0 + TT, :], in_=x.t[:]), x, reads=[x], writes=[DY[sfx]])

        nph = [0]

        def go():
            nph[0] += 1
            return nph[0] <= DEBUG_STOP
        for l in range(DEPTH):
            load_params(l)
            if go():
                precast(l)
            for sfx, T in (("p", T_P), ("s", T_S)):
                xin = I["x" + sfx] if l == 0 else O["y" + sfx]
                Dxin = mk.dram("xin" + sfx) if l == 0 else DY[sfx]
                if go():
                    proj_phase(sfx, T, l, xin, Dxin)
                if go():
                    gdn_phase(sfx, T, l)
                if go():
                    sb_phase(sfx, T, l)
                if go():
                    out_phase(sfx, T, l, xin, Dxin)
        mk.barrier()
        mk.flush()
        nc._mk_ninst = mk.ninst
    return nc


def make_consts():
    p = np.arange(128)[:, None]
    f = np.arange(128)[None, :]
    ident = (p == f).astype(np.float32)
    uinc = (p <= f).astype(np.float32)
    sus = (p > f).astype(np.float32)
    masku = np.where(f >= p, 0.0, NEG).astype(np.float32)
    maskl = np.where(f < p, 0.0, NEG).astype(np.float32)
    t = np.arange(512)[None, None, :]
    a = np.arange(4)[None, :, None]
    md = np.where(t > 128 * a + p[:, :, None], 0.0, NEG).astype(np.float32).reshape(128, 2048)
    return np.ascontiguousarray(np.concatenate([ident, uinc, sus, masku, maskl, md], axis=1))


def make_level_masks():
    i = np.arange(128)[:, None]
    j = np.arange(128)[None, :]
    low, up = [], []
    for s_ in range(7):
        m = ((i >> (s_ + 1)) == (j >> (s_ + 1))) & (((i >> s_) & 1) == 1) & (((j >> s_) & 1) == 0)
        low.append(m.astype(np.float32))
        up.append(m.T.astype(np.float32))
    return np.ascontiguousarray(np.concatenate(low + up, axis=1))


_CACHE = {}


def run(inputs, T_P, T_S, PAST, DEPTH, n_cores=8):
    key = (T_P, T_S, PAST, DEPTH)
    if key not in _CACHE:
        _CACHE[key] = build_program(T_P, T_S, PAST, DEPTH)
    nc = _CACHE[key]
    f = lambda a: np.ascontiguousarray(np.asarray(a, dtype=np.float32))
    xp = f(inputs["x_prompt"]); xs = f(inputs["x_sample"])
    BP = xp.shape[0]
    ck = f(inputs["cache_sb_k"]); cv = f(inputs["cache_sb_v"])
    sg = f(inputs["state_gdn"]); sgc = f(inputs["state_gdn_conv"]); ssc = f(inputs["state_sc_conv"])
    shared = dict(w_in=f(inputs["w_in"]), w_out=f(inputs["w_out"]), norm_pre=f(inputs["norm_pre"]),
                  norm_post=f(inputs["norm_post"]), gcw=f(inputs["gdn_conv_w"]), alog=f(inputs["gdn_a_log"]),
                  dtb=f(inputs["gdn_dt_bias"]), gnorm=f(inputs["gdn_norm"]), scw=f(inputs["sc_conv_w"]), cst=make_consts(),
                  mks=make_level_masks())
    in_maps = []
    for c in range(n_cores):
        m = dict(shared)
        m["xp"] = xp[c % BP]
        m["xs"] = xs[c]
        m["ck"] = np.ascontiguousarray(ck[:, c].reshape(DEPTH, PAST, GW))
        m["cv"] = np.ascontiguousarray(cv[:, c].reshape(DEPTH, PAST, GW))
        m["sg"] = np.ascontiguousarray(sg[:, c])
        m["sgc"] = np.ascontiguousarray(sgc[:, c])
        m["ssc"] = np.ascontiguousarray(ssc[:, c])
        in_maps.append(m)
    res = run_bass_kernel_spmd(nc, in_maps, core_ids=list(range(n_cores)))
    R = res.results
    st = lambda name, cores, ax: np.stack([R[c][name] for c in cores], axis=ax)
    pc = list(range(min(BP, n_cores)))
    sc = list(range(n_cores))
    yp = st("yp", pc, 0)
    ys = st("ys", sc, 0)
    outs = [yp, ys]
    for sfx, cores in (("p", pc), ("s", sc)):
        T = T_P if sfx == "p" else T_S
        outs.append(st("k" + sfx, cores, 1).reshape(DEPTH, len(cores), T, H, HD))
        outs.append(st("v" + sfx, cores, 1).reshape(DEPTH, len(cores), T, H, HD))
        outs.append(st("g" + sfx, cores, 1))
        outs.append(st("gc" + sfx, cores, 1))
        outs.append(st("sc" + sfx, cores, 1))
    return tuple(np.ascontiguousarray(o.astype(np.float32)) for o in outs)


def kernel(**inputs):
    return run(inputs, 4096, 32, 2048, 4)
```

```python
import contextlib
import numpy as np
import concourse.bass as bass
import concourse.mybir as mybir
from concourse.bass_utils import run_bass_kernel_spmd

F32 = mybir.dt.float32
BF16 = mybir.dt.bfloat16
AF = mybir.ActivationFunctionType
ALU = mybir.AluOpType
AX = mybir.AxisListType

D = 4096
DIN = 16408
H = 12
HD = 128
GW = 1536
EPS = 1e-6
NEG = -30000.0
DEBUG_STOP = 10 ** 9
DEBUG_OPS = 10 ** 12


class Buf:
    def __init__(self, name, t=None):
        self.name = name
        self.t = t
        self.w = {}
        self.r = {}
        self.dsem = None
        self.dcnt = 0
        self.excl = False


class _Rec:
    def __init__(self):
        self.call = None

    def __getattr__(self, name):
        def f(*a, **k):
            self.call = (name, a, k)
            return self
        return f

    def then_inc(self, *a):
        return self


def _record(fn):
    r = _Rec()
    fn(r)
    assert r.call is not None
    return r.call


class MK:
    ENG = ("pe", "act", "dve", "pool", "sp")

    def __init__(self, nc, es, block):
        self.nc = nc
        self.es = es
        self.block = block
        self.sem = {}
        self.cnt = {}
        self.q = {e: [] for e in self.ENG}
        self.waited = {e: {} for e in self.ENG}
        for e in ("pe", "act", "dve", "pool"):
            self.sem[e] = es.enter_context(nc.semaphore("s_" + e))
            self.cnt[e] = 0
        self.free_sems = []
        self.live = []
        self.nsem = 0
        self.ninst = 0
        self.scope = None

    def sb(self, name, shape, dt=F32, es=None):
        es = es or self.scope or self.es
        self.nalloc = getattr(self, "nalloc", 0) + 1
        name = "%s_u%d" % (name, self.nalloc)
        t = es.enter_context(self.nc.sbuf_tensor(name, list(shape), dt))
        return Buf(name, t)

    def ps(self, name, shape, dt=F32, es=None):
        es = es or self.scope or self.es
        self.nalloc = getattr(self, "nalloc", 0) + 1
        name = "%s_u%d" % (name, self.nalloc)
        t = es.enter_context(self.nc.psum_tensor(name, list(shape), dt))
        b = Buf(name, t)
        b.excl = True
        return b

    def dram(self, name):
        return Buf(name, None)

    def _getsem(self, b):
        if b.dsem is None:
            if self.free_sems:
                b.dsem, b.dcnt = self.free_sems.pop()
            else:
                self.nsem += 1
                b.dsem = self.es.enter_context(self.nc.semaphore("d%d" % self.nsem))
                b.dcnt = 0
            self.live.append(b)

    def release(self, bufs):
        for b in bufs:
            if b.dsem is not None:
                self.free_sems.append((b.dsem, b.dcnt))
                self.live.remove(b)
                b.dsem = None

    def _deps(self, eng, reads, writes, dma_buf=None):
        deps = {}

        def add(d):
            for k, (s, v) in d.items():
                if k not in deps or deps[k][1] < v:
                    deps[k] = (s, v)
        own = self.sem.get(eng)
        for b in reads:
            add(b.w)
            if b.excl:
                add({k: v for k, v in b.r.items() if v[0] is not own})
        for b in writes:
            if dma_buf is not None and b is dma_buf and not b.r and b.w and all(k == id(b.dsem) for k in b.w):
                continue
            add(b.w)
            add(b.r)
        out = []
        wd = self.waited[eng]
        pes = self.sem["pe"]
        for k, (s, v) in deps.items():
            if eng == "pe" and s is pes:
                continue
            if wd.get(k, 0) >= v:
                continue
            wd[k] = v
            out.append((s, v))
        return out

    def _post(self, reads, writes, ev):
        k = id(ev[0])
        for b in reads:
            b.r[k] = ev
        for b in writes:
            b.w = {k: ev}
            b.r = {}

    def op(self, eng, fn, reads=(), writes=()):
        self.nops = getattr(self, "nops", 0) + 1
        if self.nops > getattr(self, "limit", 10 ** 12):
            return
        waits = self._deps(eng, reads, writes)
        self.cnt[eng] += 1
        s = self.sem[eng]
        v = self.cnt[eng]
        call = _record(fn)

        def emit(e, call=call, waits=waits, s=s):
            for (ws, wv) in waits:
                e.wait_ge(ws, wv)
            getattr(e, call[0])(*call[1], **call[2]).then_inc(s, 1)
        self.q[eng].append(emit)
        self.ninst += 1 + len(waits)
        self._post(reads, writes, (s, v))

    def dma(self, eng, fn, sbuf, reads=(), writes=()):
        self.nops = getattr(self, "nops", 0) + 1
        if self.nops > getattr(self, "limit", 10 ** 12):
            return
        self._getsem(sbuf)
        waits = self._deps(eng, reads, writes, dma_buf=sbuf)
        sbuf.dcnt += 16
        s = sbuf.dsem
        v = sbuf.dcnt
        call = _record(fn)

        def emit(e, call=call, waits=waits, s=s):
            for (ws, wv) in waits:
                e.wait_ge(ws, wv)
            getattr(e, call[0])(*call[1], **call[2]).then_inc(s, 16)
        self.q[eng].append(emit)
        self.ninst += 1 + len(waits)
        self._post(reads, writes, (s, v))

    def barrier(self):
        evs = [(self.sem[e], self.cnt[e]) for e in ("pe", "act", "dve", "pool") if self.cnt[e] > 0]
        evs += [(b.dsem, b.dcnt) for b in self.live if b.dcnt > 0]
        for eng in self.ENG:
            wd = self.waited[eng]
            waits = []
            for (s, v) in evs:
                if wd.get(id(s), 0) >= v:
                    continue
                wd[id(s)] = v
                waits.append((s, v))
            if waits:
                def emit(e, waits=waits):
                    for (ws, wv) in waits:
                        e.wait_ge(ws, wv)
                self.q[eng].append(emit)
                self.ninst += len(waits)

    def flush(self):
        b = self.block
        m = {"pe": b.tensor, "act": b.scalar, "dve": b.vector, "pool": b.gpsimd, "sp": b.sync}
        for eng in self.ENG:
            lst = self.q[eng]
            if not lst:
                continue

            def body(e, lst=lst):
                for f in lst:
                    f(e)
            m[eng](body)
            self.q[eng] = []

    @contextlib.contextmanager
    def phase(self):
        with contextlib.ExitStack() as pes:
            old = self.scope
            self.scope = pes
            nlive = list(self.live)
            yield pes
            self.barrier()
            self.flush()
            self.release([b for b in self.live if b not in nlive])
            self.scope = old


class Rot:
    def __init__(self, bufs):
        self.bufs = bufs
        self.i = 0

    def next(self):
        b = self.bufs[self.i % len(self.bufs)]
        self.i += 1
        return b


def sblk_col(s):
    return 512 * s if s < 12 else 6168 + 512 * (s - 12)


def build_program(T_P, T_S, PAST, DEPTH):
    nc = bass.Bass("TRN2", target_bir_lowering=False)

    def din(name, shape, dt=F32):
        return nc.dram_tensor(name, list(shape), dt, kind="ExternalInput").ap()

    def dout(name, shape, dt=F32):
        return nc.dram_tensor(name, list(shape), dt, kind="ExternalOutput").ap()

    def dscr(name, shape, dt=BF16):
        return nc.dram_tensor(name, list(shape), dt, kind="Internal").ap()

    NPB = PAST // 128
    I = dict(
        xp=din("xp", [T_P, D]), xs=din("xs", [T_S, D]),
        ck=din("ck", [DEPTH, PAST, GW]), cv=din("cv", [DEPTH, PAST, GW]),
        sg=din("sg", [DEPTH, H, HD, HD]), sgc=din("sgc", [DEPTH, 3, 3 * GW]), ssc=din("ssc", [DEPTH, 2, 1024]),
        w_in=din("w_in", [DEPTH, D, DIN]), w_out=din("w_out", [DEPTH, D, D]),
        norm_pre=din("norm_pre", [DEPTH, D]), norm_post=din("norm_post", [DEPTH, D]),
        gcw=din("gcw", [DEPTH, 4, 3 * GW]), alog=din("alog", [DEPTH, H]), dtb=din("dtb", [DEPTH, H]),
        gnorm=din("gnorm", [DEPTH, HD]), scw=din("scw", [DEPTH, 3, 1024]),
        cst=din("cst", [128, 5 * 128 + 4 * 512]),
        mks=din("mks", [128, 14 * 128]),
    )
    O = {}
    for sfx, T in (("p", T_P), ("s", T_S)):
        O["y" + sfx] = dout("y" + sfx, [T, D])
        O["k" + sfx] = dout("k" + sfx, [DEPTH, T, GW])
        O["v" + sfx] = dout("v" + sfx, [DEPTH, T, GW])
        O["g" + sfx] = dout("g" + sfx, [DEPTH, H, HD, HD])
        O["gc" + sfx] = dout("gc" + sfx, [DEPTH, 3, 3 * GW])
        O["sc" + sfx] = dout("sc" + sfx, [DEPTH, 2, 1024])
    wbf = dscr("wbf", [32, 128, 32, 512])
    wab = dscr("wab", [128, 32, 24])
    wobf = dscr("wobf", [8, 128, 32, 512])
    SCR = {}
    for sfx, T in (("p", T_P), ("s", T_S)):
        SCR[sfx] = dict(
            gqT=dscr("gqT" + sfx, [H, 128, T]), gkT=dscr("gkT" + sfx, [H, 128, T]),
            gk=dscr("gk" + sfx, [T, GW]), gv=dscr("gv" + sfx, [T, GW]),
            azT=dscr("azT" + sfx, [H, 128, T]), gbt=dscr("gbt" + sfx, [T, 24], F32),
            QT=dscr("QT" + sfx, [H, 128, T]), bzT=dscr("bzT" + sfx, [H, 128, T]),
            mixT=dscr("mixT" + sfx, [32, 128, T]),
        )

    with contextlib.ExitStack() as es:
        es.enter_context(nc.allow_non_contiguous_dma(reason="small strided parameter / state transfers"))
        es.enter_context(nc.allow_low_precision(reason="bf16 matmul operands, fp32 accumulation"))
        block = es.enter_context(nc.Block())
        mk = MK(nc, es, block)
        Dw = mk.dram("wbf")
        Dwo = mk.dram("wobf")
        DS = {sfx: {k: mk.dram(k + sfx) for k in SCR[sfx]} for sfx in ("p", "s")}
        DY = {sfx: mk.dram("y" + sfx) for sfx in ("p", "s")}
        DKV = {sfx: mk.dram("kv" + sfx) for sfx in ("p", "s")}

        cst = mk.sb("cst", [128, 5 * 128 + 4 * 512])
        mk.dma("sp", lambda e: e.dma_start(out=cst.t[:], in_=I["cst"]), cst, writes=[cst])
        ident = cst.t[:, 0:128]
        UINC = cst.t[:, 128:256]
        SUS = cst.t[:, 256:384]
        MASKU = cst.t[:, 384:512]
        MASKL = cst.t[:, 512:640]
        MD = cst.t[:, 640:640 + 2048].rearrange("p (a b) -> p a b", a=4)
        cbf = mk.sb("cbf", [128, 4 * 128], BF16)
        onesf = mk.sb("onesf", [128, 128])
        mk.op("dve", lambda e: e.tensor_copy(out=cbf.t[:, 0:128], in_=ident), reads=[cst], writes=[cbf])
        mk.op("dve", lambda e: e.memset(cbf.t[:, 128:256], 1.0), writes=[cbf])
        mk.op("dve", lambda e: e.tensor_copy(out=cbf.t[:, 256:384], in_=SUS), reads=[cst], writes=[cbf])
        mk.op("dve", lambda e: e.tensor_tensor(out=cbf.t[:, 384:512], in0=SUS, in1=ident, op=ALU.add), reads=[cst], writes=[cbf])
        mk.op("dve", lambda e: e.memset(onesf.t[:], 1.0), writes=[onesf])
        mkb = mk.sb("mkb", [128, 14, 128], BF16)
        with mk.phase():
            mkf = mk.sb("mkf", [128, 14 * 128])
            mk.dma("sp", lambda e: e.dma_start(out=mkf.t[:], in_=I["mks"]), mkf, writes=[mkf])
            mk.op("dve", lambda e: e.tensor_copy(out=mkb.t[:].rearrange("p a b -> p (a b)"), in_=mkf.t[:]), reads=[mkf], writes=[mkb])
        IDB = cbf.t[:, 0:128]
        ONESB = cbf.t[:, 128:256]
        SUSB = cbf.t[:, 256:384]
        TRIB = cbf.t[:, 384:512]
        gpreT = mk.sb("gpreT", [128, 32])
        cwT = mk.sb("cwT", [128, 36, 4])
        scwT = mk.sb("scwT", [128, 8, 3])
        negA = mk.sb("negA", [128, H])
        dtbb = mk.sb("dtbb", [128, H])
        gnT = mk.sb("gnT", [128, 1])

        def load_params(l):
            mk.dma("sp", lambda e: e.dma_start(out=gpreT.t[:], in_=I["norm_pre"][l].rearrange("(c p) -> p c", p=128)), gpreT, writes=[gpreT])
            for i in range(4):
                mk.dma("sp", lambda e, i=i: e.dma_start(out=cwT.t[:, :, i], in_=I["gcw"][l, i].rearrange("(c p) -> p c", p=128)), cwT, writes=[cwT])
            for i in range(3):
                mk.dma("sp", lambda e, i=i: e.dma_start(out=scwT.t[:, :, i], in_=I["scw"][l, i].rearrange("(c p) -> p c", p=128)), scwT, writes=[scwT])
            mk.dma("sp", lambda e: e.dma_start(out=negA.t[:], in_=I["alog"][l:l + 1, :].to_broadcast([128, H])), negA, writes=[negA])
            mk.dma("sp", lambda e: e.dma_start(out=dtbb.t[:], in_=I["dtb"][l:l + 1, :].to_broadcast([128, H])), dtbb, writes=[dtbb])
            mk.dma("sp", lambda e: e.dma_start(out=gnT.t[:], in_=I["gnorm"][l].rearrange("(p o) -> p o", o=1)), gnT, writes=[gnT])
            mk.op("act", lambda e: e.activation(out=negA.t[:], in_=negA.t[:], func=AF.Exp), reads=[negA], writes=[negA])
            mk.op("dve", lambda e: e.tensor_scalar(out=negA.t[:], in0=negA.t[:], scalar1=-1.0, scalar2=None, op0=ALU.mult), reads=[negA], writes=[negA])

        def precast(l):
            with mk.phase():
                wf = Rot([mk.sb("wf%d" % i, [128, 8, 512]) for i in range(3)])
                wb = Rot([mk.sb("wb%d" % i, [128, 8, 512], BF16) for i in range(3)])
                n = 0
                for s in range(32):
                    c0 = sblk_col(s)
                    for kg in range(4):
                        f = wf.next()
                        b = wb.next()
                        src = I["w_in"][l, kg * 1024:(kg + 1) * 1024, c0:c0 + 512].rearrange("(kc p) n -> p kc n", p=128)
                        mk.dma("sp", lambda e, f=f, src=src: e.dma_start(out=f.t[:], in_=src), f, writes=[f])
                        eng = "dve" if n % 2 == 0 else "pool"
                        n += 1
                        gsl = gpreT.t[:, kg * 8:(kg + 1) * 8].unsqueeze(2).to_broadcast([128, 8, 512])
                        mk.op(eng, lambda e, f=f, b=b, gsl=gsl: e.tensor_tensor(out=b.t[:], in0=f.t[:], in1=gsl, op=ALU.mult), reads=[f, gpreT], writes=[b])
                        dst = wbf[s, :, kg * 8:(kg + 1) * 8, :]
                        mk.dma("act", lambda e, b=b, dst=dst: e.dma_start(out=dst, in_=b.t[:]), b, reads=[b], writes=[Dw])
                fab = mk.sb("fab", [128, 32, 24])
                bab = mk.sb("bab", [128, 32, 24], BF16)
                mk.dma("sp", lambda e: e.dma_start(out=fab.t[:], in_=I["w_in"][l, :, 6144:6168].rearrange("(kc p) n -> p kc n", p=128)), fab, writes=[fab])
                mk.op("dve", lambda e: e.tensor_tensor(out=bab.t[:], in0=fab.t[:], in1=gpreT.t[:].unsqueeze(2).to_broadcast([128, 32, 24]), op=ALU.mult), reads=[fab, gpreT], writes=[bab])
                mk.dma("sp", lambda e: e.dma_start(out=wab, in_=bab.t[:]), bab, reads=[bab], writes=[Dw])
                for nb in range(8):
                    for kg in range(4):
                        f = wf.next()
                        b = wb.next()
                        src = I["w_out"][l, kg * 1024:(kg + 1) * 1024, nb * 512:(nb + 1) * 512].rearrange("(kc p) n -> p kc n", p=128)
                        mk.dma("sp", lambda e, f=f, src=src: e.dma_start(out=f.t[:], in_=src), f, writes=[f])
                        eng = "dve" if n % 2 == 0 else "pool"
                        n += 1
                        mk.op(eng, lambda e, f=f, b=b: e.tensor_copy(out=b.t[:], in_=f.t[:]), reads=[f], writes=[b])
                        dst = wobf[nb, :, kg * 8:(kg + 1) * 8, :]
                        mk.dma("act", lambda e, b=b, dst=dst: e.dma_start(out=dst, in_=b.t[:]), b, reads=[b], writes=[Dwo])

        SECT = ([("qkv", s) for s in range(9)] + [("az", s) for s in range(9, 12)] + [("ab", None)] +
                [("bq", s) for s in range(12, 15)] + [("bk", s) for s in range(15, 18)] + [("bv", s) for s in range(18, 21)] +
                [("bz", s) for s in range(21, 24)] +
                [("cc", 26), ("ch", 28), ("cb", 24), ("cz", 30), ("cc", 27), ("ch", 29), ("cb", 25), ("cz", 31)])
        SECBASE = dict(qkv=0, az=9, bq=12, bk=15, bv=18, bz=21, cb=24, cc=26, ch=28, cz=30)

        def proj_phase(sfx, T, l, xin, Dxin):
            S = SCR[sfx]
            DSs = DS[sfx]
            TT = min(128, T)
            TS = min(512, T)
            NTT = TS // TT
            NST = T // TS
            with mk.phase():
                hT = mk.sb("hT", [128, 32, TS], BF16)
                Wt = Rot([mk.sb("Wt%d" % i, [128, 32, 512], BF16) for i in range(2)])
                Wab = mk.sb("Wab", [128, 32, 24], BF16)
                xt = Rot([mk.sb("xt%d" % i, [TT, D]) for i in range(1)])
                junk = mk.sb("junk", [TT, D], BF16)
                st1 = Rot([mk.sb("st1_%d" % i, [TT, 2]) for i in range(2)])
                pm = Rot([mk.ps("pm%d" % i, [128, 512]) for i in range(4)])
                ptr = Rot([mk.ps("ptr%d" % i, [128, 512]) for i in range(2)])
                pn = Rot([mk.ps("pn%d" % i, [128, 512]) for i in range(2)])
                xa = Rot([mk.sb("xa%d" % i, [128, TS + 3]) for i in range(2)])
                acc = Rot([mk.sb("acc%d" % i, [128, TS]) for i in range(2)])
                sil = Rot([mk.sb("sil%d" % i, [128, TS]) for i in range(2)])
                sqb = Rot([mk.sb("sqb%d" % i, [128, TS], BF16) for i in range(2)])
                rr = Rot([mk.sb("rr%d" % i, [128, TS]) for i in range(2)])
                snf = Rot([mk.sb("snf%d" % i, [128, TS]) for i in range(2)])
                obf = Rot([mk.sb("obf%d" % i, [128, TS], BF16) for i in range(3)])
                tokb = Rot([mk.sb("tokb%d" % i, [TT, NTT, 128], BF16) for i in range(2)])
                kvst = Rot([mk.sb("kvst%d" % i, [TT, 512]) for i in range(2)])
                abt = Rot([mk.sb("abt%d" % i, [TT, 24]) for i in range(2)])
                abo = Rot([mk.sb("abo%d" % i, [TT, 24]) for i in range(2)])
                carry = mk.sb("carry", [128, 36, 3])
                sccarry = mk.sb("sccarry", [128, 8, 2])
                scU = mk.sb("scU", [128, 4, TS + 2])
                scV = mk.sb("scV", [128, 4, TS])
                if sfx == "p":
                    mk.op("dve", lambda e: e.memset(carry.t[:], 0.0), writes=[carry])
                    mk.op("dve", lambda e: e.memset(sccarry.t[:], 0.0), writes=[sccarry])
                else:
                    for t in range(3):
                        mk.dma("sp", lambda e, t=t: e.dma_start(out=carry.t[:, :, t], in_=I["sgc"][l, t].rearrange("(c p) -> p c", p=128)), carry, writes=[carry])
                    for t in range(2):
                        mk.dma("sp", lambda e, t=t: e.dma_start(out=sccarry.t[:, :, t], in_=I["ssc"][l, t].rearrange("(c p) -> p c", p=128)), sccarry, writes=[sccarry])
                mk.dma("sp", lambda e: e.dma_start(out=Wab.t[:], in_=wab), Wab, reads=[Dw], writes=[Wab])
                nev = [0]

                def evac_eng():
                    nev[0] += 1
                    return "act" if nev[0] % 2 else "dve"

                def copy_op(eng, out, in_, reads, writes, scale=None):
                    if eng == "act":
                        if scale is None:
                            mk.op("act", lambda e: e.activation(out=out, in_=in_, func=AF.Copy), reads=reads, writes=writes)
                        else:
                            mk.op("act", lambda e: e.activation(out=out, in_=in_, func=AF.Copy, scale=scale), reads=reads, writes=writes)
                    else:
                        if scale is None:
                            mk.op("dve", lambda e: e.tensor_copy(out=out, in_=in_), reads=reads, writes=writes)
                        else:
                            mk.op("dve", lambda e: e.tensor_scalar(out=out, in0=in_, scalar1=scale, scalar2=None, op0=ALU.mult), reads=reads, writes=writes)

                for st in range(NST):
                    t0 = st * TS
                    for tt in range(NTT):
                        x = xt.next()
                        s1 = st1.next()
                        r0 = t0 + tt * TT
                        mk.dma("sp", lambda e, x=x, r0=r0: e.dma_start(out=x.t[:], in_=xin[r0:r0 + TT, :]), x, reads=[Dxin], writes=[x])
                        mk.op("act", lambda e, x=x, s1=s1: e.activation(out=junk.t[:], in_=x.t[:], func=AF.Square, accum_out=s1.t[:, 0:1]), reads=[x], writes=[junk, s1])
                        mk.op("act", lambda e, s1=s1: e.activation(out=s1.t[:, 1:2], in_=s1.t[:, 0:1], func=AF.Sqrt, scale=1.0 / D, bias=EPS), reads=[s1], writes=[s1])
                        mk.op("dve", lambda e, s1=s1: e.reciprocal(out=s1.t[:, 1:2], in_=s1.t[:, 1:2]), reads=[s1], writes=[s1])
                        xn = x
                        mk.op("dve", lambda e, x=x, s1=s1: e.tensor_scalar(out=x.t[:], in0=x.t[:], scalar1=s1.t[:, 1:2], scalar2=None, op0=ALU.mult), reads=[x, s1], writes=[x])
                        for k4 in range(8):
                            p = ptr.next()
                            for j in range(4):
                                kc = k4 * 4 + j
                                mk.op("pe", lambda e, p=p, j=j, kc=kc: e.transpose(out=p.t[:, j * TT:(j + 1) * TT], in_=xn.t[:TT, kc * 128:(kc + 1) * 128], identity=ident[:TT, :TT]), reads=[xn, cst], writes=[p])
                            copy_op(evac_eng(), hT.t[:, k4 * 4:(k4 + 1) * 4, tt * TT:(tt + 1) * TT], p.t[:, 0:4 * TT].rearrange("p (a b) -> p a b", a=4), [p], [hT])
                    for (sec, s) in SECT:
                        if sec == "ab":
                            for tt in range(NTT):
                                p = pm.next()
                                for kc in range(32):
                                    mk.op("pe", lambda e, p=p, kc=kc, tt=tt: e.matmul(p.t[:TT, 0:24], lhsT=hT.t[:, kc, tt * TT:(tt + 1) * TT], rhs=Wab.t[:, kc, :], start=(kc == 0), stop=(kc == 31)), reads=[hT, Wab], writes=[p])
                                a1 = abt.next()
                                ao = abo.next()
                                mk.op("dve", lambda e, p=p, a1=a1: e.tensor_tensor(out=a1.t[:, 0:12], in0=p.t[:TT, 0:12], in1=dtbb.t[:TT, :], op=ALU.add), reads=[p, dtbb], writes=[a1])
                                mk.op("act", lambda e, a1=a1: e.activation(out=a1.t[:, 0:12], in_=a1.t[:, 0:12], func=AF.Exp), reads=[a1], writes=[a1])
                                mk.op("act", lambda e, a1=a1: e.activation(out=a1.t[:, 0:12], in_=a1.t[:, 0:12], func=AF.Ln, bias=1.0), reads=[a1], writes=[a1])
                                mk.op("dve", lambda e, a1=a1, ao=ao: e.tensor_tensor(out=ao.t[:, 0:12], in0=a1.t[:, 0:12], in1=negA.t[:TT, :], op=ALU.mult), reads=[a1, negA], writes=[ao])
                                mk.op("act", lambda e, p=p, ao=ao: e.activation(out=ao.t[:, 12:24], in_=p.t[:TT, 12:24], func=AF.Sigmoid), reads=[p, ao], writes=[ao])
                                r0 = t0 + tt * TT
                                mk.dma("pool", lambda e, ao=ao, r0=r0: e.dma_start(out=S["gbt"][r0:r0 + TT, :], in_=ao.t[:]), ao, reads=[ao], writes=[DSs["gbt"]])
                            continue
                        W = Wt.next()
                        mk.dma("sp", lambda e, W=W, s=s: e.dma_start(out=W.t[:], in_=wbf[s]), W, reads=[Dw], writes=[W])
                        if sec in ("bk", "bv"):
                            okv = O[("k" if sec == "bk" else "v") + sfx]
                            c0 = (s - SECBASE[sec]) * 512
                            for tt in range(NTT):
                                p = pm.next()
                                for kc in range(32):
                                    mk.op("pe", lambda e, p=p, kc=kc, tt=tt, W=W: e.matmul(p.t[:TT, :], lhsT=hT.t[:, kc, tt * TT:(tt + 1) * TT], rhs=W.t[:, kc, :], start=(kc == 0), stop=(kc == 31)), reads=[hT, W], writes=[p])
                                kv = kvst.next()
                                copy_op(evac_eng(), kv.t[:], p.t[:TT, :], [p], [kv])
                                r0 = t0 + tt * TT
                                mk.dma("pool", lambda e, kv=kv, r0=r0, c0=c0, okv=okv: e.dma_start(out=okv[l, r0:r0 + TT, c0:c0 + 512], in_=kv.t[:]), kv, reads=[kv], writes=[DKV[sfx]])
                            continue
                        for c in range(4):
                            ci = (s - SECBASE[sec]) * 4 + c
                            p = pm.next()
                            for kc in range(32):
                                mk.op("pe", lambda e, p=p, kc=kc, c=c, W=W: e.matmul(p.t[:, 0:TS], lhsT=W.t[:, kc, c * 128:(c + 1) * 128], rhs=hT.t[:, kc, :], start=(kc == 0), stop=(kc == 31)), reads=[hT, W], writes=[p])
                            P = p.t[:, 0:TS]
                            if sec == "qkv":
                                a = xa.next()
                                mk.op("act", lambda e, a=a, P=P: e.activation(out=a.t[:, 3:3 + TS], in_=P, func=AF.Copy), reads=[p], writes=[a])
                                mk.op("dve", lambda e, a=a, ci=ci: e.tensor_copy(out=a.t[:, 0:3], in_=carry.t[:, ci, :]), reads=[carry, a], writes=[a])
                                ac = acc.next()
                                mk.op("dve", lambda e, a=a, ac=ac, ci=ci: e.tensor_scalar(out=ac.t[:], in0=a.t[:, 0:TS], scalar1=cwT.t[:, ci, 0:1], scalar2=None, op0=ALU.mult), reads=[a, cwT], writes=[ac])
                                for i in range(1, 4):
                                    mk.op("dve", lambda e, a=a, ac=ac, ci=ci, i=i: e.scalar_tensor_tensor(out=ac.t[:], in0=a.t[:, i:i + TS], scalar=cwT.t[:, ci, i:i + 1], in1=ac.t[:], op0=ALU.mult, op1=ALU.add), reads=[a, cwT, ac], writes=[ac])
                                mk.op("dve", lambda e, a=a, ci=ci: e.tensor_copy(out=carry.t[:, ci, :], in_=a.t[:, TS:TS + 3]), reads=[a, carry], writes=[carry])
                                sl = sil.next()
                                mk.op("act", lambda e, ac=ac, sl=sl: e.activation(out=sl.t[:], in_=ac.t[:], func=AF.Silu), reads=[ac], writes=[sl])
                                kind = ci // 12
                                hh = ci % 12
                                if kind < 2:
                                    sq = sqb.next()
                                    mk.op("dve", lambda e, sl=sl, sq=sq: e.tensor_tensor(out=sq.t[:], in0=sl.t[:], in1=sl.t[:], op=ALU.mult), reads=[sl], writes=[sq])
                                    pp = pn.next()
                                    mk.op("pe", lambda e, pp=pp, sq=sq: e.matmul(pp.t[:, 0:TS], lhsT=ONESB, rhs=sq.t[:], start=True, stop=True), reads=[cbf, sq], writes=[pp])
                                    r = rr.next()
                                    scl = 128.0 if kind == 0 else 1.0
                                    mk.op("act", lambda e, pp=pp, r=r, scl=scl: e.activation(out=r.t[:], in_=pp.t[:, 0:TS], func=AF.Sqrt, scale=scl, bias=EPS * scl), reads=[pp], writes=[r])
                                    mk.op("dve", lambda e, r=r: e.reciprocal(out=r.t[:], in_=r.t[:]), reads=[r], writes=[r])
                                    ob = obf.next()
                                    mk.op("dve", lambda e, sl=sl, r=r, ob=ob: e.tensor_tensor(out=ob.t[:], in0=sl.t[:], in1=r.t[:], op=ALU.mult), reads=[sl, r], writes=[ob])
                                    dstT = (S["gqT"] if kind == 0 else S["gkT"])[hh, :, t0:t0 + TS]
                                    mk.dma("pool", lambda e, ob=ob, dstT=dstT: e.dma_start(out=dstT, in_=ob.t[:]), ob, reads=[ob], writes=[DSs["gqT" if kind == 0 else "gkT"]])
                                    if kind == 1:
                                        sn = snf.next()
                                        mk.op("dve", lambda e, sl=sl, r=r, sn=sn: e.tensor_tensor(out=sn.t[:], in0=sl.t[:], in1=r.t[:], op=ALU.mult), reads=[sl, r], writes=[sn])
                                        src_f = sn
                                else:
                                    src_f = sl
                                if kind >= 1:
                                    pt_ = ptr.next()
                                    for j in range(NTT):
                                        mk.op("pe", lambda e, pt_=pt_, j=j, src_f=src_f: e.transpose(out=pt_.t[:TT, j * 128:(j + 1) * 128], in_=src_f.t[:, j * TT:(j + 1) * TT], identity=ident), reads=[src_f, cst], writes=[pt_])
                                    tb = tokb.next()
                                    copy_op(evac_eng(), tb.t[:], pt_.t[:TT, 0:NTT * 128].rearrange("p (a b) -> p a b", a=NTT), [pt_], [tb])
                                    dtok = (S["gk"] if kind == 1 else S["gv"])[t0:t0 + TS, hh * 128:(hh + 1) * 128].rearrange("(j p) d -> p j d", p=TT)
                                    mk.dma("pool", lambda e, tb=tb, dtok=dtok: e.dma_start(out=dtok, in_=tb.t[:]), tb, reads=[tb], writes=[DSs["gk" if kind == 1 else "gv"]])
                            elif sec == "az":
                                sl = sil.next()
                                mk.op("act", lambda e, sl=sl, P=P: e.activation(out=sl.t[:], in_=P, func=AF.Silu), reads=[p], writes=[sl])
                                ob = obf.next()
                                mk.op("dve", lambda e, sl=sl, ob=ob: e.tensor_scalar(out=ob.t[:], in0=sl.t[:], scalar1=gnT.t[:, 0:1], scalar2=None, op0=ALU.mult), reads=[sl, gnT], writes=[ob])
                                mk.dma("pool", lambda e, ob=ob, ci=ci: e.dma_start(out=S["azT"][ci, :, t0:t0 + TS], in_=ob.t[:]), ob, reads=[ob], writes=[DSs["azT"]])
                            elif sec == "bq":
                                ob = obf.next()
                                copy_op(evac_eng(), ob.t[:], P, [p], [ob], scale=HD ** -0.5)
                                mk.dma("pool", lambda e, ob=ob, ci=ci: e.dma_start(out=S["QT"][ci, :, t0:t0 + TS], in_=ob.t[:]), ob, reads=[ob], writes=[DSs["QT"]])
                            elif sec == "bz":
                                ob = obf.next()
                                mk.op("act", lambda e, ob=ob, P=P: e.activation(out=ob.t[:], in_=P, func=AF.Silu), reads=[p], writes=[ob])
                                mk.dma("pool", lambda e, ob=ob, ci=ci: e.dma_start(out=S["bzT"][ci, :, t0:t0 + TS], in_=ob.t[:]), ob, reads=[ob], writes=[DSs["bzT"]])
                            elif sec == "cc":
                                mk.op("act", lambda e, c=c, P=P: e.activation(out=scU.t[:, c, 2:2 + TS], in_=P, func=AF.Copy), reads=[p], writes=[scU])
                            elif sec == "ch":
                                mk.op("dve", lambda e, c=c, P=P: e.tensor_tensor(out=scU.t[:, c, 2:2 + TS], in0=P, in1=scU.t[:, c, 2:2 + TS], op=ALU.mult), reads=[p, scU], writes=[scU])
                                mk.op("dve", lambda e, c=c, ci=ci: e.tensor_copy(out=scU.t[:, c, 0:2], in_=sccarry.t[:, ci, :]), reads=[sccarry, scU], writes=[scU])
                                mk.op("dve", lambda e, c=c, ci=ci: e.tensor_scalar(out=scV.t[:, c, :], in0=scU.t[:, c, 0:TS], scalar1=scwT.t[:, ci, 0:1], scalar2=None, op0=ALU.mult), reads=[scU, scwT], writes=[scV])
                                for i in range(1, 3):
                                    mk.op("dve", lambda e, c=c, ci=ci, i=i: e.scalar_tensor_tensor(out=scV.t[:, c, :], in0=scU.t[:, c, i:i + TS], scalar=scwT.t[:, ci, i:i + 1], in1=scV.t[:, c, :], op0=ALU.mult, op1=ALU.add), reads=[scU, scwT, scV], writes=[scV])
                                mk.op("dve", lambda e, c=c, ci=ci: e.tensor_copy(out=sccarry.t[:, ci, :], in_=scU.t[:, c, TS:TS + 2]), reads=[scU, sccarry], writes=[sccarry])
                            elif sec == "cb":
                                mk.op("dve", lambda e, c=c, P=P: e.tensor_tensor(out=scV.t[:, c, :], in0=P, in1=scV.t[:, c, :], op=ALU.mult), reads=[p, scV], writes=[scV])
                            elif sec == "cz":
                                sl = sil.next()
                                mk.op("act", lambda e, sl=sl, P=P: e.activation(out=sl.t[:], in_=P, func=AF.Silu), reads=[p], writes=[sl])
                                ob = obf.next()
                                mk.op("dve", lambda e, sl=sl, ob=ob, c=c: e.tensor_tensor(out=ob.t[:], in0=sl.t[:], in1=scV.t[:, c, :], op=ALU.mult), reads=[sl, scV], writes=[ob])
                                mk.dma("pool", lambda e, ob=ob, ci=ci: e.dma_start(out=S["mixT"][24 + ci, :, t0:t0 + TS], in_=ob.t[:]), ob, reads=[ob], writes=[DSs["mixT"]])
                for t in range(3):
                    mk.dma("pool", lambda e, t=t: e.dma_start(out=O["gc" + sfx][l, t].rearrange("(c p) -> p c", p=128), in_=carry.t[:, :, t]), carry, reads=[carry])
                for t in range(2):
                    mk.dma("pool", lambda e, t=t: e.dma_start(out=O["sc" + sfx][l, t].rearrange("(c p) -> p c", p=128), in_=sccarry.t[:, :, t]), sccarry, reads=[sccarry])

        def gdn_phase(sfx, T, l):
            S = SCR[sfx]
            DSs = DS[sfx]
            C = min(128, T)
            NCH = T // C
            NLEV = 6 if C == 128 else 4
            W3 = H * C
            with mk.phase():
                mk.limit = getattr(mk, "nops", 0) + DEBUG_OPS
                Sst = mk.sb("Sst", [128, H, 128])
                Sbf = mk.sb("Sbf", [128, H, 128], BF16)
                if sfx == "p":
                    mk.op("dve", lambda e: e.memset(Sst.t[:], 0.0), writes=[Sst])
                else:
                    mk.dma("sp", lambda e: e.dma_start(out=Sst.t[:], in_=I["sg"][l].rearrange("h k v -> k h v")), Sst, writes=[Sst])
                mk.op("act", lambda e: e.activation(out=Sbf.t[:], in_=Sst.t[:], func=AF.Copy), reads=[Sst], writes=[Sbf])
                NB = 2
                qT = Rot([mk.sb("qT%d" % i, [128, H, C], BF16) for i in range(NB)])
                kT = Rot([mk.sb("kT%d" % i, [128, H, C], BF16) for i in range(NB)])
                ktok = Rot([mk.sb("ktok%d" % i, [C, H, 128], BF16) for i in range(NB)])
                vtok = Rot([mk.sb("vtok%d" % i, [C, H, 128], BF16) for i in range(NB)])
                gbt = Rot([mk.sb("gbt%d" % i, [C, 24]) for i in range(NB)])
                azt = Rot([mk.sb("azt%d" % i, [128, H, C], BF16) for i in range(NB)])
                pA = [mk.ps("pA%d" % i, [128, 512]) for i in range(3)]
                pB = [mk.ps("pB%d" % i, [128, 512]) for i in range(3)]
                pC = mk.ps("pC", [128, 512])
                pD = mk.ps("pD", [128, 512])
                sm = mk.sb("sm", [128, 8, H])
                X2 = mk.sb("X2", [C, H, C])
                D0 = mk.sb("D0", [C, H, C])
                DU = mk.sb("DU", [C, H, C])
                DL = mk.sb("DL", [C, H, C])
                EGB = mk.sb("EGB", [128, H, C], BF16)
                BBs = mk.sb("BBs", [C, H, C])
                tmp = mk.sb("tmp", [C, H, C])
                X1 = tmp
                Mx = [mk.sb("Mx%d" % i, [C, H, C]) for i in range(1)]
                MTx = [mk.sb("MTx%d" % i, [C, H, C]) for i in range(1)]
                Pm = [mk.sb("Pm%d" % i, [C, H, C], BF16) for i in range(2)]
                Rm = [mk.sb("Rm%d" % i, [C, H, C], BF16) for i in range(2)]
                MmB = Rot([mk.sb("MmB%d" % i, [C, H, C], BF16) for i in range(2)])
                MTmB = Rot([mk.sb("MTmB%d" % i, [C, H, C], BF16) for i in range(2)])
                EsB = mk.sb("EsB", [C, H, C], BF16)
                EpsB = mk.sb("EpsB", [C, H, C], BF16)

                attnT = mk.sb("attnT", [C, H, C], BF16)
                vb = mk.sb("vb", [C, H, 128], BF16)
                kbg = mk.sb("kbg", [C, H, 128], BF16)
                kdec = mk.sb("kdec", [C, H, 128], BF16)
                usb = mk.sb("usb", [C, H, 128])
                wT = mk.sb("wT", [128, H, C], BF16)
                qdT = mk.sb("qdT", [128, H, C], BF16)
                vnew = mk.sb("vnew", [C, H, 128], BF16)
                sqo = mk.sb("sqo", [128, H, C], BF16)
                rno = D0 if C == 128 else mk.sb("rno", [128, H, C])
                yo = DL if C == 128 else mk.sb("yo", [128, H, C])
                yb = Rot([mk.sb("yb%d" % i, [128, H, C], BF16) for i in range(2)])
                f2 = lambda ap: ap.rearrange("p a b -> p (a b)")
                HG = min(H, 512 // C)
                HP = [(h0, min(H, h0 + HG)) for h0 in range(0, H, HG)]
                HP128 = [(h0, h0 + 4) for h0 in range(0, H, 4)]

                def BK(ps, h, w):
                    return ps[(h * w) // 512]

                def PW(ps, rows, h, w):
                    o = (h * w) % 512
                    return ps[(h * w) // 512].t[:rows, o:o + w]

                def P3(ps, rows, h0, h1, w):
                    o = (h0 * w) % 512
                    return ps[(h0 * w) // 512].t[:rows, o:o + (h1 - h0) * w].rearrange("p (a b) -> p a b", a=h1 - h0)

                def bch(ap2, h0, h1, n):
                    return ap2[:, h0:h1].unsqueeze(2).to_broadcast([ap2.shape[0], h1 - h0, n])

                def mb(m2, h0, h1):
                    return m2.unsqueeze(1).to_broadcast([m2.shape[0], h1 - h0, m2.shape[1]])

                def bc(ap2, n):
                    return ap2.unsqueeze(2).to_broadcast([ap2.shape[0], H, n])

                def mm_banks(dst, lhsT, rhs_buf, rhs2, width, reads):
                    for c0 in range(0, width, 512):
                        c1 = min(width, c0 + 512)
                        mk.op("pe", lambda e, c0=c0, c1=c1: e.matmul(dst[:, c0:c1], lhsT=lhsT, rhs=rhs2[:, c0:c1], start=True, stop=True), reads=reads, writes=[rhs_buf[1]])

                for c in range(NCH):
                    t0 = c * C
                    q_ = qT.next(); k_ = kT.next(); kt = ktok.next(); vt = vtok.next(); gb = gbt.next(); az = azt.next()
                    mk.dma("sp", lambda e, q_=q_: e.dma_start(out=q_.t[:], in_=S["gqT"][:, :, t0:t0 + C].rearrange("h d t -> d h t")), q_, reads=[DSs["gqT"]], writes=[q_])
                    mk.dma("sp", lambda e, k_=k_: e.dma_start(out=k_.t[:], in_=S["gkT"][:, :, t0:t0 + C].rearrange("h d t -> d h t")), k_, reads=[DSs["gkT"]], writes=[k_])
                    mk.dma("sp", lambda e, kt=kt: e.dma_start(out=kt.t[:].rearrange("p a b -> p (a b)"), in_=S["gk"][t0:t0 + C, :]), kt, reads=[DSs["gk"]], writes=[kt])
                    mk.dma("sp", lambda e, vt=vt: e.dma_start(out=vt.t[:].rearrange("p a b -> p (a b)"), in_=S["gv"][t0:t0 + C, :]), vt, reads=[DSs["gv"]], writes=[vt])
                    mk.dma("sp", lambda e, gb=gb: e.dma_start(out=gb.t[:], in_=S["gbt"][t0:t0 + C, :]), gb, reads=[DSs["gbt"]], writes=[gb])
                    mk.dma("sp", lambda e, az=az: e.dma_start(out=az.t[:], in_=S["azT"][:, :, t0:t0 + C].rearrange("h d t -> d h t")), az, reads=[DSs["azT"]], writes=[az])
                    G = gb.t[:, 0:12]
                    Bt = gb.t[:, 12:24]
                    gcum = sm.t[:C, 0, :]; eg = sm.t[:C, 1, :]; egl = sm.t[:C, 2, :]; bk = sm.t[:C, 3, :]; gl = sm.t[:, 4, :]
                    mk.op("pe", lambda e: e.matmul(pC.t[:C, 0:12], lhsT=UINC[:C, :C], rhs=G, start=True, stop=True), reads=[cst, gb], writes=[pC])
                    mk.op("pe", lambda e: e.matmul(pD.t[:, 0:12], lhsT=onesf.t[:C, :], rhs=G, start=True, stop=True), reads=[onesf, gb], writes=[pD])
                    mk.op("dve", lambda e: e.tensor_copy(out=gcum, in_=pC.t[:C, 0:12]), reads=[pC], writes=[sm])
                    mk.op("act", lambda e: e.activation(out=eg, in_=pC.t[:C, 0:12], func=AF.Exp), reads=[pC], writes=[sm])
                    mk.op("act", lambda e: e.activation(out=gl, in_=pD.t[:, 0:12], func=AF.Exp), reads=[pD], writes=[sm])
                    mk.op("dve", lambda e: e.tensor_tensor(out=egl, in0=pD.t[:C, 0:12], in1=gcum, op=ALU.subtract), reads=[pD, sm], writes=[sm])
                    mk.op("act", lambda e: e.activation(out=egl, in_=egl, func=AF.Exp), reads=[sm], writes=[sm])
                    mk.op("dve", lambda e: e.tensor_tensor(out=bk, in0=Bt, in1=eg, op=ALU.mult), reads=[gb, sm], writes=[sm])
                    mk.op("dve", lambda e: e.tensor_tensor(out=X1.t[:], in0=bc(G, C), in1=UINC[:C, :C].unsqueeze(1).to_broadcast([C, H, C]), op=ALU.mult), reads=[gb, cst], writes=[X1])
                    for c0 in range(0, W3, 512):
                        c1 = min(W3, c0 + 512)
                        mk.op("pe", lambda e, c0=c0, c1=c1: e.matmul(pA[c0 // 512].t[:, 0:c1 - c0], lhsT=onesf.t[:C, :], rhs=f2(X1.t[:])[:, c0:c1], start=True, stop=True), reads=[onesf, X1], writes=[pA[c0 // 512]])
                    for (h0, h1) in HP:
                        mk.op("dve", lambda e, h0=h0, h1=h1: e.tensor_tensor(out=D0.t[:, h0:h1, :], in0=P3(pA, C, h0, h1, C), in1=bch(gcum, h0, h1, C), op=ALU.subtract), reads=[BK(pA, h0, C), sm], writes=[D0])
                        mk.op("act", lambda e, h0=h0, h1=h1: e.activation(out=EGB.t[:, h0:h1, :], in_=P3(pA, 128, h0, h1, C), func=AF.Exp), reads=[BK(pA, h0, C), D0], writes=[EGB])
                    mk.op("pool", lambda e: e.tensor_tensor(out=DU.t[:], in0=D0.t[:], in1=MASKU[:C, :C].unsqueeze(1).to_broadcast([C, H, C]), op=ALU.add), reads=[D0, cst], writes=[DU])
                    mk.op("dve", lambda e: e.scalar_tensor_tensor(out=DL.t[:], in0=D0.t[:], scalar=-1.0, in1=MASKL[:C, :C].unsqueeze(1).to_broadcast([C, H, C]), op0=ALU.mult, op1=ALU.add), reads=[D0, cst], writes=[DL])
                    mk.op("act", lambda e: e.activation(out=f2(DU.t[:]), in_=f2(DU.t[:]), func=AF.Exp), reads=[DU], writes=[DU])
                    mk.op("act", lambda e: e.activation(out=f2(DL.t[:]), in_=f2(DL.t[:]), func=AF.Exp), reads=[DL], writes=[DL])
                    mk.op("pool", lambda e: e.tensor_tensor(out=X2.t[:], in0=bc(Bt, C), in1=ident[:C, :C].unsqueeze(1).to_broadcast([C, H, C]), op=ALU.mult), reads=[gb, cst], writes=[X2])
                    for c0 in range(0, W3, 512):
                        c1 = min(W3, c0 + 512)
                        mk.op("pe", lambda e, c0=c0, c1=c1: e.matmul(pB[c0 // 512].t[:C, 0:c1 - c0], lhsT=SUS[:C, :C], rhs=f2(X2.t[:])[:, c0:c1], start=True, stop=True), reads=[cst, X2], writes=[pB[c0 // 512]])
                    for (h0, h1) in HP:
                        mk.op("act", lambda e, h0=h0, h1=h1: e.activation(out=BBs.t[:, h0:h1, :], in_=P3(pB, C, h0, h1, C), func=AF.Copy), reads=[BK(pB, h0, C)], writes=[BBs])
                    for h in range(H):
                        mk.op("pe", lambda e, h=h: e.matmul(PW(pA, C, h, C), lhsT=k_.t[:, h, :], rhs=k_.t[:, h, :], start=True, stop=True), reads=[k_], writes=[BK(pA, h, C)])
                    for h in range(H):
                        mk.op("pe", lambda e, h=h: e.matmul(PW(pB, C, h, C), lhsT=k_.t[:, h, :], rhs=q_.t[:, h, :], start=True, stop=True), reads=[k_, q_], writes=[BK(pB, h, C)])
                    M0, MT0 = Mx[0], MTx[0]
                    for (h0, h1) in HP:
                        mk.op("dve", lambda e, h0=h0, h1=h1: e.tensor_tensor(out=tmp.t[:, h0:h1, :], in0=P3(pA, C, h0, h1, C), in1=DU.t[:, h0:h1, :], op=ALU.mult), reads=[BK(pA, h0, C), DU], writes=[tmp])
                    mk.op("dve", lambda e: e.scalar_tensor_tensor(out=f2(M0.t[:]), in0=f2(tmp.t[:]), scalar=-1.0, in1=f2(BBs.t[:]), op0=ALU.mult, op1=ALU.mult), reads=[tmp, BBs], writes=[M0])
                    for (h0, h1) in HP:
                        mk.op("dve", lambda e, h0=h0, h1=h1: e.tensor_tensor(out=tmp.t[:, h0:h1, :], in0=P3(pA, C, h0, h1, C), in1=DL.t[:, h0:h1, :], op=ALU.mult), reads=[BK(pA, h0, C), DL], writes=[tmp])
                    mk.op("dve", lambda e: e.scalar_tensor_tensor(out=MT0.t[:], in0=tmp.t[:], scalar=-1.0, in1=bc(Bt, C), op0=ALU.mult, op1=ALU.mult), reads=[tmp, gb], writes=[MT0])
                    for (h0, h1) in HP:
                        mk.op("dve", lambda e, h0=h0, h1=h1: e.tensor_tensor(out=attnT.t[:, h0:h1, :], in0=P3(pB, C, h0, h1, C), in1=DU.t[:, h0:h1, :], op=ALU.mult), reads=[BK(pB, h0, C), DU], writes=[attnT])
                    Es, Eps = EsB, EpsB
                    NL = 7 if C == 128 else 5
                    Q, R = Pm[0], Rm[0]
                    mk.op("pool", lambda e: e.tensor_tensor(out=Q.t[:], in0=M0.t[:], in1=mb(mkb.t[:C, 7, 0:C], 0, H), op=ALU.mult), reads=[M0, mkb], writes=[Q])
                    mk.op("pool", lambda e: e.tensor_tensor(out=Q.t[:], in0=Q.t[:], in1=mb(ident[:C, :C], 0, H), op=ALU.add), reads=[Q, cst], writes=[Q])
                    mk.op("pool", lambda e: e.tensor_tensor(out=R.t[:], in0=MT0.t[:], in1=mb(mkb.t[:C, 0, 0:C], 0, H), op=ALU.mult), reads=[MT0, mkb], writes=[R])
                    mk.op("pool", lambda e: e.tensor_tensor(out=R.t[:], in0=R.t[:], in1=mb(ident[:C, :C], 0, H), op=ALU.add), reads=[R, cst], writes=[R])
                    cur = 0
                    for lev in range(1, NL):
                        Q, R = Pm[cur], Rm[cur]
                        Qn, Rn = Pm[1 - cur], Rm[1 - cur]
                        Mm, MTm = MmB.next(), MTmB.next()
                        mk.op("pool", lambda e, lev=lev: e.tensor_tensor(out=Mm.t[:], in0=M0.t[:], in1=mb(mkb.t[:C, 7 + lev, 0:C], 0, H), op=ALU.mult), reads=[M0, mkb], writes=[Mm])
                        mk.op("pool", lambda e, lev=lev: e.tensor_tensor(out=MTm.t[:], in0=MT0.t[:], in1=mb(mkb.t[:C, lev, 0:C], 0, H), op=ALU.mult), reads=[MT0, mkb], writes=[MTm])
                        for h in range(H):
                            mk.op("pe", lambda e, h=h, Q=Q: e.matmul(PW(pA, C, h, C), lhsT=MTm.t[:, h, :], rhs=Q.t[:, h, :], start=True, stop=True), reads=[MTm, Q], writes=[BK(pA, h, C)])
                        for h in range(H):
                            mk.op("pe", lambda e, h=h, R=R: e.matmul(PW(pB, C, h, C), lhsT=Mm.t[:, h, :], rhs=R.t[:, h, :], start=True, stop=True), reads=[Mm, R], writes=[BK(pB, h, C)])
                        for (h0, h1) in HP:
                            mk.op("act", lambda e, h0=h0, h1=h1: e.activation(out=Es.t[:, h0:h1, :], in_=P3(pA, C, h0, h1, C), func=AF.Copy), reads=[BK(pA, h0, C)], writes=[Es])
                            mk.op("dve", lambda e, h0=h0, h1=h1: e.tensor_copy(out=Eps.t[:, h0:h1, :], in_=P3(pB, C, h0, h1, C)), reads=[BK(pB, h0, C)], writes=[Eps])
                        for h in range(H):
                            mk.op("pe", lambda e, h=h, R=R: e.matmul(PW(pA, C, h, C), lhsT=R.t[:, h, :], rhs=Es.t[:, h, :], start=True, stop=True), reads=[R, Es], writes=[BK(pA, h, C)])
                        for h in range(H):
                            mk.op("pe", lambda e, h=h, Q=Q: e.matmul(PW(pB, C, h, C), lhsT=Q.t[:, h, :], rhs=Eps.t[:, h, :], start=True, stop=True), reads=[Q, Eps], writes=[BK(pB, h, C)])
                        for (h0, h1) in HP:
                            mk.op("dve", lambda e, h0=h0, h1=h1, Q=Q, Qn=Qn: e.tensor_tensor(out=Qn.t[:, h0:h1, :], in0=P3(pA, C, h0, h1, C), in1=Q.t[:, h0:h1, :], op=ALU.add), reads=[BK(pA, h0, C), Q], writes=[Qn])
                            mk.op("dve", lambda e, h0=h0, h1=h1, R=R, Rn=Rn: e.tensor_tensor(out=Rn.t[:, h0:h1, :], in0=P3(pB, C, h0, h1, C), in1=R.t[:, h0:h1, :], op=ALU.add), reads=[BK(pB, h0, C), R], writes=[Rn])
                        cur = 1 - cur
                    PT = Pm[cur]
                    mk.op("pool", lambda e: e.tensor_tensor(out=vb.t[:], in0=vt.t[:], in1=bc(Bt, 128), op=ALU.mult), reads=[vt, gb], writes=[vb])
                    mk.op("pool", lambda e: e.tensor_tensor(out=kbg.t[:], in0=kt.t[:], in1=bc(bk, 128), op=ALU.mult), reads=[kt, sm], writes=[kbg])
                    mk.op("pool", lambda e: e.tensor_tensor(out=kdec.t[:], in0=kt.t[:], in1=bc(egl, 128), op=ALU.mult), reads=[kt, sm], writes=[kdec])
                    mk.op("pool", lambda e: e.tensor_tensor(out=qdT.t[:], in0=q_.t[:], in1=EGB.t[:], op=ALU.mult), reads=[q_, EGB], writes=[qdT])
                    for h in range(H):
                        mk.op("pe", lambda e, h=h: e.matmul(PW(pA, C, h, 128), lhsT=PT.t[:, h, :], rhs=vb.t[:, h, :], start=True, stop=True), reads=[PT, vb], writes=[BK(pA, h, 128)])
                    for h in range(H):
                        mk.op("pe", lambda e, h=h: e.matmul(PW(pB, 128, h, C), lhsT=kbg.t[:, h, :], rhs=PT.t[:, h, :], start=True, stop=True), reads=[PT, kbg], writes=[BK(pB, h, C)])
                    for (h0, h1) in HP128:
                        mk.op("act", lambda e, h0=h0, h1=h1: e.activation(out=usb.t[:, h0:h1, :], in_=P3(pA, C, h0, h1, 128), func=AF.Copy), reads=[BK(pA, h0, 128)], writes=[usb])
                    for (h0, h1) in HP:
                        mk.op("dve", lambda e, h0=h0, h1=h1: e.tensor_copy(out=wT.t[:, h0:h1, :], in_=P3(pB, 128, h0, h1, C)), reads=[BK(pB, h0, C)], writes=[wT])
                    for h in range(H):
                        mk.op("pe", lambda e, h=h: e.matmul(PW(pA, C, h, 128), lhsT=wT.t[:, h, :], rhs=Sbf.t[:, h, :], start=True, stop=True), reads=[wT, Sbf], writes=[BK(pA, h, 128)])
                    for (h0, h1) in HP128:
                        mk.op("dve", lambda e, h0=h0, h1=h1: e.tensor_tensor(out=vnew.t[:, h0:h1, :], in0=usb.t[:, h0:h1, :], in1=P3(pA, C, h0, h1, 128), op=ALU.subtract), reads=[usb, BK(pA, h0, 128)], writes=[vnew])
                    for h in range(H):
                        mk.op("pe", lambda e, h=h: e.matmul(PW(pB, 128, h, C), lhsT=Sbf.t[:, h, :], rhs=qdT.t[:, h, :], start=True, stop=False), reads=[Sbf, qdT], writes=[BK(pB, h, C)])
                        mk.op("pe", lambda e, h=h: e.matmul(PW(pB, 128, h, C), lhsT=vnew.t[:, h, :], rhs=attnT.t[:, h, :], start=False, stop=True), reads=[vnew, attnT], writes=[BK(pB, h, C)])
                    for h in range(H):
                        mk.op("pe", lambda e, h=h: e.matmul(PW(pA, 128, h, 128), lhsT=kdec.t[:, h, :], rhs=vnew.t[:, h, :], start=True, stop=True), reads=[kdec, vnew], writes=[BK(pA, h, 128)])
                    mk.op("dve", lambda e: e.tensor_tensor(out=Sst.t[:], in0=Sst.t[:], in1=sm.t[:, 4, :].unsqueeze(2).to_broadcast([128, H, 128]), op=ALU.mult), reads=[Sst, sm], writes=[Sst])
                    for (h0, h1) in HP128:
                        mk.op("dve", lambda e, h0=h0, h1=h1: e.tensor_tensor(out=Sst.t[:, h0:h1, :], in0=Sst.t[:, h0:h1, :], in1=P3(pA, 128, h0, h1, 128), op=ALU.add), reads=[Sst, BK(pA, h0, 128)], writes=[Sst])
                    mk.op("act", lambda e: e.activation(out=f2(Sbf.t[:]), in_=f2(Sst.t[:]), func=AF.Copy), reads=[Sst], writes=[Sbf])
                    for (h0, h1) in HP:
                        mk.op("act", lambda e, h0=h0, h1=h1: e.activation(out=sqo.t[:, h0:h1, :], in_=P3(pB, 128, h0, h1, C), func=AF.Square), reads=[BK(pB, h0, C)], writes=[sqo])
                    for c0 in range(0, W3, 512):
                        c1 = min(W3, c0 + 512)
                        mk.op("pe", lambda e, c0=c0, c1=c1: e.matmul(pA[c0 // 512].t[:, 0:c1 - c0], lhsT=ONESB, rhs=f2(sqo.t[:])[:, c0:c1], start=True, stop=True), reads=[cbf, sqo], writes=[pA[c0 // 512]])
                    for (h0, h1) in HP:
                        mk.op("act", lambda e, h0=h0, h1=h1: e.activation(out=rno.t[:, h0:h1, :], in_=P3(pA, 128, h0, h1, C), func=AF.Sqrt, scale=1.0 / 128, bias=EPS), reads=[BK(pA, h0, C)], writes=[rno])
                    mk.op("dve", lambda e: e.reciprocal(out=f2(rno.t[:]), in_=f2(rno.t[:])), reads=[rno], writes=[rno])
                    for (h0, h1) in HP:
                        mk.op("dve", lambda e, h0=h0, h1=h1: e.tensor_tensor(out=yo.t[:, h0:h1, :], in0=P3(pB, 128, h0, h1, C), in1=rno.t[:, h0:h1, :], op=ALU.mult), reads=[BK(pB, h0, C), rno], writes=[yo])
                    y_ = yb.next()
                    mk.op("pool", lambda e, y_=y_: e.tensor_tensor(out=y_.t[:], in0=yo.t[:], in1=az.t[:], op=ALU.mult), reads=[yo, az], writes=[y_])
                    mk.dma("pool", lambda e, y_=y_: e.dma_start(out=S["mixT"][0:12, :, t0:t0 + C].rearrange("h d t -> d h t"), in_=y_.t[:]), y_, reads=[y_], writes=[DSs["mixT"]])
                mk.dma("pool", lambda e: e.dma_start(out=O["g" + sfx][l].rearrange("h k v -> k h v"), in_=Sst.t[:]), Sst, reads=[Sst])

        def sb_phase(sfx, T, l):
            S = SCR[sfx]
            DSs = DS[sfx]
            KB = min(128, T)
            NQ_ = min(512, T)
            NQT = T // NQ_
            NNB = T // KB
            npast = NPB if sfx == "s" else 0
            NBT = npast + NNB
            with mk.phase():
                QTh = Rot([mk.sb("QTh%d" % i, [128, T], BF16) for i in range(2)])
                bzh = Rot([mk.sb("bzh%d" % i, [128, T], BF16) for i in range(2)])
                Kf = Rot([mk.sb("Kf%d" % i, [128, NBT, 128]) for i in range(2)])
                Vf = Rot([mk.sb("Vf%d" % i, [128, NBT, 128]) for i in range(2)])
                KTh = Rot([mk.sb("KTh%d" % i, [128, NBT, 128], BF16) for i in range(2)])
                Vb = Rot([mk.sb("Vb%d" % i, [128, NBT, 128], BF16) for i in range(2)])
                pz = Rot([mk.ps("pz%d" % i, [128, 512]) for i in range(2)])
                pc = Rot([mk.ps("pc%d" % i, [128, 512]) for i in range(2)])
                po = Rot([mk.ps("po%d" % i, [128, 512]) for i in range(2)])
                pt = Rot([mk.ps("pt%d" % i, [128, 512]) for i in range(2)])
                zs = Rot([mk.sb("zs%d" % i, [128, NQ_]) for i in range(3)])
                ee = Rot([mk.sb("ee%d" % i, [128, NQ_]) for i in range(2)])
                sp_ = Rot([mk.sb("sp%d" % i, [128, NQ_], BF16) for i in range(3)])
                lg = Rot([mk.sb("lg%d" % i, [128, NQ_]) for i in range(2)])
                aT = Rot([mk.sb("aT%d" % i, [128, NQ_], BF16) for i in range(3)])
                Rr = Rot([mk.sb("R%d" % i, [128, NQ_], BF16) for i in range(2)])
                om = Rot([mk.sb("om%d" % i, [128, NQ_], BF16) for i in range(2)])
                kout = O["k" + sfx]
                vout = O["v" + sfx]
                for h in range(H):
                    qh = QTh.next(); bz = bzh.next(); kf = Kf.next(); vf = Vf.next(); kth = KTh.next(); vb = Vb.next()
                    hs = slice(h * 128, (h + 1) * 128)
                    mk.dma("sp", lambda e, qh=qh, h=h: e.dma_start(out=qh.t[:], in_=S["QT"][h]), qh, reads=[DSs["QT"]], writes=[qh])
                    mk.dma("sp", lambda e, bz=bz, h=h: e.dma_start(out=bz.t[:], in_=S["bzT"][h]), bz, reads=[DSs["bzT"]], writes=[bz])
                    if npast:
                        mk.dma("sp", lambda e, kf=kf, hs=hs: e.dma_start(out=kf.t[:, 0:npast, :], in_=I["ck"][l, :, hs].rearrange("(b p) d -> p b d", p=128)), kf, writes=[kf])
                        mk.dma("sp", lambda e, vf=vf, hs=hs: e.dma_start(out=vf.t[:, 0:npast, :], in_=I["cv"][l, :, hs].rearrange("(b p) d -> p b d", p=128)), vf, writes=[vf])
                    mk.dma("sp", lambda e, kf=kf, hs=hs: e.dma_start(out=kf.t[:KB, npast:NBT, :], in_=kout[l, :, hs].rearrange("(b p) d -> p b d", p=KB)), kf, reads=[DKV[sfx]], writes=[kf])
                    mk.dma("sp", lambda e, vf=vf, hs=hs: e.dma_start(out=vf.t[:KB, npast:NBT, :], in_=vout[l, :, hs].rearrange("(b p) d -> p b d", p=KB)), vf, reads=[DKV[sfx]], writes=[vf])
                    for b0 in range(0, NBT, 4):
                        p = pt.next()
                        nb_ = min(4, NBT - b0)
                        kbs = []
                        for j in range(nb_):
                            b = b0 + j
                            kb = 128 if b < npast else KB
                            kbs.append(kb)
                            mk.op("pe", lambda e, p=p, j=j, b=b, kb=kb, kf=kf: e.transpose(out=p.t[:, j * 128:j * 128 + kb], in_=kf.t[:kb, b, :], identity=ident[:kb, :kb]), reads=[kf, cst], writes=[p])
                        if all(k == 128 for k in kbs):
                            mk.op("act", lambda e, p=p, b0=b0, nb_=nb_, kth=kth: e.activation(out=kth.t[:, b0:b0 + nb_, :].rearrange("p a b -> p (a b)"), in_=p.t[:, 0:nb_ * 128], func=AF.Copy), reads=[p], writes=[kth])
                        else:
                            for j in range(nb_):
                                mk.op("act", lambda e, p=p, b0=b0, j=j, kth=kth, kb=kbs[j]: e.activation(out=kth.t[:, b0 + j, 0:kb], in_=p.t[:, j * 128:j * 128 + kb], func=AF.Copy), reads=[p], writes=[kth])
                    if npast:
                        mk.op("pool", lambda e, vb=vb, vf=vf: e.tensor_copy(out=vb.t[:, 0:npast, :], in_=vf.t[:, 0:npast, :]), reads=[vf], writes=[vb])
                    mk.op("pool", lambda e, vb=vb, vf=vf: e.tensor_copy(out=vb.t[:KB, npast:NBT, :], in_=vf.t[:KB, npast:NBT, :]), reads=[vf], writes=[vb])
                    for qi in range(NQT):
                        q0 = qi * NQ_
                        blocks = []
                        nb_hi = (q0 + NQ_ - 1) // KB
                        for b in range(nb_hi, -1, -1):
                            bd = b - q0 // KB
                            blocks.append((npast + b, KB, bd if bd >= 0 else None))
                        for b in range(npast - 1, -1, -1):
                            blocks.append((b, 128, None))
                        R = Rr.next()
                        mk.op("pool", lambda e, R=R: e.memset(R.t[:], 0.0), writes=[R])
                        pov = po.next()
                        for bi, (b, kb, bd) in enumerate(blocks):
                            first = bi == 0
                            lastb = bi == len(blocks) - 1
                            z = pz.next()
                            mk.op("pe", lambda e, z=z, b=b, kb=kb, kth=kth, qh=qh: e.matmul(z.t[:kb, 0:NQ_], lhsT=kth.t[:, b, 0:kb], rhs=qh.t[:, q0:q0 + NQ_], start=True, stop=True), reads=[kth, qh], writes=[z])
                            zz = zs.next()
                            if bd is None:
                                mk.op("dve", lambda e, z=z, zz=zz, kb=kb: e.tensor_copy(out=zz.t[:kb, :], in_=z.t[:kb, 0:NQ_]), reads=[z], writes=[zz])
                            else:
                                mk.op("dve", lambda e, z=z, zz=zz, kb=kb, bd=bd: e.tensor_tensor(out=zz.t[:kb, :], in0=z.t[:kb, 0:NQ_], in1=MD[:kb, bd, 0:NQ_], op=ALU.add), reads=[z, cst], writes=[zz])
                            ex = ee.next()
                            mk.op("act", lambda e, zz=zz, ex=ex, kb=kb: e.activation(out=ex.t[:kb, :], in_=zz.t[:kb, :], func=AF.Exp), reads=[zz], writes=[ex])
                            sp = sp_.next()
                            mk.op("act", lambda e, sp=sp, ex=ex, kb=kb: e.activation(out=sp.t[:kb, :], in_=ex.t[:kb, :], func=AF.Ln, bias=1.0), reads=[ex], writes=[sp])
                            cc = pc.next()
                            mk.op("pe", lambda e, cc=cc, sp=sp, kb=kb, first=first: e.matmul(cc.t[:kb, 0:NQ_], lhsT=TRIB[:kb, :kb], rhs=sp.t[:kb, :], start=True, stop=first), reads=[cbf, sp], writes=[cc])
                            if not first:
                                mk.op("pe", lambda e, cc=cc, R=R, kb=kb: e.matmul(cc.t[:kb, 0:NQ_], lhsT=ONESB[:, :kb], rhs=R.t[:], start=False, stop=True), reads=[cbf, R], writes=[cc])
                            if not lastb:
                                mk.op("pool", lambda e, R=R, sp=sp, kb=kb: e.tensor_tensor(out=R.t[:kb, :], in0=R.t[:kb, :], in1=sp.t[:kb, :], op=ALU.add), reads=[R, sp], writes=[R])
                            lgt = lg.next()
                            mk.op("dve", lambda e, lgt=lgt, zz=zz, cc=cc, kb=kb: e.tensor_tensor(out=lgt.t[:kb, :], in0=zz.t[:kb, :], in1=cc.t[:kb, 0:NQ_], op=ALU.subtract), reads=[zz, cc], writes=[lgt])
                            a_ = aT.next()
                            mk.op("act", lambda e, a_=a_, lgt=lgt, kb=kb: e.activation(out=a_.t[:kb, :], in_=lgt.t[:kb, :], func=AF.Exp), reads=[lgt], writes=[a_])
                            mk.op("pe", lambda e, pov=pov, vb=vb, a_=a_, b=b, kb=kb, first=first, lastb=lastb: e.matmul(pov.t[:, 0:NQ_], lhsT=vb.t[:kb, b, :], rhs=a_.t[:kb, :], start=first, stop=lastb), reads=[vb, a_], writes=[pov])
                        o_ = om.next()
                        mk.op("dve", lambda e, o_=o_, pov=pov, bz=bz: e.tensor_tensor(out=o_.t[:], in0=pov.t[:, 0:NQ_], in1=bz.t[:, q0:q0 + NQ_], op=ALU.mult), reads=[pov, bz], writes=[o_])
                        mk.dma("pool", lambda e, o_=o_, h=h: e.dma_start(out=S["mixT"][12 + h, :, q0:q0 + NQ_], in_=o_.t[:]), o_, reads=[o_], writes=[DSs["mixT"]])

        def out_phase(sfx, T, l, xin, Dxin):
            S = SCR[sfx]
            DSs = DS[sfx]
            TT = min(128, T)
            TSO = min(256, T)
            NTT = TSO // TT
            yout = O["y" + sfx]
            with mk.phase():
                mx = Rot([mk.sb("mx%d" % i, [128, 32, TSO], BF16) for i in range(2)])
                Wo = Rot([mk.sb("Wo%d" % i, [128, 32, 512], BF16) for i in range(2)])
                ybuf = [mk.sb("ybuf%d" % i, [TT, D]) for i in range(NTT)]
                xt = Rot([mk.sb("xto%d" % i, [TT, D]) for i in range(2)])
                junk = mk.sb("junko", [TT, D], BF16)
                st1 = Rot([mk.sb("sto%d" % i, [TT, 2]) for i in range(2)])
                pm = Rot([mk.ps("pmo%d" % i, [128, 512]) for i in range(4)])
                gpostb = mk.sb("gpostb", [128, D])
                mk.dma("sp", lambda e: e.dma_start(out=gpostb.t[:], in_=I["norm_post"][l:l + 1, :].to_broadcast([128, D])), gpostb, writes=[gpostb])
                nev = 0
                for so in range(T // TSO):
                    t0 = so * TSO
                    m = mx.next()
                    mk.dma("sp", lambda e, m=m: e.dma_start(out=m.t[:], in_=S["mixT"][:, :, t0:t0 + TSO].rearrange("c p t -> p c t")), m, reads=[DSs["mixT"]], writes=[m])
                    for nb in range(8):
                        W = Wo.next()
                        mk.dma("sp", lambda e, W=W, nb=nb: e.dma_start(out=W.t[:], in_=wobf[nb]), W, reads=[Dwo], writes=[W])
                        for tt in range(NTT):
                            p = pm.next()
                            for mc in range(32):
                                mk.op("pe", lambda e, p=p, mc=mc, tt=tt, W=W, m=m: e.matmul(p.t[:TT, :], lhsT=m.t[:, mc, tt * TT:(tt + 1) * TT], rhs=W.t[:, mc, :], start=(mc == 0), stop=(mc == 31)), reads=[m, W], writes=[p])
                            nev += 1
                            yb_ = ybuf[tt]
                            if nev % 2:
                                mk.op("act", lambda e, p=p, yb_=yb_, nb=nb: e.activation(out=yb_.t[:, nb * 512:(nb + 1) * 512], in_=p.t[:TT, :], func=AF.Copy), reads=[p], writes=[yb_])
                            else:
                                mk.op("dve", lambda e, p=p, yb_=yb_, nb=nb: e.tensor_copy(out=yb_.t[:, nb * 512:(nb + 1) * 512], in_=p.t[:TT, :]), reads=[p], writes=[yb_])
                    for tt in range(NTT):
                        yb_ = ybuf[tt]
                        x = xt.next()
                        s1 = st1.next()
                        r0 = t0 + tt * TT
                        mk.dma("sp", lambda e, x=x, r0=r0: e.dma_start(out=x.t[:], in_=xin[r0:r0 + TT, :]), x, reads=[Dxin], writes=[x])
                        mk.op("act", lambda e, yb_=yb_, s1=s1: e.activation(out=junk.t[:], in_=yb_.t[:], func=AF.Square, accum_out=s1.t[:, 0:1]), reads=[yb_], writes=[junk, s1])
                        mk.op("act", lambda e, s1=s1: e.activation(out=s1.t[:, 1:2], in_=s1.t[:, 0:1], func=AF.Sqrt, scale=1.0 / D, bias=EPS), reads=[s1], writes=[s1])
                        mk.op("dve", lambda e, s1=s1: e.reciprocal(out=s1.t[:, 1:2], in_=s1.t[:, 1:2]), reads=[s1], writes=[s1])
                        mk.op("dve", lambda e, yb_=yb_, s1=s1: e.scalar_tensor_tensor(out=yb_.t[:], in0=yb_.t[:], scalar=s1.t[:, 1:2], in1=gpostb.t[:TT, :], op0=ALU.mult, op1=ALU.mult), reads=[yb_, s1, gpostb], writes=[yb_])
                        mk.op("pool", lambda e, yb_=yb_, x=x: e.tensor_tensor(out=x.t[:], in0=yb_.t[:], in1=x.t[:], op=ALU.add), reads=[yb_, x], writes=[x])
                        mk.dma("pool", lambda e, x=x, r0=r0: e.dma_start(out=yout[r0:r0 + TT, :], in_=x.t[:]), x, reads=[x], writes=[DY[sfx]])

        nph = [0]

        def go():
            nph[0] += 1
            return nph[0] <= DEBUG_STOP
        for l in range(DEPTH):
            load_params(l)
            if go():
                precast(l)
            for sfx, T in (("p", T_P), ("s", T_S)):
                xin = I["x" + sfx] if l == 0 else O["y" + sfx]
                Dxin = mk.dram("xin" + sfx) if l == 0 else DY[sfx]
                if go():
                    proj_phase(sfx, T, l, xin, Dxin)
                if go():
                    gdn_phase(sfx, T, l)
                if go():
                    sb_phase(sfx, T, l)
                if go():
                    out_phase(sfx, T, l, xin, Dxin)
        mk.barrier()
        mk.flush()
        nc._mk_ninst = mk.ninst
    return nc


def make_consts():
    p = np.arange(128)[:, None]
    f = np.arange(128)[None, :]
    ident = (p == f).astype(np.float32)
    uinc = (p <= f).astype(np.float32)
    sus = (p > f).astype(np.float32)
    masku = np.where(f >= p, 0.0, NEG).astype(np.float32)
    maskl = np.where(f < p, 0.0, NEG).astype(np.float32)
    t = np.arange(512)[None, None, :]
    a = np.arange(4)[None, :, None]
    md = np.where(t > 128 * a + p[:, :, None], 0.0, NEG).astype(np.float32).reshape(128, 2048)
    return np.ascontiguousarray(np.concatenate([ident, uinc, sus, masku, maskl, md], axis=1))


def make_level_masks():
    i = np.arange(128)[:, None]
    j = np.arange(128)[None, :]
    low, up = [], []
    for s_ in range(7):
        m = ((i >> (s_ + 1)) == (j >> (s_ + 1))) & (((i >> s_) & 1) == 1) & (((j >> s_) & 1) == 0)
        low.append(m.astype(np.float32))
        up.append(m.T.astype(np.float32))
    return np.ascontiguousarray(np.concatenate(low + up, axis=1))


_CACHE = {}


def run(inputs, T_P, T_S, PAST, DEPTH, n_cores=8):
    key = (T_P, T_S, PAST, DEPTH)
    if key not in _CACHE:
        _CACHE[key] = build_program(T_P, T_S, PAST, DEPTH)
    nc = _CACHE[key]
    f = lambda a: np.ascontiguousarray(np.asarray(a, dtype=np.float32))
    xp = f(inputs["x_prompt"]); xs = f(inputs["x_sample"])
    BP = xp.shape[0]
    ck = f(inputs["cache_sb_k"]); cv = f(inputs["cache_sb_v"])
    sg = f(inputs["state_gdn"]); sgc = f(inputs["state_gdn_conv"]); ssc = f(inputs["state_sc_conv"])
    shared = dict(w_in=f(inputs["w_in"]), w_out=f(inputs["w_out"]), norm_pre=f(inputs["norm_pre"]),
                  norm_post=f(inputs["norm_post"]), gcw=f(inputs["gdn_conv_w"]), alog=f(inputs["gdn_a_log"]),
                  dtb=f(inputs["gdn_dt_bias"]), gnorm=f(inputs["gdn_norm"]), scw=f(inputs["sc_conv_w"]), cst=make_consts(),
                  mks=make_level_masks())
    in_maps = []
    for c in range(n_cores):
        m = dict(shared)
        m["xp"] = xp[c % BP]
        m["xs"] = xs[c]
        m["ck"] = np.ascontiguousarray(ck[:, c].reshape(DEPTH, PAST, GW))
        m["cv"] = np.ascontiguousarray(cv[:, c].reshape(DEPTH, PAST, GW))
        m["sg"] = np.ascontiguousarray(sg[:, c])
        m["sgc"] = np.ascontiguousarray(sgc[:, c])
        m["ssc"] = np.ascontiguousarray(ssc[:, c])
        in_maps.append(m)
    res = run_bass_kernel_spmd(nc, in_maps, core_ids=list(range(n_cores)))
    R = res.results
    st = lambda name, cores, ax: np.stack([R[c][name] for c in cores], axis=ax)
    pc = list(range(min(BP, n_cores)))
    sc = list(range(n_cores))
    yp = st("yp", pc, 0)
    ys = st("ys", sc, 0)
    outs = [yp, ys]
    for sfx, cores in (("p", pc), ("s", sc)):
        T = T_P if sfx == "p" else T_S
        outs.append(st("k" + sfx, cores, 1).reshape(DEPTH, len(cores), T, H, HD))
        outs.append(st("v" + sfx, cores, 1).reshape(DEPTH, len(cores), T, H, HD))
        outs.append(st("g" + sfx, cores, 1))
        outs.append(st("gc" + sfx, cores, 1))
        outs.append(st("sc" + sfx, cores, 1))
    return tuple(np.ascontiguousarray(o.astype(np.float32)) for o in outs)


def kernel(**inputs):
    return run(inputs, 4096, 32, 2048, 4)
```

```python
import contextlib
import numpy as np
import concourse.bass as bass
import concourse.mybir as mybir
from concourse.bass_utils import run_bass_kernel_spmd

F32 = mybir.dt.float32
BF16 = mybir.dt.bfloat16
AF = mybir.ActivationFunctionType
ALU = mybir.AluOpType
AX = mybir.AxisListType

D = 4096
DIN = 16408
H = 12
HD = 128
GW = 1536
EPS = 1e-6
NEG = -30000.0
DEBUG_STOP = 10 ** 9
DEBUG_OPS = 10 ** 12


class Buf:
    def __init__(self, name, t=None):
        self.name = name
        self.t = t
        self.w = {}
        self.r = {}
        self.dsem = None
        self.dcnt = 0
        self.excl = False


class _Rec:
    def __init__(self):
        self.call = None

    def __getattr__(self, name):
        def f(*a, **k):
            self.call = (name, a, k)
            return self
        return f

    def then_inc(self, *a):
        return self


def _record(fn):
    r = _Rec()
    fn(r)
    assert r.call is not None
    return r.call


class MK:
    ENG = ("pe", "act", "dve", "pool", "sp")

    def __init__(self, nc, es, block):
        self.nc = nc
        self.es = es
        self.block = block
        self.sem = {}
        self.cnt = {}
        self.q = {e: [] for e in self.ENG}
        self.waited = {e: {} for e in self.ENG}
        for e in ("pe", "act", "dve", "pool"):
            self.sem[e] = es.enter_context(nc.semaphore("s_" + e))
            self.cnt[e] = 0
        self.free_sems = []
        self.live = []
        self.nsem = 0
        self.ninst = 0
        self.scope = None

    def sb(self, name, shape, dt=F32, es=None):
        es = es or self.scope or self.es
        self.nalloc = getattr(self, "nalloc", 0) + 1
        name = "%s_u%d" % (name, self.nalloc)
        t = es.enter_context(self.nc.sbuf_tensor(name, list(shape), dt))
        return Buf(name, t)

    def ps(self, name, shape, dt=F32, es=None):
        es = es or self.scope or self.es
        self.nalloc = getattr(self, "nalloc", 0) + 1
        name = "%s_u%d" % (name, self.nalloc)
        t = es.enter_context(self.nc.psum_tensor(name, list(shape), dt))
        b = Buf(name, t)
        b.excl = True
        return b

    def dram(self, name):
        return Buf(name, None)

    def _getsem(self, b):
        if b.dsem is None:
            if self.free_sems:
                b.dsem, b.dcnt = self.free_sems.pop()
            else:
                self.nsem += 1
                b.dsem = self.es.enter_context(self.nc.semaphore("d%d" % self.nsem))
                b.dcnt = 0
            self.live.append(b)

    def release(self, bufs):
        for b in bufs:
            if b.dsem is not None:
                self.free_sems.append((b.dsem, b.dcnt))
                self.live.remove(b)
                b.dsem = None

    def _deps(self, eng, reads, writes, dma_buf=None):
        deps = {}

        def add(d):
            for k, (s, v) in d.items():
                if k not in deps or deps[k][1] < v:
                    deps[k] = (s, v)
        own = self.sem.get(eng)
        for b in reads:
            add(b.w)
            if b.excl:
                add({k: v for k, v in b.r.items() if v[0] is not own})
        for b in writes:
            if dma_buf is not None and b is dma_buf and not b.r and b.w and all(k == id(b.dsem) for k in b.w):
                continue
            add(b.w)
            add(b.r)
        out = []
        wd = self.waited[eng]
        pes = self.sem["pe"]
        for k, (s, v) in deps.items():
            if eng == "pe" and s is pes:
                continue
            if wd.get(k, 0) >= v:
                continue
            wd[k] = v
            out.append((s, v))
        return out

    def _post(self, reads, writes, ev):
        k = id(ev[0])
        for b in reads:
            b.r[k] = ev
        for b in writes:
            b.w = {k: ev}
            b.r = {}

    def op(self, eng, fn, reads=(), writes=()):
        self.nops = getattr(self, "nops", 0) + 1
        if self.nops > getattr(self, "limit", 10 ** 12):
            return
        waits = self._deps(eng, reads, writes)
        self.cnt[eng] += 1
        s = self.sem[eng]
        v = self.cnt[eng]
        call = _record(fn)

        def emit(e, call=call, waits=waits, s=s):
            for (ws, wv) in waits:
                e.wait_ge(ws, wv)
            getattr(e, call[0])(*call[1], **call[2]).then_inc(s, 1)
        self.q[eng].append(emit)
        self.ninst += 1 + len(waits)
        self._post(reads, writes, (s, v))

    def dma(self, eng, fn, sbuf, reads=(), writes=()):
        self.nops = getattr(self, "nops", 0) + 1
        if self.nops > getattr(self, "limit", 10 ** 12):
            return
        self._getsem(sbuf)
        waits = self._deps(eng, reads, writes, dma_buf=sbuf)
        sbuf.dcnt += 16
        s = sbuf.dsem
        v = sbuf.dcnt
        call = _record(fn)

        def emit(e, call=call, waits=waits, s=s):
            for (ws, wv) in waits:
                e.wait_ge(ws, wv)
            getattr(e, call[0])(*call[1], **call[2]).then_inc(s, 16)
        self.q[eng].append(emit)
        self.ninst += 1 + len(waits)
        self._post(reads, writes, (s, v))

    def barrier(self):
        evs = [(self.sem[e], self.cnt[e]) for e in ("pe", "act", "dve", "pool") if self.cnt[e] > 0]
        evs += [(b.dsem, b.dcnt) for b in self.live if b.dcnt > 0]
        for eng in self.ENG:
            wd = self.waited[eng]
            waits = []
            for (s, v) in evs:
                if wd.get(id(s), 0) >= v:
                    continue
                wd[id(s)] = v
                waits.append((s, v))
            if waits:
                def emit(e, waits=waits):
                    for (ws, wv) in waits:
                        e.wait_ge(ws, wv)
                self.q[eng].append(emit)
                self.ninst += len(waits)

    def flush(self):
        b = self.block
        m = {"pe": b.tensor, "act": b.scalar, "dve": b.vector, "pool": b.gpsimd, "sp": b.sync}
        for eng in self.ENG:
            lst = self.q[eng]
            if not lst:
                continue

            def body(e, lst=lst):
                for f in lst:
                    f(e)
            m[eng](body)
            self.q[eng] = []

    @contextlib.contextmanager
    def phase(self):
        with contextlib.ExitStack() as pes:
            old = self.scope
            self.scope = pes
            nlive = list(self.live)
            yield pes
            self.barrier()
            self.flush()
            self.release([b for b in self.live if b not in nlive])
            self.scope = old


class Rot:
    def __init__(self, bufs):
        self.bufs = bufs
        self.i = 0

    def next(self):
        b = self.bufs[self.i % len(self.bufs)]
        self.i += 1
        return b


def sblk_col(s):
    return 512 * s if s < 12 else 6168 + 512 * (s - 12)


def build_program(T_P, T_S, PAST, DEPTH):
    nc = bass.Bass("TRN2", target_bir_lowering=False)

    def din(name, shape, dt=F32):
        return nc.dram_tensor(name, list(shape), dt, kind="ExternalInput").ap()

    def dout(name, shape, dt=F32):
        return nc.dram_tensor(name, list(shape), dt, kind="ExternalOutput").ap()

    def dscr(name, shape, dt=BF16):
        return nc.dram_tensor(name, list(shape), dt, kind="Internal").ap()

    NPB = PAST // 128
    I = dict(
        xp=din("xp", [T_P, D]), xs=din("xs", [T_S, D]),
        ck=din("ck", [DEPTH, PAST, GW]), cv=din("cv", [DEPTH, PAST, GW]),
        sg=din("sg", [DEPTH, H, HD, HD]), sgc=din("sgc", [DEPTH, 3, 3 * GW]), ssc=din("ssc", [DEPTH, 2, 1024]),
        w_in=din("w_in", [DEPTH, D, DIN]), w_out=din("w_out", [DEPTH, D, D]),
        norm_pre=din("norm_pre", [DEPTH, D]), norm_post=din("norm_post", [DEPTH, D]),
        gcw=din("gcw", [DEPTH, 4, 3 * GW]), alog=din("alog", [DEPTH, H]), dtb=din("dtb", [DEPTH, H]),
        gnorm=din("gnorm", [DEPTH, HD]), scw=din("scw", [DEPTH, 3, 1024]),
        cst=din("cst", [128, 5 * 128 + 4 * 512]),
        mks=din("mks", [128, 14 * 128]),
    )
    O = {}
    for sfx, T in (("p", T_P), ("s", T_S)):
        O["y" + sfx] = dout("y" + sfx, [T, D])
        O["k" + sfx] = dout("k" + sfx, [DEPTH, T, GW])
        O["v" + sfx] = dout("v" + sfx, [DEPTH, T, GW])
        O["g" + sfx] = dout("g" + sfx, [DEPTH, H, HD, HD])
        O["gc" + sfx] = dout("gc" + sfx, [DEPTH, 3, 3 * GW])
        O["sc" + sfx] = dout("sc" + sfx, [DEPTH, 2, 1024])
    wbf = dscr("wbf", [32, 128, 32, 512])
    wab = dscr("wab", [128, 32, 24])
    wobf = dscr("wobf", [8, 128, 32, 512])
    SCR = {}
    for sfx, T in (("p", T_P), ("s", T_S)):
        SCR[sfx] = dict(
            gqT=dscr("gqT" + sfx, [H, 128, T]), gkT=dscr("gkT" + sfx, [H, 128, T]),
            gk=dscr("gk" + sfx, [T, GW]), gv=dscr("gv" + sfx, [T, GW]),
            azT=dscr("azT" + sfx, [H, 128, T]), gbt=dscr("gbt" + sfx, [T, 24], F32),
            QT=dscr("QT" + sfx, [H, 128, T]), bzT=dscr("bzT" + sfx, [H, 128, T]),
            mixT=dscr("mixT" + sfx, [32, 128, T]),
        )

    with contextlib.ExitStack() as es:
        es.enter_context(nc.allow_non_contiguous_dma(reason="small strided parameter / state transfers"))
        es.enter_context(nc.allow_low_precision(reason="bf16 matmul operands, fp32 accumulation"))
        block = es.enter_context(nc.Block())
        mk = MK(nc, es, block)
        Dw = mk.dram("wbf")
        Dwo = mk.dram("wobf")
        DS = {sfx: {k: mk.dram(k + sfx) for k in SCR[sfx]} for sfx in ("p", "s")}
        DY = {sfx: mk.dram("y" + sfx) for sfx in ("p", "s")}
        DKV = {sfx: mk.dram("kv" + sfx) for sfx in ("p", "s")}

        cst = mk.sb("cst", [128, 5 * 128 + 4 * 512])
        mk.dma("sp", lambda e: e.dma_start(out=cst.t[:], in_=I["cst"]), cst, writes=[cst])
        ident = cst.t[:, 0:128]
        UINC = cst.t[:, 128:256]
        SUS = cst.t[:, 256:384]
        MASKU = cst.t[:, 384:512]
        MASKL = cst.t[:, 512:640]
        MD = cst.t[:, 640:640 + 2048].rearrange("p (a b) -> p a b", a=4)
        cbf = mk.sb("cbf", [128, 4 * 128], BF16)
        onesf = mk.sb("onesf", [128, 128])
        mk.op("dve", lambda e: e.tensor_copy(out=cbf.t[:, 0:128], in_=ident), reads=[cst], writes=[cbf])
        mk.op("dve", lambda e: e.memset(cbf.t[:, 128:256], 1.0), writes=[cbf])
        mk.op("dve", lambda e: e.tensor_copy(out=cbf.t[:, 256:384], in_=SUS), reads=[cst], writes=[cbf])
        mk.op("dve", lambda e: e.tensor_tensor(out=cbf.t[:, 384:512], in0=SUS, in1=ident, op=ALU.add), reads=[cst], writes=[cbf])
        mk.op("dve", lambda e: e.memset(onesf.t[:], 1.0), writes=[onesf])
        mkb = mk.sb("mkb", [128, 14, 128], BF16)
        with mk.phase():
            mkf = mk.sb("mkf", [128, 14 * 128])
            mk.dma("sp", lambda e: e.dma_start(out=mkf.t[:], in_=I["mks"]), mkf, writes=[mkf])
            mk.op("dve", lambda e: e.tensor_copy(out=mkb.t[:].rearrange("p a b -> p (a b)"), in_=mkf.t[:]), reads=[mkf], writes=[mkb])
        IDB = cbf.t[:, 0:128]
        ONESB = cbf.t[:, 128:256]
        SUSB = cbf.t[:, 256:384]
        TRIB = cbf.t[:, 384:512]
        gpreT = mk.sb("gpreT", [128, 32])
        cwT = mk.sb("cwT", [128, 36, 4])
        scwT = mk.sb("scwT", [128, 8, 3])
        negA = mk.sb("negA", [128, H])
        dtbb = mk.sb("dtbb", [128, H])
        gnT = mk.sb("gnT", [128, 1])

        def load_params(l):
            mk.dma("sp", lambda e: e.dma_start(out=gpreT.t[:], in_=I["norm_pre"][l].rearrange("(c p) -> p c", p=128)), gpreT, writes=[gpreT])
            for i in range(4):
                mk.dma("sp", lambda e, i=i: e.dma_start(out=cwT.t[:, :, i], in_=I["gcw"][l, i].rearrange("(c p) -> p c", p=128)), cwT, writes=[cwT])
            for i in range(3):
                mk.dma("sp", lambda e, i=i: e.dma_start(out=scwT.t[:, :, i], in_=I["scw"][l, i].rearrange("(c p) -> p c", p=128)), scwT, writes=[scwT])
            mk.dma("sp", lambda e: e.dma_start(out=negA.t[:], in_=I["alog"][l:l + 1, :].to_broadcast([128, H])), negA, writes=[negA])
            mk.dma("sp", lambda e: e.dma_start(out=dtbb.t[:], in_=I["dtb"][l:l + 1, :].to_broadcast([128, H])), dtbb, writes=[dtbb])
            mk.dma("sp", lambda e: e.dma_start(out=gnT.t[:], in_=I["gnorm"][l].rearrange("(p o) -> p o", o=1)), gnT, writes=[gnT])
            mk.op("act", lambda e: e.activation(out=negA.t[:], in_=negA.t[:], func=AF.Exp), reads=[negA], writes=[negA])
            mk.op("dve", lambda e: e.tensor_scalar(out=negA.t[:], in0=negA.t[:], scalar1=-1.0, scalar2=None, op0=ALU.mult), reads=[negA], writes=[negA])

        def precast(l):
            with mk.phase():
                wf = Rot([mk.sb("wf%d" % i, [128, 8, 512]) for i in range(3)])
                wb = Rot([mk.sb("wb%d" % i, [128, 8, 512], BF16) for i in range(3)])
                n = 0
                for s in range(32):
                    c0 = sblk_col(s)
                    for kg in range(4):
                        f = wf.next()
                        b = wb.next()
                        src = I["w_in"][l, kg * 1024:(kg + 1) * 1024, c0:c0 + 512].rearrange("(kc p) n -> p kc n", p=128)
                        mk.dma("sp", lambda e, f=f, src=src: e.dma_start(out=f.t[:], in_=src), f, writes=[f])
                        eng = "dve" if n % 2 == 0 else "pool"
                        n += 1
                        gsl = gpreT.t[:, kg * 8:(kg + 1) * 8].unsqueeze(2).to_broadcast([128, 8, 512])
                        mk.op(eng, lambda e, f=f, b=b, gsl=gsl: e.tensor_tensor(out=b.t[:], in0=f.t[:], in1=gsl, op=ALU.mult), reads=[f, gpreT], writes=[b])
                        dst = wbf[s, :, kg * 8:(kg + 1) * 8, :]
                        mk.dma("act", lambda e, b=b, dst=dst: e.dma_start(out=dst, in_=b.t[:]), b, reads=[b], writes=[Dw])
                fab = mk.sb("fab", [128, 32, 24])
                bab = mk.sb("bab", [128, 32, 24], BF16)
                mk.dma("sp", lambda e: e.dma_start(out=fab.t[:], in_=I["w_in"][l, :, 6144:6168].rearrange("(kc p) n -> p kc n", p=128)), fab, writes=[fab])
                mk.op("dve", lambda e: e.tensor_tensor(out=bab.t[:], in0=fab.t[:], in1=gpreT.t[:].unsqueeze(2).to_broadcast([128, 32, 24]), op=ALU.mult), reads=[fab, gpreT], writes=[bab])
                mk.dma("sp", lambda e: e.dma_start(out=wab, in_=bab.t[:]), bab, reads=[bab], writes=[Dw])
                for nb in range(8):
                    for kg in range(4):
                        f = wf.next()
                        b = wb.next()
                        src = I["w_out"][l, kg * 1024:(kg + 1) * 1024, nb * 512:(nb + 1) * 512].rearrange("(kc p) n -> p kc n", p=128)
                        mk.dma("sp", lambda e, f=f, src=src: e.dma_start(out=f.t[:], in_=src), f, writes=[f])
                        eng = "dve" if n % 2 == 0 else "pool"
                        n += 1
                        mk.op(eng, lambda e, f=f, b=b: e.tensor_copy(out=b.t[:], in_=f.t[:]), reads=[f], writes=[b])
                        dst = wobf[nb, :, kg * 8:(kg + 1) * 8, :]
                        mk.dma("act", lambda e, b=b, dst=dst: e.dma_start(out=dst, in_=b.t[:]), b, reads=[b], writes=[Dwo])

        SECT = ([("qkv", s) for s in range(9)] + [("az", s) for s in range(9, 12)] + [("ab", None)] +
                [("bq", s) for s in range(12, 15)] + [("bk", s) for s in range(15, 18)] + [("bv", s) for s in range(18, 21)] +
                [("bz", s) for s in range(21, 24)] +
                [("cc", 26), ("ch", 28), ("cb", 24), ("cz", 30), ("cc", 27), ("ch", 29), ("cb", 25), ("cz", 31)])
        SECBASE = dict(qkv=0, az=9, bq=12, bk=15, bv=18, bz=21, cb=24, cc=26, ch=28, cz=30)

        def proj_phase(sfx, T, l, xin, Dxin):
            S = SCR[sfx]
            DSs = DS[sfx]
            TT = min(128, T)
            TS = min(512, T)
            NTT = TS // TT
            NST = T // TS
            with mk.phase():
                hT = mk.sb("hT", [128, 32, TS], BF16)
                Wt = Rot([mk.sb("Wt%d" % i, [128, 32, 512], BF16) for i in range(2)])
                Wab = mk.sb("Wab", [128, 32, 24], BF16)
                xt = Rot([mk.sb("xt%d" % i, [TT, D]) for i in range(2)])
                junk = mk.sb("junk", [TT, D], BF16)
                st1 = Rot([mk.sb("st1_%d" % i, [TT, 2]) for i in range(2)])
                pm = Rot([mk.ps("pm%d" % i, [128, 512]) for i in range(4)])
                ptr = Rot([mk.ps("ptr%d" % i, [128, 512]) for i in range(2)])
                pn = Rot([mk.ps("pn%d" % i, [128, 512]) for i in range(2)])
                xa = Rot([mk.sb("xa%d" % i, [128, TS + 3]) for i in range(2)])
                acc = Rot([mk.sb("acc%d" % i, [128, TS]) for i in range(2)])
                sil = Rot([mk.sb("sil%d" % i, [128, TS]) for i in range(2)])
                sqb = Rot([mk.sb("sqb%d" % i, [128, TS], BF16) for i in range(2)])
                rr = Rot([mk.sb("rr%d" % i, [128, TS]) for i in range(2)])
                snf = Rot([mk.sb("snf%d" % i, [128, TS]) for i in range(2)])
                obf = Rot([mk.sb("obf%d" % i, [128, TS], BF16) for i in range(3)])
                tokb = Rot([mk.sb("tokb%d" % i, [TT, NTT, 128], BF16) for i in range(2)])
                kvst = Rot([mk.sb("kvst%d" % i, [TT, 512]) for i in range(2)])
                abt = Rot([mk.sb("abt%d" % i, [TT, 24]) for i in range(2)])
                abo = Rot([mk.sb("abo%d" % i, [TT, 24]) for i in range(2)])
                carry = mk.sb("carry", [128, 36, 3])
                sccarry = mk.sb("sccarry", [128, 8, 2])
                scU = mk.sb("scU", [128, 4, TS + 2])
                scV = mk.sb("scV", [128, 4, TS])
                if sfx == "p":
                    mk.op("dve", lambda e: e.memset(carry.t[:], 0.0), writes=[carry])
                    mk.op("dve", lambda e: e.memset(sccarry.t[:], 0.0), writes=[sccarry])
                else:
                    for t in range(3):
                        mk.dma("sp", lambda e, t=t: e.dma_start(out=carry.t[:, :, t], in_=I["sgc"][l, t].rearrange("(c p) -> p c", p=128)), carry, writes=[carry])
                    for t in range(2):
                        mk.dma("sp", lambda e, t=t: e.dma_start(out=sccarry.t[:, :, t], in_=I["ssc"][l, t].rearrange("(c p) -> p c", p=128)), sccarry, writes=[sccarry])
                mk.dma("sp", lambda e: e.dma_start(out=Wab.t[:], in_=wab), Wab, reads=[Dw], writes=[Wab])
                nev = [0]

                def evac_eng():
                    nev[0] += 1
                    return "act" if nev[0] % 2 else "dve"

                def copy_op(eng, out, in_, reads, writes, scale=None):
                    if eng == "act":
                        if scale is None:
                            mk.op("act", lambda e: e.activation(out=out, in_=in_, func=AF.Copy), reads=reads, writes=writes)
                        else:
                            mk.op("act", lambda e: e.activation(out=out, in_=in_, func=AF.Copy, scale=scale), reads=reads, writes=writes)
                    else:
                        if scale is None:
                            mk.op("dve", lambda e: e.tensor_copy(out=out, in_=in_), reads=reads, writes=writes)
                        else:
                            mk.op("dve", lambda e: e.tensor_scalar(out=out, in0=in_, scalar1=scale, scalar2=None, op0=ALU.mult), reads=reads, writes=writes)

                for st in range(NST):
                    t0 = st * TS
                    for tt in range(NTT):
                        x = xt.next()
                        s1 = st1.next()
                        r0 = t0 + tt * TT
                        mk.dma("sp", lambda e, x=x, r0=r0: e.dma_start(out=x.t[:], in_=xin[r0:r0 + TT, :]), x, reads=[Dxin], writes=[x])
                        mk.op("act", lambda e, x=x, s1=s1: e.activation(out=junk.t[:], in_=x.t[:], func=AF.Square, accum_out=s1.t[:, 0:1]), reads=[x], writes=[junk, s1])
                        mk.op("act", lambda e, s1=s1: e.activation(out=s1.t[:, 1:2], in_=s1.t[:, 0:1], func=AF.Sqrt, scale=1.0 / D, bias=EPS), reads=[s1], writes=[s1])
                        mk.op("dve", lambda e, s1=s1: e.reciprocal(out=s1.t[:, 1:2], in_=s1.t[:, 1:2]), reads=[s1], writes=[s1])
                        xn = x
                        mk.op("dve", lambda e, x=x, s1=s1: e.tensor_scalar(out=x.t[:], in0=x.t[:], scalar1=s1.t[:, 1:2], scalar2=None, op0=ALU.mult), reads=[x, s1], writes=[x])
                        for k4 in range(8):
                            p = ptr.next()
                            for j in range(4):
                                kc = k4 * 4 + j
                                mk.op("pe", lambda e, p=p, j=j, kc=kc: e.transpose(out=p.t[:, j * TT:(j + 1) * TT], in_=xn.t[:TT, kc * 128:(kc + 1) * 128], identity=ident[:TT, :TT]), reads=[xn, cst], writes=[p])
                            copy_op(evac_eng(), hT.t[:, k4 * 4:(k4 + 1) * 4, tt * TT:(tt + 1) * TT], p.t[:, 0:4 * TT].rearrange("p (a b) -> p a b", a=4), [p], [hT])
                    for (sec, s) in SECT:
                        if sec == "ab":
                            for tt in range(NTT):
                                p = pm.next()
                                for kc in range(32):
                                    mk.op("pe", lambda e, p=p, kc=kc, tt=tt: e.matmul(p.t[:TT, 0:24], lhsT=hT.t[:, kc, tt * TT:(tt + 1) * TT], rhs=Wab.t[:, kc, :], start=(kc == 0), stop=(kc == 31)), reads=[hT, Wab], writes=[p])
                                a1 = abt.next()
                                ao = abo.next()
                                mk.op("dve", lambda e, p=p, a1=a1: e.tensor_tensor(out=a1.t[:, 0:12], in0=p.t[:TT, 0:12], in1=dtbb.t[:TT, :], op=ALU.add), reads=[p, dtbb], writes=[a1])
                                mk.op("act", lambda e, a1=a1: e.activation(out=a1.t[:, 0:12], in_=a1.t[:, 0:12], func=AF.Exp), reads=[a1], writes=[a1])
                                mk.op("act", lambda e, a1=a1: e.activation(out=a1.t[:, 0:12], in_=a1.t[:, 0:12], func=AF.Ln, bias=1.0), reads=[a1], writes=[a1])
                                mk.op("dve", lambda e, a1=a1, ao=ao: e.tensor_tensor(out=ao.t[:, 0:12], in0=a1.t[:, 0:12], in1=negA.t[:TT, :], op=ALU.mult), reads=[a1, negA], writes=[ao])
                                mk.op("act", lambda e, p=p, ao=ao: e.activation(out=ao.t[:, 12:24], in_=p.t[:TT, 12:24], func=AF.Sigmoid), reads=[p, ao], writes=[ao])
                                r0 = t0 + tt * TT
                                mk.dma("pool", lambda e, ao=ao, r0=r0: e.dma_start(out=S["gbt"][r0:r0 + TT, :], in_=ao.t[:]), ao, reads=[ao], writes=[DSs["gbt"]])
                            continue
                        W = Wt.next()
                        mk.dma("sp", lambda e, W=W, s=s: e.dma_start(out=W.t[:], in_=wbf[s]), W, reads=[Dw], writes=[W])
                        if sec in ("bk", "bv"):
                            okv = O[("k" if sec == "bk" else "v") + sfx]
                            c0 = (s - SECBASE[sec]) * 512
                            for tt in range(NTT):
                                p = pm.next()
                                for kc in range(32):
                                    mk.op("pe", lambda e, p=p, kc=kc, tt=tt, W=W: e.matmul(p.t[:TT, :], lhsT=hT.t[:, kc, tt * TT:(tt + 1) * TT], rhs=W.t[:, kc, :], start=(kc == 0), stop=(kc == 31)), reads=[hT, W], writes=[p])
                                kv = kvst.next()
                                copy_op(evac_eng(), kv.t[:], p.t[:TT, :], [p], [kv])
                                r0 = t0 + tt * TT
                                mk.dma("pool", lambda e, kv=kv, r0=r0, c0=c0, okv=okv: e.dma_start(out=okv[l, r0:r0 + TT, c0:c0 + 512], in_=kv.t[:]), kv, reads=[kv], writes=[DKV[sfx]])
                            continue
                        for c in range(4):
                            ci = (s - SECBASE[sec]) * 4 + c
                            p = pm.next()
                            for kc in range(32):
                                mk.op("pe", lambda e, p=p, kc=kc, c=c, W=W: e.matmul(p.t[:, 0:TS], lhsT=W.t[:, kc, c * 128:(c + 1) * 128], rhs=hT.t[:, kc, :], start=(kc == 0), stop=(kc == 31)), reads=[hT, W], writes=[p])
                            P = p.t[:, 0:TS]
                            if sec == "qkv":
                                a = xa.next()
                                mk.op("act", lambda e, a=a, P=P: e.activation(out=a.t[:, 3:3 + TS], in_=P, func=AF.Copy), reads=[p], writes=[a])
                                mk.op("dve", lambda e, a=a, ci=ci: e.tensor_copy(out=a.t[:, 0:3], in_=carry.t[:, ci, :]), reads=[carry, a], writes=[a])
                                ac = acc.next()
                                mk.op("dve", lambda e, a=a, ac=ac, ci=ci: e.tensor_scalar(out=ac.t[:], in0=a.t[:, 0:TS], scalar1=cwT.t[:, ci, 0:1], scalar2=None, op0=ALU.mult), reads=[a, cwT], writes=[ac])
                                for i in range(1, 4):
                                    mk.op("dve", lambda e, a=a, ac=ac, ci=ci, i=i: e.scalar_tensor_tensor(out=ac.t[:], in0=a.t[:, i:i + TS], scalar=cwT.t[:, ci, i:i + 1], in1=ac.t[:], op0=ALU.mult, op1=ALU.add), reads=[a, cwT, ac], writes=[ac])
                                mk.op("dve", lambda e, a=a, ci=ci: e.tensor_copy(out=carry.t[:, ci, :], in_=a.t[:, TS:TS + 3]), reads=[a, carry], writes=[carry])
                                sl = sil.next()
                                mk.op("act", lambda e, ac=ac, sl=sl: e.activation(out=sl.t[:], in_=ac.t[:], func=AF.Silu), reads=[ac], writes=[sl])
                                kind = ci // 12
                                hh = ci % 12
                                if kind < 2:
                                    sq = sqb.next()
                                    mk.op("dve", lambda e, sl=sl, sq=sq: e.tensor_tensor(out=sq.t[:], in0=sl.t[:], in1=sl.t[:], op=ALU.mult), reads=[sl], writes=[sq])
                                    pp = pn.next()
                                    mk.op("pe", lambda e, pp=pp, sq=sq: e.matmul(pp.t[:, 0:TS], lhsT=ONESB, rhs=sq.t[:], start=True, stop=True), reads=[cbf, sq], writes=[pp])
                                    r = rr.next()
                                    scl = 128.0 if kind == 0 else 1.0
                                    mk.op("act", lambda e, pp=pp, r=r, scl=scl: e.activation(out=r.t[:], in_=pp.t[:, 0:TS], func=AF.Sqrt, scale=scl, bias=EPS * scl), reads=[pp], writes=[r])
                                    mk.op("dve", lambda e, r=r: e.reciprocal(out=r.t[:], in_=r.t[:]), reads=[r], writes=[r])
                                    ob = obf.next()
                                    mk.op("dve", lambda e, sl=sl, r=r, ob=ob: e.tensor_tensor(out=ob.t[:], in0=sl.t[:], in1=r.t[:], op=ALU.mult), reads=[sl, r], writes=[ob])
                                    dstT = (S["gqT"] if kind == 0 else S["gkT"])[hh, :, t0:t0 + TS]
                                    mk.dma("pool", lambda e, ob=ob, dstT=dstT: e.dma_start(out=dstT, in_=ob.t[:]), ob, reads=[ob], writes=[DSs["gqT" if kind == 0 else "gkT"]])
                                    if kind == 1:
                                        sn = snf.next()
                                        mk.op("dve", lambda e, sl=sl, r=r, sn=sn: e.tensor_tensor(out=sn.t[:], in0=sl.t[:], in1=r.t[:], op=ALU.mult), reads=[sl, r], writes=[sn])
                                        src_f = sn
                                else:
                                    src_f = sl
                                if kind >= 1:
                                    pt_ = ptr.next()
                                    for j in range(NTT):
                                        mk.op("pe", lambda e, pt_=pt_, j=j, src_f=src_f: e.transpose(out=pt_.t[:TT, j * 128:(j + 1) * 128], in_=src_f.t[:, j * TT:(j + 1) * TT], identity=ident), reads=[src_f, cst], writes=[pt_])
                                    tb = tokb.next()
                                    copy_op(evac_eng(), tb.t[:], pt_.t[:TT, 0:NTT * 128].rearrange("p (a b) -> p a b", a=NTT), [pt_], [tb])
                                    dtok = (S["gk"] if kind == 1 else S["gv"])[t0:t0 + TS, hh * 128:(hh + 1) * 128].rearrange("(j p) d -> p j d", p=TT)
                                    mk.dma("pool", lambda e, tb=tb, dtok=dtok: e.dma_start(out=dtok, in_=tb.t[:]), tb, reads=[tb], writes=[DSs["gk" if kind == 1 else "gv"]])
                            elif sec == "az":
                                sl = sil.next()
                                mk.op("act", lambda e, sl=sl, P=P: e.activation(out=sl.t[:], in_=P, func=AF.Silu), reads=[p], writes=[sl])
                                ob = obf.next()
                                mk.op("dve", lambda e, sl=sl, ob=ob: e.tensor_scalar(out=ob.t[:], in0=sl.t[:], scalar1=gnT.t[:, 0:1], scalar2=None, op0=ALU.mult), reads=[sl, gnT], writes=[ob])
                                mk.dma("pool", lambda e, ob=ob, ci=ci: e.dma_start(out=S["azT"][ci, :, t0:t0 + TS], in_=ob.t[:]), ob, reads=[ob], writes=[DSs["azT"]])
                            elif sec == "bq":
                                ob = obf.next()
                                copy_op(evac_eng(), ob.t[:], P, [p], [ob], scale=HD ** -0.5)
                                mk.dma("pool", lambda e, ob=ob, ci=ci: e.dma_start(out=S["QT"][ci, :, t0:t0 + TS], in_=ob.t[:]), ob, reads=[ob], writes=[DSs["QT"]])
                            elif sec == "bz":
                                ob = obf.next()
                                mk.op("act", lambda e, ob=ob, P=P: e.activation(out=ob.t[:], in_=P, func=AF.Silu), reads=[p], writes=[ob])
                                mk.dma("pool", lambda e, ob=ob, ci=ci: e.dma_start(out=S["bzT"][ci, :, t0:t0 + TS], in_=ob.t[:]), ob, reads=[ob], writes=[DSs["bzT"]])
                            elif sec == "cc":
                                mk.op("act", lambda e, c=c, P=P: e.activation(out=scU.t[:, c, 2:2 + TS], in_=P, func=AF.Copy), reads=[p], writes=[scU])
                            elif sec == "ch":
                                mk.op("dve", lambda e, c=c, P=P: e.tensor_tensor(out=scU.t[:, c, 2:2 + TS], in0=P, in1=scU.t[:, c, 2:2 + TS], op=ALU.mult), reads=[p, scU], writes=[scU])
                                mk.op("dve", lambda e, c=c, ci=ci: e.tensor_copy(out=scU.t[:, c, 0:2], in_=sccarry.t[:, ci, :]), reads=[sccarry, scU], writes=[scU])
                                mk.op("dve", lambda e, c=c, ci=ci: e.tensor_scalar(out=scV.t[:, c, :], in0=scU.t[:, c, 0:TS], scalar1=scwT.t[:, ci, 0:1], scalar2=None, op0=ALU.mult), reads=[scU, scwT], writes=[scV])
                                for i in range(1, 3):
                                    mk.op("dve", lambda e, c=c, ci=ci, i=i: e.scalar_tensor_tensor(out=scV.t[:, c, :], in0=scU.t[:, c, i:i + TS], scalar=scwT.t[:, ci, i:i + 1], in1=scV.t[:, c, :], op0=ALU.mult, op1=ALU.add), reads=[scU, scwT, scV], writes=[scV])
                                mk.op("dve", lambda e, c=c, ci=ci: e.tensor_copy(out=sccarry.t[:, ci, :], in_=scU.t[:, c, TS:TS + 2]), reads=[scU, sccarry], writes=[sccarry])
                            elif sec == "cb":
                                mk.op("dve", lambda e, c=c, P=P: e.tensor_tensor(out=scV.t[:, c, :], in0=P, in1=scV.t[:, c, :], op=ALU.mult), reads=[p, scV], writes=[scV])
                            elif sec == "cz":
                                sl = sil.next()
                                mk.op("act", lambda e, sl=sl, P=P: e.activation(out=sl.t[:], in_=P, func=AF.Silu), reads=[p], writes=[sl])
                                ob = obf.next()
                                mk.op("dve", lambda e, sl=sl, ob=ob, c=c: e.tensor_tensor(out=ob.t[:], in0=sl.t[:], in1=scV.t[:, c, :], op=ALU.mult), reads=[sl, scV], writes=[ob])
                                mk.dma("pool", lambda e, ob=ob, ci=ci: e.dma_start(out=S["mixT"][24 + ci, :, t0:t0 + TS], in_=ob.t[:]), ob, reads=[ob], writes=[DSs["mixT"]])
                for t in range(3):
                    mk.dma("pool", lambda e, t=t: e.dma_start(out=O["gc" + sfx][l, t].rearrange("(c p) -> p c", p=128), in_=carry.t[:, :, t]), carry, reads=[carry])
                for t in range(2):
                    mk.dma("pool", lambda e, t=t: e.dma_start(out=O["sc" + sfx][l, t].rearrange("(c p) -> p c", p=128), in_=sccarry.t[:, :, t]), sccarry, reads=[sccarry])

        def gdn_phase(sfx, T, l):
            S = SCR[sfx]
            DSs = DS[sfx]
            C = min(128, T)
            NCH = T // C
            NLEV = 6 if C == 128 else 4
            W3 = H * C
            with mk.phase():
                mk.limit = getattr(mk, "nops", 0) + DEBUG_OPS
                Sst = mk.sb("Sst", [128, H, 128])
                Sbf = mk.sb("Sbf", [128, H, 128], BF16)
                if sfx == "p":
                    mk.op("dve", lambda e: e.memset(Sst.t[:], 0.0), writes=[Sst])
                else:
                    mk.dma("sp", lambda e: e.dma_start(out=Sst.t[:], in_=I["sg"][l].rearrange("h k v -> k h v")), Sst, writes=[Sst])
                mk.op("act", lambda e: e.activation(out=Sbf.t[:], in_=Sst.t[:], func=AF.Copy), reads=[Sst], writes=[Sbf])
                NB = 2
                qT = Rot([mk.sb("qT%d" % i, [128, H, C], BF16) for i in range(NB)])
                kT = Rot([mk.sb("kT%d" % i, [128, H, C], BF16) for i in range(NB)])
                ktok = Rot([mk.sb("ktok%d" % i, [C, H, 128], BF16) for i in range(NB)])
                vtok = Rot([mk.sb("vtok%d" % i, [C, H, 128], BF16) for i in range(NB)])
                gbt = Rot([mk.sb("gbt%d" % i, [C, 24]) for i in range(NB)])
                azt = Rot([mk.sb("azt%d" % i, [128, H, C], BF16) for i in range(NB)])
                pA = [mk.ps("pA%d" % i, [128, 512]) for i in range(3)]
                pB = [mk.ps("pB%d" % i, [128, 512]) for i in range(3)]
                pC = mk.ps("pC", [128, 512])
                pD = mk.ps("pD", [128, 512])
                sm = mk.sb("sm", [128, 8, H])
                X2 = mk.sb("X2", [C, H, C])
                D0 = mk.sb("D0", [C, H, C])
                DU = mk.sb("DU", [C, H, C])
                DL = mk.sb("DL", [C, H, C])
                EGB = mk.sb("EGB", [128, H, C], BF16)
                BBs = mk.sb("BBs", [C, H, C])
                tmp = mk.sb("tmp", [C, H, C])
                X1 = tmp
                Mx = [mk.sb("Mx%d" % i, [C, H, C]) for i in range(1)]
                MTx = [mk.sb("MTx%d" % i, [C, H, C]) for i in range(1)]
                Pm = [mk.sb("Pm%d" % i, [C, H, C], BF16) for i in range(2)]
                Rm = [mk.sb("Rm%d" % i, [C, H, C], BF16) for i in range(2)]
                MmB = Rot([mk.sb("MmB%d" % i, [C, H, C], BF16) for i in range(2)])
                MTmB = Rot([mk.sb("MTmB%d" % i, [C, H, C], BF16) for i in range(2)])
                EsB = mk.sb("EsB", [C, H, C], BF16)
                EpsB = mk.sb("EpsB", [C, H, C], BF16)

                attnT = mk.sb("attnT", [C, H, C], BF16)
                vb = mk.sb("vb", [C, H, 128], BF16)
                kbg = mk.sb("kbg", [C, H, 128], BF16)
                kdec = mk.sb("kdec", [C, H, 128], BF16)
                usb = mk.sb("usb", [C, H, 128])
                wT = mk.sb("wT", [128, H, C], BF16)
                qdT = mk.sb("qdT", [128, H, C], BF16)
                vnew = mk.sb("vnew", [C, H, 128], BF16)
                sqo = mk.sb("sqo", [128, H, C], BF16)
                rno = D0 if C == 128 else mk.sb("rno", [128, H, C])
                yo = DL if C == 128 else mk.sb("yo", [128, H, C])
                yb = Rot([mk.sb("yb%d" % i, [128, H, C], BF16) for i in range(2)])
                f2 = lambda ap: ap.rearrange("p a b -> p (a b)")
                HG = min(H, 512 // C)
                HP = [(h0, min(H, h0 + HG)) for h0 in range(0, H, HG)]
                HP128 = [(h0, h0 + 4) for h0 in range(0, H, 4)]

                def BK(ps, h, w):
                    return ps[(h * w) // 512]

                def PW(ps, rows, h, w):
                    o = (h * w) % 512
                    return ps[(h * w) // 512].t[:rows, o:o + w]

                def P3(ps, rows, h0, h1, w):
                    o = (h0 * w) % 512
                    return ps[(h0 * w) // 512].t[:rows, o:o + (h1 - h0) * w].rearrange("p (a b) -> p a b", a=h1 - h0)

                def bch(ap2, h0, h1, n):
                    return ap2[:, h0:h1].unsqueeze(2).to_broadcast([ap2.shape[0], h1 - h0, n])

                def mb(m2, h0, h1):
                    return m2.unsqueeze(1).to_broadcast([m2.shape[0], h1 - h0, m2.shape[1]])

                def bc(ap2, n):
                    return ap2.unsqueeze(2).to_broadcast([ap2.shape[0], H, n])

                def mm_banks(dst, lhsT, rhs_buf, rhs2, width, reads):
                    for c0 in range(0, width, 512):
                        c1 = min(width, c0 + 512)
                        mk.op("pe", lambda e, c0=c0, c1=c1: e.matmul(dst[:, c0:c1], lhsT=lhsT, rhs=rhs2[:, c0:c1], start=True, stop=True), reads=reads, writes=[rhs_buf[1]])

                for c in range(NCH):
                    t0 = c * C
                    q_ = qT.next(); k_ = kT.next(); kt = ktok.next(); vt = vtok.next(); gb = gbt.next(); az = azt.next()
                    mk.dma("sp", lambda e, q_=q_: e.dma_start(out=q_.t[:], in_=S["gqT"][:, :, t0:t0 + C].rearrange("h d t -> d h t")), q_, reads=[DSs["gqT"]], writes=[q_])
                    mk.dma("sp", lambda e, k_=k_: e.dma_start(out=k_.t[:], in_=S["gkT"][:, :, t0:t0 + C].rearrange("h d t -> d h t")), k_, reads=[DSs["gkT"]], writes=[k_])
                    mk.dma("sp", lambda e, kt=kt: e.dma_start(out=kt.t[:].rearrange("p a b -> p (a b)"), in_=S["gk"][t0:t0 + C, :]), kt, reads=[DSs["gk"]], writes=[kt])
                    mk.dma("sp", lambda e, vt=vt: e.dma_start(out=vt.t[:].rearrange("p a b -> p (a b)"), in_=S["gv"][t0:t0 + C, :]), vt, reads=[DSs["gv"]], writes=[vt])
                    mk.dma("sp", lambda e, gb=gb: e.dma_start(out=gb.t[:], in_=S["gbt"][t0:t0 + C, :]), gb, reads=[DSs["gbt"]], writes=[gb])
                    mk.dma("sp", lambda e, az=az: e.dma_start(out=az.t[:], in_=S["azT"][:, :, t0:t0 + C].rearrange("h d t -> d h t")), az, reads=[DSs["azT"]], writes=[az])
                    G = gb.t[:, 0:12]
                    Bt = gb.t[:, 12:24]
                    gcum = sm.t[:C, 0, :]; eg = sm.t[:C, 1, :]; egl = sm.t[:C, 2, :]; bk = sm.t[:C, 3, :]; gl = sm.t[:, 4, :]
                    mk.op("pe", lambda e: e.matmul(pC.t[:C, 0:12], lhsT=UINC[:C, :C], rhs=G, start=True, stop=True), reads=[cst, gb], writes=[pC])
                    mk.op("pe", lambda e: e.matmul(pD.t[:, 0:12], lhsT=onesf.t[:C, :], rhs=G, start=True, stop=True), reads=[onesf, gb], writes=[pD])
                    mk.op("dve", lambda e: e.tensor_copy(out=gcum, in_=pC.t[:C, 0:12]), reads=[pC], writes=[sm])
                    mk.op("act", lambda e: e.activation(out=eg, in_=pC.t[:C, 0:12], func=AF.Exp), reads=[pC], writes=[sm])
                    mk.op("act", lambda e: e.activation(out=gl, in_=pD.t[:, 0:12], func=AF.Exp), reads=[pD], writes=[sm])
                    mk.op("dve", lambda e: e.tensor_tensor(out=egl, in0=pD.t[:C, 0:12], in1=gcum, op=ALU.subtract), reads=[pD, sm], writes=[sm])
                    mk.op("act", lambda e: e.activation(out=egl, in_=egl, func=AF.Exp), reads=[sm], writes=[sm])
                    mk.op("dve", lambda e: e.tensor_tensor(out=bk, in0=Bt, in1=eg, op=ALU.mult), reads=[gb, sm], writes=[sm])
                    mk.op("dve", lambda e: e.tensor_tensor(out=X1.t[:], in0=bc(G, C), in1=UINC[:C, :C].unsqueeze(1).to_broadcast([C, H, C]), op=ALU.mult), reads=[gb, cst], writes=[X1])
                    for c0 in range(0, W3, 512):
                        c1 = min(W3, c0 + 512)
                        mk.op("pe", lambda e, c0=c0, c1=c1: e.matmul(pA[c0 // 512].t[:, 0:c1 - c0], lhsT=onesf.t[:C, :], rhs=f2(X1.t[:])[:, c0:c1], start=True, stop=True), reads=[onesf, X1], writes=[pA[c0 // 512]])
                    for (h0, h1) in HP:
                        mk.op("dve", lambda e, h0=h0, h1=h1: e.tensor_tensor(out=D0.t[:, h0:h1, :], in0=P3(pA, C, h0, h1, C), in1=bch(gcum, h0, h1, C), op=ALU.subtract), reads=[BK(pA, h0, C), sm], writes=[D0])
                        mk.op("act", lambda e, h0=h0, h1=h1: e.activation(out=EGB.t[:, h0:h1, :], in_=P3(pA, 128, h0, h1, C), func=AF.Exp), reads=[BK(pA, h0, C), D0], writes=[EGB])
                    mk.op("pool", lambda e: e.tensor_tensor(out=DU.t[:], in0=D0.t[:], in1=MASKU[:C, :C].unsqueeze(1).to_broadcast([C, H, C]), op=ALU.add), reads=[D0, cst], writes=[DU])
                    mk.op("dve", lambda e: e.scalar_tensor_tensor(out=DL.t[:], in0=D0.t[:], scalar=-1.0, in1=MASKL[:C, :C].unsqueeze(1).to_broadcast([C, H, C]), op0=ALU.mult, op1=ALU.add), reads=[D0, cst], writes=[DL])
                    mk.op("act", lambda e: e.activation(out=f2(DU.t[:]), in_=f2(DU.t[:]), func=AF.Exp), reads=[DU], writes=[DU])
                    mk.op("act", lambda e: e.activation(out=f2(DL.t[:]), in_=f2(DL.t[:]), func=AF.Exp), reads=[DL], writes=[DL])
                    mk.op("pool", lambda e: e.tensor_tensor(out=X2.t[:], in0=bc(Bt, C), in1=ident[:C, :C].unsqueeze(1).to_broadcast([C, H, C]), op=ALU.mult), reads=[gb, cst], writes=[X2])
                    for c0 in range(0, W3, 512):
                        c1 = min(W3, c0 + 512)
                        mk.op("pe", lambda e, c0=c0, c1=c1: e.matmul(pB[c0 // 512].t[:C, 0:c1 - c0], lhsT=SUS[:C, :C], rhs=f2(X2.t[:])[:, c0:c1], start=True, stop=True), reads=[cst, X2], writes=[pB[c0 // 512]])
                    for (h0, h1) in HP:
                        mk.op("act", lambda e, h0=h0, h1=h1: e.activation(out=BBs.t[:, h0:h1, :], in_=P3(pB, C, h0, h1, C), func=AF.Copy), reads=[BK(pB, h0, C)], writes=[BBs])
                    for h in range(H):
                        mk.op("pe", lambda e, h=h: e.matmul(PW(pA, C, h, C), lhsT=k_.t[:, h, :], rhs=k_.t[:, h, :], start=True, stop=True), reads=[k_], writes=[BK(pA, h, C)])
                    for h in range(H):
                        mk.op("pe", lambda e, h=h: e.matmul(PW(pB, C, h, C), lhsT=k_.t[:, h, :], rhs=q_.t[:, h, :], start=True, stop=True), reads=[k_, q_], writes=[BK(pB, h, C)])
                    M0, MT0 = Mx[0], MTx[0]
                    for (h0, h1) in HP:
                        mk.op("dve", lambda e, h0=h0, h1=h1: e.tensor_tensor(out=tmp.t[:, h0:h1, :], in0=P3(pA, C, h0, h1, C), in1=DU.t[:, h0:h1, :], op=ALU.mult), reads=[BK(pA, h0, C), DU], writes=[tmp])
                    mk.op("dve", lambda e: e.scalar_tensor_tensor(out=f2(M0.t[:]), in0=f2(tmp.t[:]), scalar=-1.0, in1=f2(BBs.t[:]), op0=ALU.mult, op1=ALU.mult), reads=[tmp, BBs], writes=[M0])
                    for (h0, h1) in HP:
                        mk.op("dve", lambda e, h0=h0, h1=h1: e.tensor_tensor(out=tmp.t[:, h0:h1, :], in0=P3(pA, C, h0, h1, C), in1=DL.t[:, h0:h1, :], op=ALU.mult), reads=[BK(pA, h0, C), DL], writes=[tmp])
                    mk.op("dve", lambda e: e.scalar_tensor_tensor(out=MT0.t[:], in0=tmp.t[:], scalar=-1.0, in1=bc(Bt, C), op0=ALU.mult, op1=ALU.mult), reads=[tmp, gb], writes=[MT0])
                    for (h0, h1) in HP:
                        mk.op("dve", lambda e, h0=h0, h1=h1: e.tensor_tensor(out=attnT.t[:, h0:h1, :], in0=P3(pB, C, h0, h1, C), in1=DU.t[:, h0:h1, :], op=ALU.mult), reads=[BK(pB, h0, C), DU], writes=[attnT])
                    Es, Eps = EsB, EpsB
                    NL = 7 if C == 128 else 5
                    Q, R = Pm[0], Rm[0]
                    mk.op("pool", lambda e: e.tensor_tensor(out=Q.t[:], in0=M0.t[:], in1=mb(mkb.t[:C, 7, 0:C], 0, H), op=ALU.mult), reads=[M0, mkb], writes=[Q])
                    mk.op("pool", lambda e: e.tensor_tensor(out=Q.t[:], in0=Q.t[:], in1=mb(ident[:C, :C], 0, H), op=ALU.add), reads=[Q, cst], writes=[Q])
                    mk.op("dve", lambda e: e.tensor_tensor(out=R.t[:], in0=MT0.t[:], in1=mb(mkb.t[:C, 0, 0:C], 0, H), op=ALU.mult), reads=[MT0, mkb], writes=[R])
                    mk.op("dve", lambda e: e.tensor_tensor(out=R.t[:], in0=R.t[:], in1=mb(ident[:C, :C], 0, H), op=ALU.add), reads=[R, cst], writes=[R])
                    cur = 0
                    for lev in range(1, NL):
                        Q, R = Pm[cur], Rm[cur]
                        Qn, Rn = Pm[1 - cur], Rm[1 - cur]
                        Mm, MTm = MmB.next(), MTmB.next()
                        mk.op("pool", lambda e, lev=lev: e.tensor_tensor(out=Mm.t[:], in0=M0.t[:], in1=mb(mkb.t[:C, 7 + lev, 0:C], 0, H), op=ALU.mult), reads=[M0, mkb], writes=[Mm])
                        mk.op("pool", lambda e, lev=lev: e.tensor_tensor(out=MTm.t[:], in0=MT0.t[:], in1=mb(mkb.t[:C, lev, 0:C], 0, H), op=ALU.mult), reads=[MT0, mkb], writes=[MTm])
                        for h in range(H):
                            mk.op("pe", lambda e, h=h, Q=Q: e.matmul(PW(pA, C, h, C), lhsT=MTm.t[:, h, :], rhs=Q.t[:, h, :], start=True, stop=True), reads=[MTm, Q], writes=[BK(pA, h, C)])
                        for h in range(H):
                            mk.op("pe", lambda e, h=h, R=R: e.matmul(PW(pB, C, h, C), lhsT=Mm.t[:, h, :], rhs=R.t[:, h, :], start=True, stop=True), reads=[Mm, R], writes=[BK(pB, h, C)])
                        for (h0, h1) in HP:
                            mk.op("act", lambda e, h0=h0, h1=h1: e.activation(out=Es.t[:, h0:h1, :], in_=P3(pA, C, h0, h1, C), func=AF.Copy), reads=[BK(pA, h0, C)], writes=[Es])
                            mk.op("dve", lambda e, h0=h0, h1=h1: e.tensor_copy(out=Eps.t[:, h0:h1, :], in_=P3(pB, C, h0, h1, C)), reads=[BK(pB, h0, C)], writes=[Eps])
                        for h in range(H):
                            mk.op("pe", lambda e, h=h, R=R: e.matmul(PW(pA, C, h, C), lhsT=R.t[:, h, :], rhs=Es.t[:, h, :], start=True, stop=True), reads=[R, Es], writes=[BK(pA, h, C)])
                        for h in range(H):
                            mk.op("pe", lambda e, h=h, Q=Q: e.matmul(PW(pB, C, h, C), lhsT=Q.t[:, h, :], rhs=Eps.t[:, h, :], start=True, stop=True), reads=[Q, Eps], writes=[BK(pB, h, C)])
                        for (h0, h1) in HP:
                            mk.op("dve", lambda e, h0=h0, h1=h1, Q=Q, Qn=Qn: e.tensor_tensor(out=Qn.t[:, h0:h1, :], in0=P3(pA, C, h0, h1, C), in1=Q.t[:, h0:h1, :], op=ALU.add), reads=[BK(pA, h0, C), Q], writes=[Qn])
                            mk.op("dve", lambda e, h0=h0, h1=h1, R=R, Rn=Rn: e.tensor_tensor(out=Rn.t[:, h0:h1, :], in0=P3(pB, C, h0, h1, C), in1=R.t[:, h0:h1, :], op=ALU.add), reads=[BK(pB, h0, C), R], writes=[Rn])
                        cur = 1 - cur
                    PT = Pm[cur]
                    mk.op("pool", lambda e: e.tensor_tensor(out=vb.t[:], in0=vt.t[:], in1=bc(Bt, 128), op=ALU.mult), reads=[vt, gb], writes=[vb])
                    mk.op("pool", lambda e: e.tensor_tensor(out=kbg.t[:], in0=kt.t[:], in1=bc(bk, 128), op=ALU.mult), reads=[kt, sm], writes=[kbg])
                    mk.op("pool", lambda e: e.tensor_tensor(out=kdec.t[:], in0=kt.t[:], in1=bc(egl, 128), op=ALU.mult), reads=[kt, sm], writes=[kdec])
                    mk.op("pool", lambda e: e.tensor_tensor(out=qdT.t[:], in0=q_.t[:], in1=EGB.t[:], op=ALU.mult), reads=[q_, EGB], writes=[qdT])
                    for h in range(H):
                        mk.op("pe", lambda e, h=h: e.matmul(PW(pA, C, h, 128), lhsT=PT.t[:, h, :], rhs=vb.t[:, h, :], start=True, stop=True), reads=[PT, vb], writes=[BK(pA, h, 128)])
                    for h in range(H):
                        mk.op("pe", lambda e, h=h: e.matmul(PW(pB, 128, h, C), lhsT=kbg.t[:, h, :], rhs=PT.t[:, h, :], start=True, stop=True), reads=[PT, kbg], writes=[BK(pB, h, C)])
                    for (h0, h1) in HP128:
                        mk.op("act", lambda e, h0=h0, h1=h1: e.activation(out=usb.t[:, h0:h1, :], in_=P3(pA, C, h0, h1, 128), func=AF.Copy), reads=[BK(pA, h0, 128)], writes=[usb])
                    for (h0, h1) in HP:
                        mk.op("dve", lambda e, h0=h0, h1=h1: e.tensor_copy(out=wT.t[:, h0:h1, :], in_=P3(pB, 128, h0, h1, C)), reads=[BK(pB, h0, C)], writes=[wT])
                    for h in range(H):
                        mk.op("pe", lambda e, h=h: e.matmul(PW(pA, C, h, 128), lhsT=wT.t[:, h, :], rhs=Sbf.t[:, h, :], start=True, stop=True), reads=[wT, Sbf], writes=[BK(pA, h, 128)])
                    for (h0, h1) in HP128:
                        mk.op("dve", lambda e, h0=h0, h1=h1: e.tensor_tensor(out=vnew.t[:, h0:h1, :], in0=usb.t[:, h0:h1, :], in1=P3(pA, C, h0, h1, 128), op=ALU.subtract), reads=[usb, BK(pA, h0, 128)], writes=[vnew])
                    for h in range(H):
                        mk.op("pe", lambda e, h=h: e.matmul(PW(pB, 128, h, C), lhsT=Sbf.t[:, h, :], rhs=qdT.t[:, h, :], start=True, stop=False), reads=[Sbf, qdT], writes=[BK(pB, h, C)])
                        mk.op("pe", lambda e, h=h: e.matmul(PW(pB, 128, h, C), lhsT=vnew.t[:, h, :], rhs=attnT.t[:, h, :], start=False, stop=True), reads=[vnew, attnT], writes=[BK(pB, h, C)])
                    for h in range(H):
                        mk.op("pe", lambda e, h=h: e.matmul(PW(pA, 128, h, 128), lhsT=kdec.t[:, h, :], rhs=vnew.t[:, h, :], start=True, stop=True), reads=[kdec, vnew], writes=[BK(pA, h, 128)])
                    mk.op("dve", lambda e: e.tensor_tensor(out=Sst.t[:], in0=Sst.t[:], in1=sm.t[:, 4, :].unsqueeze(2).to_broadcast([128, H, 128]), op=ALU.mult), reads=[Sst, sm], writes=[Sst])
                    for (h0, h1) in HP128:
                        mk.op("dve", lambda e, h0=h0, h1=h1: e.tensor_tensor(out=Sst.t[:, h0:h1, :], in0=Sst.t[:, h0:h1, :], in1=P3(pA, 128, h0, h1, 128), op=ALU.add), reads=[Sst, BK(pA, h0, 128)], writes=[Sst])
                    mk.op("act", lambda e: e.activation(out=f2(Sbf.t[:]), in_=f2(Sst.t[:]), func=AF.Copy), reads=[Sst], writes=[Sbf])
                    for (h0, h1) in HP:
                        mk.op("act", lambda e, h0=h0, h1=h1: e.activation(out=sqo.t[:, h0:h1, :], in_=P3(pB, 128, h0, h1, C), func=AF.Square), reads=[BK(pB, h0, C)], writes=[sqo])
                    for c0 in range(0, W3, 512):
                        c1 = min(W3, c0 + 512)
                        mk.op("pe", lambda e, c0=c0, c1=c1: e.matmul(pA[c0 // 512].t[:, 0:c1 - c0], lhsT=ONESB, rhs=f2(sqo.t[:])[:, c0:c1], start=True, stop=True), reads=[cbf, sqo], writes=[pA[c0 // 512]])
                    for (h0, h1) in HP:
                        mk.op("act", lambda e, h0=h0, h1=h1: e.activation(out=rno.t[:, h0:h1, :], in_=P3(pA, 128, h0, h1, C), func=AF.Sqrt, scale=1.0 / 128, bias=EPS), reads=[BK(pA, h0, C)], writes=[rno])
                    mk.op("dve", lambda e: e.reciprocal(out=f2(rno.t[:]), in_=f2(rno.t[:])), reads=[rno], writes=[rno])
                    for (h0, h1) in HP:
                        mk.op("dve", lambda e, h0=h0, h1=h1: e.tensor_tensor(out=yo.t[:, h0:h1, :], in0=P3(pB, 128, h0, h1, C), in1=rno.t[:, h0:h1, :], op=ALU.mult), reads=[BK(pB, h0, C), rno], writes=[yo])
                    y_ = yb.next()
                    mk.op("pool", lambda e, y_=y_: e.tensor_tensor(out=y_.t[:], in0=yo.t[:], in1=az.t[:], op=ALU.mult), reads=[yo, az], writes=[y_])
                    mk.dma("pool", lambda e, y_=y_: e.dma_start(out=S["mixT"][0:12, :, t0:t0 + C].rearrange("h d t -> d h t"), in_=y_.t[:]), y_, reads=[y_], writes=[DSs["mixT"]])
                mk.dma("pool", lambda e: e.dma_start(out=O["g" + sfx][l].rearrange("h k v -> k h v"), in_=Sst.t[:]), Sst, reads=[Sst])

        def sb_phase(sfx, T, l):
            S = SCR[sfx]
            DSs = DS[sfx]
            KB = min(128, T)
            NQ_ = min(512, T)
            NQT = T // NQ_
            NNB = T // KB
            npast = NPB if sfx == "s" else 0
            NBT = npast + NNB
            with mk.phase():
                QTh = Rot([mk.sb("QTh%d" % i, [128, T], BF16) for i in range(2)])
                bzh = Rot([mk.sb("bzh%d" % i, [128, T], BF16) for i in range(2)])
                Kf = Rot([mk.sb("Kf%d" % i, [128, NBT, 128]) for i in range(2)])
                Vf = Rot([mk.sb("Vf%d" % i, [128, NBT, 128]) for i in range(2)])
                KTh = Rot([mk.sb("KTh%d" % i, [128, NBT, 128], BF16) for i in range(2)])
                Vb = Rot([mk.sb("Vb%d" % i, [128, NBT, 128], BF16) for i in range(2)])
                pz = Rot([mk.ps("pz%d" % i, [128, 512]) for i in range(2)])
                pc = Rot([mk.ps("pc%d" % i, [128, 512]) for i in range(2)])
                po = Rot([mk.ps("po%d" % i, [128, 512]) for i in range(2)])
                pt = Rot([mk.ps("pt%d" % i, [128, 512]) for i in range(2)])
                zs = Rot([mk.sb("zs%d" % i, [128, NQ_]) for i in range(3)])
                ee = Rot([mk.sb("ee%d" % i, [128, NQ_]) for i in range(2)])
                sp_ = Rot([mk.sb("sp%d" % i, [128, NQ_], BF16) for i in range(3)])
                lg = Rot([mk.sb("lg%d" % i, [128, NQ_]) for i in range(2)])
                aT = Rot([mk.sb("aT%d" % i, [128, NQ_], BF16) for i in range(3)])
                Rr = Rot([mk.sb("R%d" % i, [128, NQ_], BF16) for i in range(2)])
                om = Rot([mk.sb("om%d" % i, [128, NQ_], BF16) for i in range(2)])
                kout = O["k" + sfx]
                vout = O["v" + sfx]
                for h in range(H):
                    qh = QTh.next(); bz = bzh.next(); kf = Kf.next(); vf = Vf.next(); kth = KTh.next(); vb = Vb.next()
                    hs = slice(h * 128, (h + 1) * 128)
                    mk.dma("sp", lambda e, qh=qh, h=h: e.dma_start(out=qh.t[:], in_=S["QT"][h]), qh, reads=[DSs["QT"]], writes=[qh])
                    mk.dma("sp", lambda e, bz=bz, h=h: e.dma_start(out=bz.t[:], in_=S["bzT"][h]), bz, reads=[DSs["bzT"]], writes=[bz])
                    if npast:
                        mk.dma("sp", lambda e, kf=kf, hs=hs: e.dma_start(out=kf.t[:, 0:npast, :], in_=I["ck"][l, :, hs].rearrange("(b p) d -> p b d", p=128)), kf, writes=[kf])
                        mk.dma("sp", lambda e, vf=vf, hs=hs: e.dma_start(out=vf.t[:, 0:npast, :], in_=I["cv"][l, :, hs].rearrange("(b p) d -> p b d", p=128)), vf, writes=[vf])
                    mk.dma("sp", lambda e, kf=kf, hs=hs: e.dma_start(out=kf.t[:KB, npast:NBT, :], in_=kout[l, :, hs].rearrange("(b p) d -> p b d", p=KB)), kf, reads=[DKV[sfx]], writes=[kf])
                    mk.dma("sp", lambda e, vf=vf, hs=hs: e.dma_start(out=vf.t[:KB, npast:NBT, :], in_=vout[l, :, hs].rearrange("(b p) d -> p b d", p=KB)), vf, reads=[DKV[sfx]], writes=[vf])
                    for b0 in range(0, NBT, 4):
                        p = pt.next()
                        nb_ = min(4, NBT - b0)
                        kbs = []
                        for j in range(nb_):
                            b = b0 + j
                            kb = 128 if b < npast else KB
                            kbs.append(kb)
                            mk.op("pe", lambda e, p=p, j=j, b=b, kb=kb, kf=kf: e.transpose(out=p.t[:, j * 128:j * 128 + kb], in_=kf.t[:kb, b, :], identity=ident[:kb, :kb]), reads=[kf, cst], writes=[p])
                        if all(k == 128 for k in kbs):
                            mk.op("act", lambda e, p=p, b0=b0, nb_=nb_, kth=kth: e.activation(out=kth.t[:, b0:b0 + nb_, :].rearrange("p a b -> p (a b)"), in_=p.t[:, 0:nb_ * 128], func=AF.Copy), reads=[p], writes=[kth])
                        else:
                            for j in range(nb_):
                                mk.op("act", lambda e, p=p, b0=b0, j=j, kth=kth, kb=kbs[j]: e.activation(out=kth.t[:, b0 + j, 0:kb], in_=p.t[:, j * 128:j * 128 + kb], func=AF.Copy), reads=[p], writes=[kth])
                    if npast:
                        mk.op("pool", lambda e, vb=vb, vf=vf: e.tensor_copy(out=vb.t[:, 0:npast, :], in_=vf.t[:, 0:npast, :]), reads=[vf], writes=[vb])
                    mk.op("pool", lambda e, vb=vb, vf=vf: e.tensor_copy(out=vb.t[:KB, npast:NBT, :], in_=vf.t[:KB, npast:NBT, :]), reads=[vf], writes=[vb])
                    for qi in range(NQT):
                        q0 = qi * NQ_
                        blocks = []
                        nb_hi = (q0 + NQ_ - 1) // KB
                        for b in range(nb_hi, -1, -1):
                            bd = b - q0 // KB
                            blocks.append((npast + b, KB, bd if bd >= 0 else None))
                        for b in range(npast - 1, -1, -1):
                            blocks.append((b, 128, None))
                        R = Rr.next()
                        mk.op("pool", lambda e, R=R: e.memset(R.t[:], 0.0), writes=[R])
                        pov = po.next()
                        for bi, (b, kb, bd) in enumerate(blocks):
                            first = bi == 0
                            lastb = bi == len(blocks) - 1
                            z = pz.next()
                            mk.op("pe", lambda e, z=z, b=b, kb=kb, kth=kth, qh=qh: e.matmul(z.t[:kb, 0:NQ_], lhsT=kth.t[:, b, 0:kb], rhs=qh.t[:, q0:q0 + NQ_], start=True, stop=True), reads=[kth, qh], writes=[z])
                            zz = zs.next()
                            if bd is None:
                                mk.op("dve", lambda e, z=z, zz=zz, kb=kb: e.tensor_copy(out=zz.t[:kb, :], in_=z.t[:kb, 0:NQ_]), reads=[z], writes=[zz])
                            else:
                                mk.op("dve", lambda e, z=z, zz=zz, kb=kb, bd=bd: e.tensor_tensor(out=zz.t[:kb, :], in0=z.t[:kb, 0:NQ_], in1=MD[:kb, bd, 0:NQ_], op=ALU.add), reads=[z, cst], writes=[zz])
                            ex = ee.next()
                            mk.op("act", lambda e, zz=zz, ex=ex, kb=kb: e.activation(out=ex.t[:kb, :], in_=zz.t[:kb, :], func=AF.Exp), reads=[zz], writes=[ex])
                            sp = sp_.next()
                            mk.op("act", lambda e, sp=sp, ex=ex, kb=kb: e.activation(out=sp.t[:kb, :], in_=ex.t[:kb, :], func=AF.Ln, bias=1.0), reads=[ex], writes=[sp])
                            cc = pc.next()
                            mk.op("pe", lambda e, cc=cc, sp=sp, kb=kb, first=first: e.matmul(cc.t[:kb, 0:NQ_], lhsT=TRIB[:kb, :kb], rhs=sp.t[:kb, :], start=True, stop=first), reads=[cbf, sp], writes=[cc])
                            if not first:
                                mk.op("pe", lambda e, cc=cc, R=R, kb=kb: e.matmul(cc.t[:kb, 0:NQ_], lhsT=ONESB[:, :kb], rhs=R.t[:], start=False, stop=True), reads=[cbf, R], writes=[cc])
                            if not lastb:
                                mk.op("pool", lambda e, R=R, sp=sp, kb=kb: e.tensor_tensor(out=R.t[:kb, :], in0=R.t[:kb, :], in1=sp.t[:kb, :], op=ALU.add), reads=[R, sp], writes=[R])
                            lgt = lg.next()
                            mk.op("dve", lambda e, lgt=lgt, zz=zz, cc=cc, kb=kb: e.tensor_tensor(out=lgt.t[:kb, :], in0=zz.t[:kb, :], in1=cc.t[:kb, 0:NQ_], op=ALU.subtract), reads=[zz, cc], writes=[lgt])
                            a_ = aT.next()
                            mk.op("act", lambda e, a_=a_, lgt=lgt, kb=kb: e.activation(out=a_.t[:kb, :], in_=lgt.t[:kb, :], func=AF.Exp), reads=[lgt], writes=[a_])
                            mk.op("pe", lambda e, pov=pov, vb=vb, a_=a_, b=b, kb=kb, first=first, lastb=lastb: e.matmul(pov.t[:, 0:NQ_], lhsT=vb.t[:kb, b, :], rhs=a_.t[:kb, :], start=first, stop=lastb), reads=[vb, a_], writes=[pov])
                        o_ = om.next()
                        mk.op("dve", lambda e, o_=o_, pov=pov, bz=bz: e.tensor_tensor(out=o_.t[:], in0=pov.t[:, 0:NQ_], in1=bz.t[:, q0:q0 + NQ_], op=ALU.mult), reads=[pov, bz], writes=[o_])
                        mk.dma("pool", lambda e, o_=o_, h=h: e.dma_start(out=S["mixT"][12 + h, :, q0:q0 + NQ_], in_=o_.t[:]), o_, reads=[o_], writes=[DSs["mixT"]])

        def out_phase(sfx, T, l, xin, Dxin):
            S = SCR[sfx]
            DSs = DS[sfx]
            TT = min(128, T)
            TSO = min(256, T)
            NTT = TSO // TT
            yout = O["y" + sfx]
            with mk.phase():
                mx = Rot([mk.sb("mx%d" % i, [128, 32, TSO], BF16) for i in range(2)])
                Wo = Rot([mk.sb("Wo%d" % i, [128, 32, 512], BF16) for i in range(2)])
                ybuf = [mk.sb("ybuf%d" % i, [TT, D]) for i in range(NTT)]
                xt = Rot([mk.sb("xto%d" % i, [TT, D]) for i in range(2)])
                junk = mk.sb("junko", [TT, D], BF16)
                st1 = Rot([mk.sb("sto%d" % i, [TT, 2]) for i in range(2)])
                pm = Rot([mk.ps("pmo%d" % i, [128, 512]) for i in range(4)])
                gpostb = mk.sb("gpostb", [128, D])
                mk.dma("sp", lambda e: e.dma_start(out=gpostb.t[:], in_=I["norm_post"][l:l + 1, :].to_broadcast([128, D])), gpostb, writes=[gpostb])
                nev = 0
                for so in range(T // TSO):
                    t0 = so * TSO
                    m = mx.next()
                    mk.dma("sp", lambda e, m=m: e.dma_start(out=m.t[:], in_=S["mixT"][:, :, t0:t0 + TSO].rearrange("c p t -> p c t")), m, reads=[DSs["mixT"]], writes=[m])
                    for nb in range(8):
                        W = Wo.next()
                        mk.dma("sp", lambda e, W=W, nb=nb: e.dma_start(out=W.t[:], in_=wobf[nb]), W, reads=[Dwo], writes=[W])
                        for tt in range(NTT):
                            p = pm.next()
                            for mc in range(32):
                                mk.op("pe", lambda e, p=p, mc=mc, tt=tt, W=W, m=m: e.matmul(p.t[:TT, :], lhsT=m.t[:, mc, tt * TT:(tt + 1) * TT], rhs=W.t[:, mc, :], start=(mc == 0), stop=(mc == 31)), reads=[m, W], writes=[p])
                            nev += 1
                            yb_ = ybuf[tt]
                            if nev % 2:
                                mk.op("act", lambda e, p=p, yb_=yb_, nb=nb: e.activation(out=yb_.t[:, nb * 512:(nb + 1) * 512], in_=p.t[:TT, :], func=AF.Copy), reads=[p], writes=[yb_])
                            else:
                                mk.op("dve", lambda e, p=p, yb_=yb_, nb=nb: e.tensor_copy(out=yb_.t[:, nb * 512:(nb + 1) * 512], in_=p.t[:TT, :]), reads=[p], writes=[yb_])
                    for tt in range(NTT):
                        yb_ = ybuf[tt]
                        x = xt.next()
                        s1 = st1.next()
                        r0 = t0 + tt * TT
                        mk.dma("sp", lambda e, x=x, r0=r0: e.dma_start(out=x.t[:], in_=xin[r0:r0 + TT, :]), x, reads=[Dxin], writes=[x])
                        mk.op("act", lambda e, yb_=yb_, s1=s1: e.activation(out=junk.t[:], in_=yb_.t[:], func=AF.Square, accum_out=s1.t[:, 0:1]), reads=[yb_], writes=[junk, s1])
                        mk.op("act", lambda e, s1=s1: e.activation(out=s1.t[:, 1:2], in_=s1.t[:, 0:1], func=AF.Sqrt, scale=1.0 / D, bias=EPS), reads=[s1], writes=[s1])
                        mk.op("dve", lambda e, s1=s1: e.reciprocal(out=s1.t[:, 1:2], in_=s1.t[:, 1:2]), reads=[s1], writes=[s1])
                        mk.op("dve", lambda e, yb_=yb_, s1=s1: e.scalar_tensor_tensor(out=yb_.t[:], in0=yb_.t[:], scalar=s1.t[:, 1:2], in1=gpostb.t[:TT, :], op0=ALU.mult, op1=ALU.mult), reads=[yb_, s1, gpostb], writes=[yb_])
                        mk.op("pool", lambda e, yb_=yb_, x=x: e.tensor_tensor(out=x.t[:], in0=yb_.t[:], in1=x.t[:], op=ALU.add), reads=[yb_, x], writes=[x])
                        mk.dma("pool", lambda e, x=x, r0=r0: e.dma_start(out=yout[r0:r0 + TT, :], in_=x.t[:]), x, reads=[x], writes=[DY[sfx]])

        nph = [0]

        def go():
            nph[0] += 1
            return nph[0] <= DEBUG_STOP
        for l in range(DEPTH):
            load_params(l)
            if go():
                precast(l)
            for sfx, T in (("p", T_P), ("s", T_S)):
                xin = I["x" + sfx] if l == 0 else O["y" + sfx]
                Dxin = mk.dram("xin" + sfx) if l == 0 else DY[sfx]
                if go():
                    proj_phase(sfx, T, l, xin, Dxin)
                if go():
                    gdn_phase(sfx, T, l)
                if go():
                    sb_phase(sfx, T, l)
                if go():
                    out_phase(sfx, T, l, xin, Dxin)
        mk.barrier()
        mk.flush()
        nc._mk_ninst = mk.ninst
    return nc


def make_consts():
    p = np.arange(128)[:, None]
    f = np.arange(128)[None, :]
    ident = (p == f).astype(np.float32)
    uinc = (p <= f).astype(np.float32)
    sus = (p > f).astype(np.float32)
    masku = np.where(f >= p, 0.0, NEG).astype(np.float32)
    maskl = np.where(f < p, 0.0, NEG).astype(np.float32)
    t = np.arange(512)[None, None, :]
    a = np.arange(4)[None, :, None]
    md = np.where(t > 128 * a + p[:, :, None], 0.0, NEG).astype(np.float32).reshape(128, 2048)
    return np.ascontiguousarray(np.concatenate([ident, uinc, sus, masku, maskl, md], axis=1))


def make_level_masks():
    i = np.arange(128)[:, None]
    j = np.arange(128)[None, :]
    low, up = [], []
    for s_ in range(7):
        m = ((i >> (s_ + 1)) == (j >> (s_ + 1))) & (((i >> s_) & 1) == 1) & (((j >> s_) & 1) == 0)
        low.append(m.astype(np.float32))
        up.append(m.T.astype(np.float32))
    return np.ascontiguousarray(np.concatenate(low + up, axis=1))


_CACHE = {}


def run(inputs, T_P, T_S, PAST, DEPTH, n_cores=8):
    key = (T_P, T_S, PAST, DEPTH)
    if key not in _CACHE:
        _CACHE[key] = build_program(T_P, T_S, PAST, DEPTH)
    nc = _CACHE[key]
    f = lambda a: np.ascontiguousarray(np.asarray(a, dtype=np.float32))
    xp = f(inputs["x_prompt"]); xs = f(inputs["x_sample"])
    BP = xp.shape[0]
    ck = f(inputs["cache_sb_k"]); cv = f(inputs["cache_sb_v"])
    sg = f(inputs["state_gdn"]); sgc = f(inputs["state_gdn_conv"]); ssc = f(inputs["state_sc_conv"])
    shared = dict(w_in=f(inputs["w_in"]), w_out=f(inputs["w_out"]), norm_pre=f(inputs["norm_pre"]),
                  norm_post=f(inputs["norm_post"]), gcw=f(inputs["gdn_conv_w"]), alog=f(inputs["gdn_a_log"]),
                  dtb=f(inputs["gdn_dt_bias"]), gnorm=f(inputs["gdn_norm"]), scw=f(inputs["sc_conv_w"]), cst=make_consts(),
                  mks=make_level_masks())
    in_maps = []
    for c in range(n_cores):
        m = dict(shared)
        m["xp"] = xp[c % BP]
        m["xs"] = xs[c]
        m["ck"] = np.ascontiguousarray(ck[:, c].reshape(DEPTH, PAST, GW))
        m["cv"] = np.ascontiguousarray(cv[:, c].reshape(DEPTH, PAST, GW))
        m["sg"] = np.ascontiguousarray(sg[:, c])
        m["sgc"] = np.ascontiguousarray(sgc[:, c])
        m["ssc"] = np.ascontiguousarray(ssc[:, c])
        in_maps.append(m)
    res = run_bass_kernel_spmd(nc, in_maps, core_ids=list(range(n_cores)))
    R = res.results
    st = lambda name, cores, ax: np.stack([R[c][name] for c in cores], axis=ax)
    pc = list(range(min(BP, n_cores)))
    sc = list(range(n_cores))
    yp = st("yp", pc, 0)
    ys = st("ys", sc, 0)
    outs = [yp, ys]
    for sfx, cores in (("p", pc), ("s", sc)):
        T = T_P if sfx == "p" else T_S
        outs.append(st("k" + sfx, cores, 1).reshape(DEPTH, len(cores), T, H, HD))
        outs.append(st("v" + sfx, cores, 1).reshape(DEPTH, len(cores), T, H, HD))
        outs.append(st("g" + sfx, cores, 1))
        outs.append(st("gc" + sfx, cores, 1))
        outs.append(st("sc" + sfx, cores, 1))
    return tuple(np.ascontiguousarray(o.astype(np.float32)) for o in outs)


def kernel(**inputs):
    return run(inputs, 4096, 32, 2048, 4)
```
